# Optimizing a Trainium2 kernel written in Bass

```python
import jax, jax.numpy as jnp
from jax import lax
import numpy as np

D_MODEL = 1024
BATCH = 8
SEQ = 2048
DEPTH = 1
DEC_BATCH = 128
DEC_SEQ = 8
PAST_LEN = 8192
PAGE_SIZE = 128

HEAD_DIM = 64
ATT_HEADS = 8
ATT_W = ATT_HEADS * HEAD_DIM
RWKV_HEADS = 8
RWKV_W = RWKV_HEADS * HEAD_DIM
MIX_W = ATT_W + RWKV_W
DIL_BRANCHES = ((128, 1), (512, 4), (2048, 16))
MAX_WINDOW = 2048
ROT_DIM = HEAD_DIM // 4
ROPE_THETA = 500000.0
DECAY_LORA = 64
AAA_LORA = 64
GATE_LORA = 128
RWKV_IN = 3 * RWKV_W + DECAY_LORA + AAA_LORA + GATE_LORA
IN_W = 3 * ATT_W + RWKV_IN
D_FF = 4 * D_MODEL
NORM_EPS = 1e-6
LNX_EPS = 64e-5

kernel_name = 'hybrid_dilated_attn_rwkv7_step'


def rms_norm(x, g):
    xf = x.astype(jnp.float32)
    return xf * lax.rsqrt(jnp.mean(xf * xf, axis=-1, keepdims=True) + NORM_EPS) * g.astype(jnp.float32)


def ada_modulation(c, w_ada, b_ada):
    m = jax.nn.silu(c.astype(jnp.float32)) @ w_ada.astype(jnp.float32) + b_ada.astype(jnp.float32)
    return jnp.split(m[:, None, :], 6, axis=-1)


def partial_rope(t, pos):
    half = ROT_DIM // 2
    inv = ROPE_THETA ** (-jnp.arange(half, dtype=jnp.float32) * (2.0 / ROT_DIM))
    ang = pos.astype(jnp.float32)[:, None] * inv[None, :]
    cos = jnp.cos(ang)[None, :, None, :]
    sin = jnp.sin(ang)[None, :, None, :]
    t = t.astype(jnp.float32)
    t1, t2, rest = t[..., :half], t[..., half:ROT_DIM], t[..., ROT_DIM:]
    return jnp.concatenate([t1 * cos - t2 * sin, t1 * sin + t2 * cos, rest], axis=-1)


def dilated_branch_prompt(q, k, v, window, dil):
    B, S, H, E = q.shape
    nw = window // dil
    span = dil * nw
    Sp = -(-S // span) * span
    pad = ((0, 0), (0, Sp - S), (0, 0), (0, 0))
    n = Sp // dil
    nb = n // nw

    def to_blocks(t):
        t = jnp.pad(t, pad).reshape(B, n, dil, H, E).transpose(0, 2, 1, 3, 4)
        return t.reshape(B, dil, nb, nw, H, E)

    def with_prev(t):
        prev = jnp.pad(t, ((0, 0), (0, 0), (1, 0), (0, 0), (0, 0), (0, 0)))[:, :, :-1]
        return jnp.concatenate([prev, t], axis=3)

    qb = to_blocks(q)
    k2 = with_prev(to_blocks(k))
    v2 = with_prev(to_blocks(v))
    s = jnp.einsum('brnqhe,brnkhe->brnhqk', qb, k2.astype(jnp.float32)) * (HEAD_DIM ** -0.5)
    a_idx = jnp.arange(nw)[:, None]
    c_idx = jnp.arange(2 * nw)[None, :]
    dist = nw + a_idx - c_idx
    band = (dist >= 0) & (dist <= nw)
    has_prev = (jnp.arange(nb) > 0)[:, None, None] | (c_idx >= nw)[None]
    mask = (band[None] & has_prev)[None, None, :, None]
    s = jnp.where(mask, s, -jnp.inf)
    lse = jax.nn.logsumexp(s, axis=-1)
    p = jnp.exp(s - lse[..., None])
    o = jnp.einsum('brnhqk,brnkhe->brnqhe', p, v2.astype(jnp.float32))
    o = o.reshape(B, dil, n, H, E).transpose(0, 2, 1, 3, 4).reshape(B, Sp, H, E)[:, :S]
    lse = lse.transpose(0, 1, 2, 4, 3).reshape(B, dil, n, H).transpose(0, 2, 1, 3).reshape(B, Sp, H)[:, :S]
    return o, lse


def dilated_branch_sample(q, k, v, k_buf, v_buf, window, dil):
    T = q.shape[1]
    WB = k_buf.shape[1]
    nw = window // dil
    scale = HEAD_DIM ** -0.5
    i = jnp.arange(T)
    rel = i[:, None] - dil * jnp.arange(nw + 1)[None, :]
    buf_idx = WB + rel
    buf_ok = (rel < 0) & (buf_idx >= 0)
    buf_idx = jnp.clip(buf_idx, 0, WB - 1)
    kg = k_buf[:, buf_idx].astype(jnp.float32)
    vg = v_buf[:, buf_idx].astype(jnp.float32)
    s_buf = jnp.einsum('bthe,btjhe->bhtj', q, kg) * scale
    s_buf = jnp.where(buf_ok[None, None], s_buf, -jnp.inf)
    diff = i[:, None] - i[None, :]
    new_ok = (diff >= 0) & (diff % dil == 0) & (diff <= window)
    s_new = jnp.einsum('bthe,bshe->bhts', q, k.astype(jnp.float32)) * scale
    s_new = jnp.where(new_ok[None, None], s_new, -jnp.inf)
    s = jnp.concatenate([s_buf, s_new], axis=-1)
    lse = jax.nn.logsumexp(s, axis=-1)
    p = jnp.exp(s - lse[..., None])
    o = (jnp.einsum('bhtj,btjhe->bthe', p[..., :nw + 1], vg)
         + jnp.einsum('bhts,bshe->bthe', p[..., nw + 1:], v.astype(jnp.float32)))
    return o, lse.transpose(0, 2, 1)


def combine_dilations(branches):
    o = jnp.stack([b[0] for b in branches])
    lse = jnp.stack([b[1] for b in branches])
    wts = jax.nn.softmax(lse, axis=0)
    return jnp.sum(wts[..., None] * o, axis=0)


def rwkv7_inputs(P, P_prev, mu, w0, w2, a0, a2, g2, k_k, k_a):
    Pm = P + (P_prev - P) * mu.astype(jnp.float32)
    cuts = [RWKV_W, 2 * RWKV_W, 3 * RWKV_W, 3 * RWKV_W + DECAY_LORA, 3 * RWKV_W + DECAY_LORA + AAA_LORA]
    r, k, v, wl, al, gl = jnp.split(Pm, cuts, axis=-1)
    w = -jax.nn.softplus(-(w0 + jnp.tanh(wl) @ w2)) - 0.5
    decay = jnp.exp(-jnp.exp(w))
    a = jax.nn.sigmoid(a0 + al @ a2)
    g = jax.nn.sigmoid(gl) @ g2
    B, T = P.shape[:2]
    heads = lambda t: t.reshape(B, T, RWKV_HEADS, HEAD_DIM)
    kk = heads(k * k_k)
    kk = kk / jnp.maximum(jnp.sqrt(jnp.sum(kk * kk, axis=-1, keepdims=True)), 1e-12)
    k = k * (1.0 + (a - 1.0) * k_a)
    return heads(r), heads(decay), heads(k), heads(v), kk, heads(a), g


def wkv7_scan(S0, r, w, k, v, kk, a):
    def step(S, inp):
        r_t, w_t, k_t, v_t, kk_t, a_t = inp
        sa = jnp.einsum('bhij,bhj->bhi', S, -kk_t)
        S = (S * w_t[:, :, None, :] + sa[..., None] * (kk_t * a_t)[:, :, None, :]
             + v_t[..., None] * k_t[:, :, None, :])
        return S, jnp.einsum('bhij,bhj->bhi', S, r_t)
    xs = tuple(jnp.swapaxes(t, 0, 1) for t in (r, w, k, v, kk, a))
    S, out = lax.scan(step, S0.astype(jnp.float32), xs)
    return jnp.swapaxes(out, 0, 1), S


def rwkv7_output(o, r, k, v, g, r_k, lnx_g, lnx_b):
    B, T = o.shape[:2]
    mean = jnp.mean(o, axis=-1, keepdims=True)
    var = jnp.mean(jnp.square(o - mean), axis=-1, keepdims=True)
    on = ((o - mean) * lax.rsqrt(var + LNX_EPS)).reshape(B, T, RWKV_W) * lnx_g + lnx_b
    bonus = jnp.sum(r * k * r_k, axis=-1, keepdims=True) * v
    return (on + bonus.reshape(B, T, RWKV_W)) * g


def decoder_layer(x, c, pos, k_past, v_past, wkv0, shift0, norm1_g, norm2_g, w_ada, b_ada, w_in, w_out,
                  mu, w0, w2, a0, a2, g2, k_k, k_a, r_k, lnx_g, lnx_b, w_ff1, w_ff2):
    sh1, sc1, gt1, sh2, sc2, gt2 = ada_modulation(c, w_ada, b_ada)
    h = rms_norm(x, norm1_g) * (1.0 + sc1) + sh1
    proj = h @ w_in
    B, T = proj.shape[:2]
    q, k, v = jnp.split(proj[..., :3 * ATT_W], 3, axis=-1)
    q = partial_rope(q.reshape(B, T, ATT_HEADS, HEAD_DIM), pos)
    k = partial_rope(k.reshape(B, T, ATT_HEADS, HEAD_DIM), pos)
    v = v.reshape(B, T, ATT_HEADS, HEAD_DIM)
    if k_past is None:
        branches = [dilated_branch_prompt(q, k, v, win, dil) for win, dil in DIL_BRANCHES]
        keep = min(MAX_WINDOW, T)
        k_state, v_state = k[:, T - keep:], v[:, T - keep:]
    else:
        branches = [dilated_branch_sample(q, k, v, k_past, v_past, win, dil) for win, dil in DIL_BRANCHES]
        k_state, v_state = k, v
    att = combine_dilations(branches).reshape(B, T, ATT_W)
    P = proj[..., 3 * ATT_W:].astype(jnp.float32)
    P_prev = jnp.concatenate([shift0[:, None, :].astype(jnp.float32), P[:, :-1]], axis=1)
    r, decay, kr, vr, kk, a, g = rwkv7_inputs(P, P_prev, mu, w0, w2, a0, a2, g2, k_k, k_a)
    o, S = wkv7_scan(wkv0, r, decay, kr, vr, kk, a)
    rw = rwkv7_output(o, r, kr, vr, g, r_k, lnx_g, lnx_b)
    mix = jnp.concatenate([att, rw], axis=-1) @ w_out
    x = x + gt1 * mix
    h2 = rms_norm(x, norm2_g) * (1.0 + sc2) + sh2
    x = x + gt2 * (jnp.square(jax.nn.relu(h2 @ w_ff1)) @ w_ff2)
    return x, k_state, v_state, S, P[:, -1]


def setup_inputs(seed: int = 0) -> dict:
    key = jax.random.key(seed)
    ks = jax.random.split(key, 32)
    f32 = jnp.float32
    nrm = lambda kk, shape, s: jax.random.normal(kk, shape, f32) * s
    WB = min(MAX_WINDOW, PAST_LEN)
    return {
        'x_prompt': nrm(ks[0], (BATCH, SEQ, D_MODEL), 1.0),
        'x_sample': nrm(ks[1], (DEC_BATCH, DEC_SEQ, D_MODEL), 1.0),
        'cache_k': nrm(ks[2], (DEPTH, DEC_BATCH, WB, ATT_HEADS, HEAD_DIM), 1.0),
        'cache_v': nrm(ks[3], (DEPTH, DEC_BATCH, WB, ATT_HEADS, HEAD_DIM), 1.0),
        'state_wkv': nrm(ks[4], (DEPTH, DEC_BATCH, RWKV_HEADS, HEAD_DIM, HEAD_DIM), 0.3),
        'state_shift': nrm(ks[5], (DEPTH, DEC_BATCH, RWKV_IN), 1.0),
        'c_prompt': nrm(ks[6], (BATCH, D_MODEL), 1.0),
        'c_sample': nrm(ks[7], (DEC_BATCH, D_MODEL), 1.0),
        'norm1_g': 1.0 + nrm(ks[8], (DEPTH, D_MODEL), 0.05),
        'norm2_g': 1.0 + nrm(ks[9], (DEPTH, D_MODEL), 0.05),
        'w_ada': nrm(ks[10], (DEPTH, D_MODEL, 6 * D_MODEL), 0.5 * D_MODEL ** -0.5),
        'b_ada': nrm(ks[11], (DEPTH, 6 * D_MODEL), 0.02),
        'w_in': nrm(ks[12], (DEPTH, D_MODEL, IN_W), D_MODEL ** -0.5),
        'w_out': nrm(ks[13], (DEPTH, MIX_W, D_MODEL), MIX_W ** -0.5),
        'mu': jax.random.uniform(ks[14], (DEPTH, RWKV_IN), f32),
        'w0': jax.random.uniform(ks[15], (DEPTH, RWKV_W), f32, -4.0, 1.0),
        'w2': nrm(ks[16], (DEPTH, DECAY_LORA, RWKV_W), 0.5 * DECAY_LORA ** -0.5),
        'a0': nrm(ks[17], (DEPTH, RWKV_W), 0.5),
        'a2': nrm(ks[18], (DEPTH, AAA_LORA, RWKV_W), 0.5 * AAA_LORA ** -0.5),
        'g2': nrm(ks[19], (DEPTH, GATE_LORA, RWKV_W), GATE_LORA ** -0.5),
        'k_k': 0.85 + nrm(ks[20], (DEPTH, RWKV_W), 0.1),
        'k_a': 1.0 + nrm(ks[21], (DEPTH, RWKV_W), 0.1),
        'r_k': nrm(ks[22], (DEPTH, RWKV_HEADS, HEAD_DIM), 0.1),
        'lnx_g': 1.0 + nrm(ks[23], (DEPTH, RWKV_W), 0.05),
        'lnx_b': nrm(ks[24], (DEPTH, RWKV_W), 0.02),
        'w_ff1': nrm(ks[25], (DEPTH, D_MODEL, D_FF), D_MODEL ** -0.5),
        'w_ff2': nrm(ks[26], (DEPTH, D_FF, D_MODEL), D_FF ** -0.5),
        'normf_g': 1.0 + nrm(ks[27], (D_MODEL,), 0.05),
    }


def reference(x_prompt, x_sample, cache_k, cache_v, state_wkv, state_shift, c_prompt, c_sample,
              norm1_g, norm2_g, w_ada, b_ada, w_in, w_out, mu, w0, w2, a0, a2, g2, k_k, k_a, r_k,
              lnx_g, lnx_b, w_ff1, w_ff2, normf_g):
    B, S = x_prompt.shape[:2]
    DB, T = x_sample.shape[:2]
    pos_p = jnp.arange(S)
    pos_s = PAST_LEN + jnp.arange(T)
    wkv_zero = jnp.zeros((B, RWKV_HEADS, HEAD_DIM, HEAD_DIM), jnp.float32)
    shift_zero = jnp.zeros((B, RWKV_IN), jnp.float32)
    hp, hs = x_prompt, x_sample
    kp_l, vp_l, sp_l, shp_l, ks_l, vs_l, ss_l, shs_l = [], [], [], [], [], [], [], []
    for l in range(DEPTH):
        lw = (norm1_g[l], norm2_g[l], w_ada[l], b_ada[l], w_in[l], w_out[l], mu[l], w0[l], w2[l],
              a0[l], a2[l], g2[l], k_k[l], k_a[l], r_k[l], lnx_g[l], lnx_b[l], w_ff1[l], w_ff2[l])
        hp, kp, vp, sp, shp = decoder_layer(hp, c_prompt, pos_p, None, None, wkv_zero, shift_zero, *lw)
        hs, kn, vn, sn, shn = decoder_layer(hs, c_sample, pos_s, cache_k[l], cache_v[l],
                                            state_wkv[l], state_shift[l], *lw)
        kp_l.append(kp); vp_l.append(vp); sp_l.append(sp); shp_l.append(shp)
        ks_l.append(kn); vs_l.append(vn); ss_l.append(sn); shs_l.append(shn)
    dt = x_prompt.dtype
    y_prompt = rms_norm(hp, normf_g).astype(dt)
    y_sample = rms_norm(hs, normf_g).astype(x_sample.dtype)
    return (y_prompt, y_sample,
            jnp.stack(kp_l).astype(dt), jnp.stack(vp_l).astype(dt),
            jnp.stack(sp_l).astype(dt), jnp.stack(shp_l).astype(dt),
            jnp.stack(ks_l).astype(dt), jnp.stack(vs_l).astype(dt),
            jnp.stack(ss_l).astype(dt), jnp.stack(shs_l).astype(dt))
```

```python
import contextlib
import numpy as np
import concourse.bass as bass
import concourse.mybir as mybir
from concourse.bass_utils import run_bass_kernel_spmd

F32 = mybir.dt.float32
BF16 = mybir.dt.bfloat16
AF = mybir.ActivationFunctionType
ALU = mybir.AluOpType
AX = mybir.AxisListType

NCORES = 8
D = 1024
S = 2048
NT = 16
DB = 16
T = 8
H = 8
E = 64
RIN = 1792
INW = 3328
DFF = 4096
PAST = 8192
ENGS = ("pe", "act", "dve", "pool", "sp")

V_MU, V_W0, V_A0, V_KK, V_KA, V_RK, V_LG, V_LB, V_G1, V_G2, V_BSH1, V_BSC1, V_BSH2, V_BSC2 = (
    0, 14, 18, 22, 26, 30, 34, 38, 42, 50, 58, 66, 74, 82)
NVEC = 90
SAMPLE_ATTN_DONE = True


class Op:
    __slots__ = ("eng", "fn", "deps", "is_dma", "signal", "sem", "target", "name")

    def __init__(s, eng, fn, is_dma, name):
        s.eng = eng; s.fn = fn; s.is_dma = is_dma; s.deps = []
        s.signal = False; s.sem = None; s.target = 0; s.name = name


class Prog:
    def __init__(s, nc, n_dma_sems=24):
        s.nc = nc
        s.ops = []
        s.st = {}
        s.group_of = {}
        s.n_dma_sems = n_dma_sems
        s.final_ops = []
        s.exclusive = {"ps"}

    def _conf(s, g, name, sub):
        ent, idx = g
        if name == "*":
            return list(ent.keys())
        if sub is None:
            keys = [(name, x) for x in idx.get(name, ())]
        else:
            keys = [k for k in ((name, sub), (name, None)) if k in ent]
        if ("*", None) in ent:
            keys.append(("*", None))
        return keys

    def _norm(s, k):
        if not isinstance(k, tuple):
            k = (k, None)
        name, sub = k
        if name.startswith("*@"):
            return name[2:], "*", None
        return s.group_of.get(name, name), name, sub

    def add(s, eng, fn, reads=(), writes=(), dma=False, name="", final=False):
        op = Op(eng, fn, dma, name)
        reads = [s._norm(k) for k in reads]; writes = [s._norm(k) for k in writes]
        deps = {}
        for gname, name_, sub in reads:
            g = s.st.setdefault(gname, ({}, {}))
            for k in s._conf(g, name_, sub):
                w = g[0][k][0]
                if w is not None:
                    deps[id(w)] = w
                if name_ in s.exclusive:
                    for r in g[0][k][1]:
                        if r.eng != eng:
                            deps[id(r)] = r
        for gname, name_, sub in writes:
            g = s.st.setdefault(gname, ({}, {}))
            for k in s._conf(g, name_, sub):
                w = g[0][k][0]
                if w is not None:
                    deps[id(w)] = w
                for r in g[0][k][1]:
                    deps[id(r)] = r
        for gname, name_, sub in reads:
            ent, idx = s.st[gname]
            rl_ = ent.setdefault((name_, sub), [None, []])[1]
            if not op.is_dma:
                rl_[:] = [r for r in rl_ if r.is_dma or r.eng != op.eng]
            rl_.append(op)
            idx.setdefault(name_, set()).add(sub)
        for gname, name_, sub in writes:
            ent, idx = s.st[gname]
            if name_ == "*":
                ent.clear(); idx.clear()
            elif sub is None:
                for x in idx.get(name_, ()):
                    ent.pop((name_, x), None)
                idx[name_] = set()
            ent[(name_, sub)] = [op, []]
            idx.setdefault(name_, set()).add(sub)
        for w in deps.values():
            if w is op:
                continue
            if (not w.is_dma) and (not op.is_dma) and w.eng == op.eng and op.eng == "pe":
                continue
            op.deps.append(w)
            w.signal = True
        s.ops.append(op)
        if final:
            op.signal = True
            s.final_ops.append(op)
        return op

    def emit(s):
        nc = s.nc
        with contextlib.ExitStack() as es:
            esems = {e: es.enter_context(nc.semaphore("s_" + e)) for e in ("pe", "act", "dve", "pool")}
            dpool = {q: [es.enter_context(nc.semaphore("d%s%d" % (q, i))) for i in range(s.n_dma_sems)]
                     for q in ("sp", "pool")}
            cnt = {e: 0 for e in esems}
            dcum = {q: [0] * s.n_dma_sems for q in dpool}
            dprev = {}
            kq = {q: 0 for q in dpool}
            for op in s.ops:
                if op.is_dma:
                    q = op.eng
                    i = kq[q] % s.n_dma_sems; kq[q] += 1
                    op.sem = dpool[q][i]; dprev[id(op)] = dcum[q][i]
                    dcum[q][i] += 16; op.target = dcum[q][i]
                elif op.signal:
                    cnt[op.eng] += 1
                    op.sem = esems[op.eng]; op.target = cnt[op.eng]
            by_eng = {e: [o for o in s.ops if o.eng == e] for e in ENGS}
            block = es.enter_context(nc.Block())

            def run(engname, eng):
                waited = {}

                def wait(sem, val):
                    if val <= 0:
                        return
                    key = id(sem)
                    if waited.get(key, 0) >= val:
                        return
                    eng.wait_ge(sem, val)
                    waited[key] = val

                for op in by_eng[engname]:
                    for w in op.deps:
                        wait(w.sem, w.target)
                    if op.is_dma:
                        wait(op.sem, dprev[id(op)])
                        op.fn(eng).then_inc(op.sem, 16)
                    else:
                        ins = op.fn(eng)
                        if op.signal:
                            ins.then_inc(op.sem, 1)
                for op in s.final_ops:
                    if op.eng == engname:
                        wait(op.sem, op.target)

            block.tensor(lambda e: run("pe", e))
            block.scalar(lambda e: run("act", e))
            block.vector(lambda e: run("dve", e))
            block.gpsimd(lambda e: run("pool", e))
            block.sync(lambda e: run("sp", e))


def build_program(probe=None):
    nc = bass.Bass("TRN2", target_bir_lowering=False)
    es = contextlib.ExitStack()
    P = Prog(nc)
    A = P.add

    def din(name, shape, dt=F32):
        return nc.dram_tensor(name, list(shape), dt, kind="ExternalInput").ap()

    def dout(name, shape, dt=F32):
        return nc.dram_tensor(name, list(shape), dt, kind="ExternalOutput").ap()

    START = 16512
    G0, R1_0, R2_0, END = START, START + 20480, START + 20480 + 71680, 229344
    ptr = {"G": G0, "R1": R1_0, "R2": R2_0}
    lim = {"G": R1_0, "R1": R2_0, "R2": END}
    cnt_names = [0]

    def sb(name, shape, dt=F32, reg="R2"):
        n = 1
        for d in shape[1:]:
            n *= d
        nbytes = n * (2 if dt == BF16 else 4)
        off = ptr[reg]
        ptr[reg] = off + (nbytes + 31) // 32 * 32
        assert ptr[reg] <= lim[reg], (name, reg, ptr[reg] - lim[reg])
        cnt_names[0] += 1
        if reg != "G":
            P.group_of[name] = "arena"
        return nc.alloc_sbuf_tensor_at("s%d_%s" % (cnt_names[0], name), list(shape), dt, offset=off)

    def new_phase(keep_r2=0):
        A("dve", lambda e: e.memset(bar_t[:], 0.0), [], ["*@arena", "bar_t"])
        ptr["R2"] = R2_0 + keep_r2

    def dma(q, out, in_, reads, writes, final=False):
        return A(q, lambda e: e.dma_start(out=out, in_=in_), reads, writes, dma=True, final=final)

    xp = din("xp", [S, D]); xsm = din("xs", [128, D])
    c17 = din("c17", [17, D]); vecs = din("vecs", [NVEC, 128]); bgate = din("bgate", [2, D])
    w_ada = din("w_ada", [D, 6 * D]); w_in = din("w_in", [D, INW])
    sshift = din("sshift", [DB, RIN])
    ident_d = din("ident", [128, 128]); rope_d = din("rope", [17, 128, 128])
    cmask_d = din("cmask", [128, 896])
    w2a2_d = din("w2a2", [128, 512]); g2_d = din("g2", [128, 512])
    o_kwin = dout("o_kwin", [S, 512]); o_vwin = dout("o_vwin", [S, 512])
    o_shp = dout("o_shp", [1, RIN]); o_knew = dout("o_knew", [128, 512]); o_vnew = dout("o_vnew", [128, 512])
    o_shs = dout("o_shs", [DB, RIN]); o_wkvp = dout("o_wkvp", [H, E, E])
    swkv = din("swkv", [128, E * E]); o_wkvs = dout("o_wkvs", [128, E * E])
    w_out = din("w_out", [D, D]); w_ff1 = din("w_ff1", [D, DFF]); w_ff2 = din("w_ff2", [DFF, D]); gfin = din("gfin", [D])
    cnt_d = din("cnt", [128, 2048]); esel_d = din("esel", [17, 256])
    ck = din("ck", [DB, 2048, 512]); cv = din("cv", [DB, 2048, 512])
    cnts_d = din("cnts", [128, 16 * T]); cntn_d = din("cntn", [128, DB * T])
    o_yp = dout("o_yp", [S, D]); o_ys = dout("o_ys", [128, D])
    scr_v = nc.dram_tensor("scr_v", [6, 128, 512], F32, kind="Internal").ap()
    scr_o = nc.dram_tensor("scr_o", [128, 512], F32, kind="Internal").ap()
    scr_acc = nc.dram_tensor("scr_acc", [DB, T, H, 65], F32, kind="Internal").ap()

    ps = [es.enter_context(nc.psum_tensor("ps%d" % i, [128, 512], F32)) for i in range(8)]

    bar_t = sb("bar_t", [128, 8], F32, "G")
    ident = sb("ident", [128, 128], F32, "G"); identb = sb("identb", [128, 128], BF16, "G")
    vecT = sb("vecT", [128, NVEC], F32, "G"); scT = sb("scT", [128, 8, 17], BF16, "G")
    modT = {n: sb("modT_" + n, [128, 8, 17], F32, "G") for n in ("A1", "B1", "A2", "B2")}
    cmask = sb("cmask", [128, 896], F32, "G"); ropet = sb("ropet", [128, 128], F32, "G")
    ss = sb("ss", [128, 1], F32, "G"); rstd = sb("rstd", [128, 1], F32, "G")
    xt = sb("xt", [128, D], F32, "G"); xsn = sb("xsn", [128, D], F32, "G"); hT = sb("hT", [128, 8, 128], BF16, "G")
    sshT = sb("sshT", [128, 14, DB], F32, "G")
    Esel = sb("Esel", [17, 256], F32, "G")
    blk = cmask[:, 0:128]; MT4 = cmask[:, 128:640]; MLs = cmask[:, 640:768]; ones_t = cmask[:, 768:896]
    QT = sb("QT", [128, 4, (NT + 1) * 128], BF16, "R1"); KT = sb("KT", [128, 4, (NT + 1) * 128], BF16, "R1")
    Vaug = sb("Vaug", [128, NT + 1, H, 65], BF16, "R1"); rwT = sb("rwT", [128, NT + 1, 4, 128], BF16, "R1")

    def vcol(c0, n=4, w=128):
        return vecT[:, c0:c0 + n].unsqueeze(2).broadcast_to([128, n, w])

    dma("sp", ident[:], ident_d, [], ["ident"])
    dma("sp", cmask[:], cmask_d, [], ["cmask"])
    dma("sp", Esel[:], esel_d, [], ["Esel"])
    A("dve", lambda e: e.tensor_copy(out=identb[:], in_=ident[:]), ["ident"], ["identb"])
    dma("sp", xt[0:NVEC, 0:128], vecs, [], ["xt"])
    A("pe", lambda e: e.transpose(out=ps[0][:, 0:NVEC], in_=xt[0:NVEC, 0:128], identity=ident[0:NVEC, 0:NVEC]),
      ["xt", "ident"], [("ps", 0)])
    A("dve", lambda e: e.tensor_copy(out=vecT[:], in_=ps[0][:, 0:NVEC]), [("ps", 0)], ["vecT"])
    c_sb = xsn[0:17, :]
    dma("sp", c_sb, c17, [], ["xsn"])
    A("act", lambda e: e.activation(out=c_sb, in_=c_sb, func=AF.Silu), ["xsn"], ["xsn"])
    for kc in range(8):
        A("pe", lambda e, kc=kc: e.transpose(out=ps[1][:, kc * 17:(kc + 1) * 17], in_=xsn[0:17, kc * 128:(kc + 1) * 128],
                                             identity=ident[0:17, 0:17]), ["xsn", "ident"], [("ps", 1)])
    A("dve", lambda e: e.tensor_copy(out=scT[:], in_=ps[1][:, 0:136].rearrange("p (k b) -> p k b", k=8)), [("ps", 1)], ["scT"])
    A("dve", lambda e: e.memset(Vaug[:, :, :, 64:65], 1.0), [], ["Vaug"])

    wada = [sb("wada%d" % i, [128, 8, D], BF16) for i in range(2)]
    ssh_sb = sb("ssh_sb", [DB, RIN])

    def ada_block(col0, buf):
        for half in range(2):
            A("pool", lambda e, half=half: e.dma_start(
                out=wada[buf][:, 4 * half:4 * half + 4, :],
                in_=w_ada[512 * half:512 * half + 512, col0:col0 + D].rearrange("(kc p) n -> p kc n", p=128)),
              [], ["wada%d" % buf], dma=True)

    def ada_fm(buf, bank, dst, bias_col, gain_col):
        wt = wada[buf]
        for c in range(8):
            for kc in range(8):
                A("pe", lambda e, c=c, kc=kc: e.matmul(ps[bank][:, c * 17:(c + 1) * 17], lhsT=wt[:, kc, c * 128:(c + 1) * 128],
                                                       rhs=scT[:, kc, :], start=(kc == 0), stop=(kc == 7)),
                  ["wada%d" % buf, "scT"], [("ps", bank)])
        pv = ps[bank][:, 0:136].rearrange("p (c b) -> p c b", c=8)
        dk = "modT_" + dst
        A("dve", lambda e: e.tensor_tensor(out=modT[dst][:], in0=pv, in1=vcol(bias_col, 8, 17), op=ALU.add), [("ps", bank), "vecT"], [dk])
        if gain_col is not None:
            A("dve", lambda e: e.scalar_tensor_tensor(out=modT[dst][:], in0=modT[dst][:], scalar=1.0, in1=vcol(gain_col, 8, 17),
                                                      op0=ALU.add, op1=ALU.mult), [dk, "vecT"], [dk])

    ada_block(1 * D, 0); ada_block(0 * D, 1)
    ada_fm(0, 2, "A1", V_BSC1, V_G1); ada_fm(1, 3, "B1", V_BSH1, None)
    ada_block(4 * D, 0); ada_block(3 * D, 1)
    ada_fm(0, 2, "A2", V_BSC2, V_G2); ada_fm(1, 3, "B2", V_BSH2, None)
    dma("sp", ssh_sb[:], sshift, [], ["ssh_sb"])
    for c in range(14):
        A("pe", lambda e, c=c: e.transpose(out=ps[4][:, c * 16:(c + 1) * 16], in_=ssh_sb[:, c * 128:(c + 1) * 128],
                                           identity=ident[0:16, 0:16]), ["ssh_sb", "ident"], [("ps", 4)])
    A("dve", lambda e: e.tensor_copy(out=sshT[:], in_=ps[4][:, 0:224].rearrange("p (c b) -> p c b", c=14)), [("ps", 4)], ["sshT"])

    BS0 = dict(xt=xt, xsn=xsn, hT=hT, ss=ss, rstd=rstd, sfx="")

    def rms_hT(ti, modA, modB, bs=None):
        bs = bs or BS0
        sample = (ti == NT)
        dma("sp", bs["xt"][:], xsm if sample else xp[ti * 128:(ti + 1) * 128, :], [], ["xt" + bs["sfx"]])
        rms_from(bs["xt"][:], "xt" + bs["sfx"], modA, modB, sample, bs)

    def rms_from(x_ap, xkey, modA, modB, sample, bs=None):
        bs = bs or BS0
        return _rms_from(x_ap, xkey, modA, modB, sample, bs["xsn"], bs["hT"], bs["ss"], bs["rstd"], bs["sfx"])

    def _rms_from(x_ap, xkey, modA, modB, sample, xsn, hT, ss, rstd, sfx):
        A("act", lambda e: e.activation(out=xsn[:], in_=x_ap, func=AF.Square, accum_out=ss[:]), [xkey], ["xsn" + sfx, "ss" + sfx])
        A("dve", lambda e: e.tensor_scalar(out=rstd[:], in0=ss[:], scalar1=1.0 / D, scalar2=1e-6, op0=ALU.mult, op1=ALU.add),
          ["ss" + sfx], ["rstd" + sfx])
        A("act", lambda e: e.activation(out=rstd[:], in_=rstd[:], func=AF.Sqrt), ["rstd" + sfx], ["rstd" + sfx])
        A("dve", lambda e: e.reciprocal(out=rstd[:], in_=rstd[:]), ["rstd" + sfx], ["rstd" + sfx])
        A("act", lambda e: e.activation(out=xsn[:], in_=x_ap, func=AF.Copy, scale=rstd[:]), [xkey, "rstd" + sfx], ["xsn" + sfx])
        for c in range(8):
            A("pe", lambda e, c=c: e.transpose(out=ps[c // 4][:, (c % 4) * 128:(c % 4 + 1) * 128], in_=xsn[:, c * 128:(c + 1) * 128],
                                               identity=ident[:]), ["xsn" + sfx, "ident"], [("ps", c // 4)])
        mA, mB = modT[modA], modT[modB]
        for g in range(2):
            if sample:
                pv = ps[g][:].rearrange("p (c b t) -> p c b t", c=4, b=DB)
                o = hT[:, 4 * g:4 * g + 4, :].rearrange("p c (b t) -> p c b t", b=DB)
                a_ = mA[:, 4 * g:4 * g + 4, 1:17].unsqueeze(3).broadcast_to([128, 4, DB, T])
                b_ = mB[:, 4 * g:4 * g + 4, 1:17].unsqueeze(3).broadcast_to([128, 4, DB, T])
                tmp = xsn[:, 512 * g:512 * g + 512].rearrange("p (c b t) -> p c b t", c=4, b=DB)
            else:
                pv = ps[g][:].rearrange("p (c t) -> p c t", c=4)
                o = hT[:, 4 * g:4 * g + 4, :]
                a_ = mA[:, 4 * g:4 * g + 4, 0:1].broadcast_to([128, 4, 128])
                b_ = mB[:, 4 * g:4 * g + 4, 0:1].broadcast_to([128, 4, 128])
                tmp = xsn[:, 512 * g:512 * g + 512].rearrange("p (c t) -> p c t", c=4)
            A("dve", lambda e, pv=pv, a_=a_, tmp=tmp: e.tensor_tensor(out=tmp, in0=pv, in1=a_, op=ALU.mult),
              [("ps", g), "modT_" + modA, "xsn" + sfx], ["xsn" + sfx])
            A("dve", lambda e, o=o, b_=b_, tmp=tmp: e.tensor_tensor(out=o, in0=tmp, in1=b_, op=ALU.add),
              ["xsn" + sfx, "modT_" + modB], [("hT" + sfx, g)])

    new_phase()
    winq = sb("winq", [128, 8, 1536], BF16)
    for kc in range(8):
        for c0 in (0, 768):
            A("pool", lambda e, kc=kc, c0=c0: e.dma_start(out=winq[:, kc, c0:c0 + 768], in_=w_in[kc * 128:(kc + 1) * 128, c0:c0 + 768]),
              [], [("winq", kc)], dma=True)
    xtB = sb("xt1", [128, D]); xsnB = sb("xsn1", [128, D]); hTB = sb("hT1", [128, 8, 128], BF16)
    ssB = sb("ss1", [128, 1]); rstdB = sb("rstd1", [128, 1])
    BS = [BS0, dict(xt=xtB, xsn=xsnB, hT=hTB, ss=ssB, rstd=rstdB, sfx="1")]
    QKV = [tuple(sb("%s%d" % (n, i), [128, 512]) for n in ("qs", "ks", "vs")) for i in range(2)]
    ropeB = [ropet, sb("ropet1", [128, 128])]
    rtmp = [sb("rtmp%d" % i, [128, H, 8]) for i in range(2)]

    def rope(bank, dst, dkey, rt, rtk):
        psb = ps[bank]
        A("act", lambda e: e.activation(out=dst[:], in_=psb[:], func=AF.Copy), [("ps", bank)], [dkey])
        pv = psb[:].rearrange("p (h e) -> p h e", h=H)
        dv = dst[:].rearrange("p (h e) -> p h e", h=H)
        cos = rt[:, 0:64].rearrange("p (h e) -> p h e", h=H)
        sin = rt[:, 64:128].rearrange("p (h e) -> p h e", h=H)
        t1 = pv[:, :, 0:8]; t2 = pv[:, :, 8:16]
        rk = [("ps", bank), rtk]
        A("dve", lambda e: e.tensor_tensor(out=rtmp[0][:], in0=t1, in1=cos, op=ALU.mult), rk, ["rtmp0"])
        A("dve", lambda e: e.tensor_tensor(out=rtmp[1][:], in0=t2, in1=sin, op=ALU.mult), rk, ["rtmp1"])
        A("dve", lambda e: e.tensor_tensor(out=dv[:, :, 0:8], in0=rtmp[0][:], in1=rtmp[1][:], op=ALU.subtract),
          ["rtmp0", "rtmp1", dkey], [dkey])
        A("dve", lambda e: e.tensor_tensor(out=rtmp[0][:], in0=t1, in1=sin, op=ALU.mult), rk, ["rtmp0"])
        A("dve", lambda e: e.tensor_tensor(out=rtmp[1][:], in0=t2, in1=cos, op=ALU.mult), rk, ["rtmp1"])
        A("dve", lambda e: e.tensor_tensor(out=dv[:, :, 8:16], in0=rtmp[0][:], in1=rtmp[1][:], op=ALU.add),
          ["rtmp0", "rtmp1", dkey], [dkey])

    def p1a_front(n_, ti):
        bs = BS[n_ % 2]
        dma("sp", ropeB[n_ % 2][:], rope_d[ti], [], ["ropet%d" % (n_ % 2)])
        rms_hT(ti, "A1", "B1", bs)

    def p1a_mm(n_, ti):
        bs = BS[n_ % 2]
        hT_ = bs["hT"]
        for g in range(3):
            for kc in range(8):
                A("pe", lambda e, g=g, kc=kc, hT_=hT_: e.matmul(ps[2 + g][:], lhsT=hT_[:, kc, :], rhs=winq[:, kc, 512 * g:512 * g + 512],
                                                                start=(kc == 0), stop=(kc == 7)), ["hT" + bs["sfx"], ("winq", kc)], [("ps", 2 + g)])

    def p1a_back(n_, ti):
        sample = (ti == NT)
        qs, ks, vs = QKV[n_ % 2]
        qk, kk_, vk = ("qs%d" % (n_ % 2), "ks%d" % (n_ % 2), "vs%d" % (n_ % 2))
        rope(2, qs, qk, ropeB[n_ % 2], "ropet%d" % (n_ % 2)); rope(3, ks, kk_, ropeB[n_ % 2], "ropet%d" % (n_ % 2))
        A("act", lambda e: e.activation(out=vs[:], in_=ps[4][:], func=AF.Copy), [("ps", 4)], [vk])
        if sample:
            dma("sp", o_knew, ks[:], [kk_], ["o_knew"], final=True)
            dma("sp", o_vnew, vs[:], [vk], ["o_vnew"], final=True)
        else:
            dma("sp", o_kwin[ti * 128:(ti + 1) * 128, :], ks[:], [kk_], [("o_kwin", ti)], final=True)
            dma("sp", o_vwin[ti * 128:(ti + 1) * 128, :], vs[:], [vk], [("o_vwin", ti)], final=True)
        for (src, skey, dst, dkey, bank, scl) in ((qs, qk, QT, "QT", 5, 0.125), (ks, kk_, KT, "KT", 6, 1.0)):
            for c in range(4):
                A("pe", lambda e, src=src, c=c, bank=bank: e.transpose(out=ps[bank][:, 128 * c:128 * c + 128], in_=src[:, 128 * c:128 * c + 128],
                                                                       identity=ident[:]), [skey, "ident"], [("ps", bank)])
            A("act", lambda e, dst=dst, bank=bank, scl=scl: e.activation(
                out=dst[:, :, ti * 128:(ti + 1) * 128], in_=ps[bank][:].rearrange("p (c t) -> p c t", c=4), func=AF.Copy, scale=scl),
              [("ps", bank)], [(dkey, ti)])
        A("act", lambda e: e.activation(out=Vaug[:, ti, :, 0:64], in_=vs[:].rearrange("p (h e) -> p h e", h=H), func=AF.Copy),
          [vk], [("Vaug", ti)])

    TL1A = [0, NT] if probe == "quick" else list(range(NT + 1))
    p1a_front(0, TL1A[0])
    for n_, ti in enumerate(TL1A):
        p1a_mm(n_, ti)
        if n_ + 1 < len(TL1A):
            p1a_front(n_ + 1, TL1A[n_ + 1])
        p1a_back(n_, ti)

    new_phase()
    Pm = sb("Pm", [128, 14, 128])

    def t4(name, dt=F32):
        return sb(name, [128, 4, 128], dt)

    gT = t4("gT"); bsum = t4("bsum"); onT = t4("onT")
    O32 = sb("O32", [128, H, 64]); Osq = sb("Osq", [128, H, 64]); gst = sb("gst", [128, 4, H])
    KEEP_1C = ptr["R2"] - R2_0
    winp = sb("winp", [128, 8, RIN], BF16)
    for kc in range(8):
        for c0 in (0, 896):
            A("pool", lambda e, kc=kc, c0=c0: e.dma_start(out=winp[:, kc, c0:c0 + 896], in_=w_in[kc * 128:(kc + 1) * 128, 1536 + c0:1536 + c0 + 896]),
              [], [("winp", kc)], dma=True)
    w2a2 = sb("w2a2", [128, 512], BF16); g2w = sb("g2w", [128, 512], BF16)
    A("pool", lambda e: e.dma_start(out=w2a2[:], in_=w2a2_d), [], ["w2a2"], dma=True)
    A("pool", lambda e: e.dma_start(out=g2w[:], in_=g2_d), [], ["g2w"], dma=True)
    PTx = sb("PTx", [128, 14, 129]); dtmp = sb("dtmp", [128, 14, 128])
    Ptok = dtmp[:].rearrange("p c t -> p (c t)")
    la = sb("la", [128, 128], BF16); sgl = sb("sgl", [128, 128], BF16)
    ld = t4("ld"); alpha = t4("alpha"); Lc = t4("Lc"); kxk = t4("kxk"); nrm = t4("nrm"); kk = t4("kk"); kmod = t4("kmod")
    bb = t4("bb"); tmp4 = t4("tmp4"); Wt = t4("Wt"); Winv = t4("Winv"); Wend = t4("Wend")
    Bh, Kh = kxk, nrm
    AR = sb("AR", [128, 4, 2, 128], BF16); Bt = t4("Bt", BF16); Kt = t4("Kt", BF16)
    Vtok = sb("Vtok", [128, 512], BF16); BhTok = sb("BhTok", [128, 512], BF16); KhTok = sb("KhTok", [128, 512], BF16)
    AX4 = sb("AX4", [128, 8, 4, 128], BF16)
    Lk = [sb("Lk%d" % i, [128, 4, 2, 128], BF16) for i in range(2)]
    MT32 = sb("MT32", [128, 4, 128]); MTb = sb("MTb", [128, 4, 128], BF16); NTb = sb("NTb", [128, 8, 128], BF16)
    St32 = sb("St32", [128, 4, 64]); Stb = sb("Stb", [128, 4, 64], BF16); WC = sb("WC", [128, 4])
    RHS32 = sb("RHS32", [128, 512]); RHSb = sb("RHSb", [128, 512], BF16); Ub = sb("Ub", [128, 512], BF16)
    A("dve", lambda e: e.memset(PTx[:, :, 0:1], 0.0), [], [("PTx", "carry")])
    A("dve", lambda e: e.memset(St32[:], 0.0), [], ["St32"])
    A("dve", lambda e: e.memset(Stb[:], 0.0), [], ["Stb"])

    def ptok_from_PTx():
        for c in range(14):
            A("pe", lambda e, c=c: e.transpose(out=ps[1][:, (c % 4) * 128:(c % 4 + 1) * 128], in_=PTx[:, c, 1:129], identity=ident[:]),
              [("PTx", "cur"), "ident"], [("ps", 1)])
            if c % 4 == 3 or c == 13:
                c0 = (c // 4) * 4
                n = c - c0 + 1
                A("act", lambda e, c0=c0, n=n: e.activation(out=Ptok[:, c0 * 128:(c0 + n) * 128], in_=ps[1][:, 0:128 * n], func=AF.Copy),
                  [("ps", 1)], ["dtmp"])

    def phase1b_proj(ti):
        sample = (ti == NT)
        rms_hT(ti, "A1", "B1")
        for c in range(14):
            bank = (5, 6, 7, 4)[c // 4]
            for kc in range(8):
                A("pe", lambda e, c=c, kc=kc, bank=bank: e.matmul(
                    ps[bank][:, (c % 4) * 128:(c % 4 + 1) * 128], lhsT=winp[:, kc, c * 128:(c + 1) * 128],
                    rhs=hT[:, kc, :], start=(kc == 0), stop=(kc == 7)), ["hT", ("winp", kc)], [("ps", bank)])
        for g in range(4):
            bank = (5, 6, 7, 4)[g]
            n = 4 if g < 3 else 2
            A("act", lambda e, g=g, bank=bank, n=n: e.activation(
                out=PTx[:, 4 * g:4 * g + n, 1:129], in_=ps[bank][:, 0:128 * n].rearrange("p (c t) -> p c t", c=n), func=AF.Copy),
              [("ps", bank)], [("PTx", "cur")])
        if sample or ti == NT - 1:
            ptok_from_PTx()
            if sample:
                dma("sp", o_shs, Ptok[T - 1:128:T, :], ["dtmp"], ["o_shs"], final=True)
            else:
                dma("sp", o_shp, Ptok[127:128, :], ["dtmp"], ["o_shp"], final=True)

    def rwkv_prep(ti):
        sample = (ti == NT)
        Pcur = PTx[:, :, 1:129]
        if sample:
            d4 = dtmp[:].rearrange("p c (b t) -> p c b t", b=DB); c4 = Pcur.rearrange("p c (b t) -> p c b t", b=DB)
            A("dve", lambda e: e.tensor_tensor(out=d4[:, :, :, 1:T], in0=c4[:, :, :, 0:T - 1], in1=c4[:, :, :, 1:T], op=ALU.subtract),
              [("PTx", "cur")], ["dtmp"])
            A("dve", lambda e: e.tensor_tensor(out=d4[:, :, :, 0:1], in0=sshT[:].unsqueeze(3), in1=c4[:, :, :, 0:1], op=ALU.subtract),
              [("PTx", "cur"), "sshT", "dtmp"], ["dtmp"])
        else:
            A("dve", lambda e: e.tensor_tensor(out=dtmp[:], in0=PTx[:, :, 0:128], in1=Pcur, op=ALU.subtract),
              [("PTx", "cur"), ("PTx", "carry")], ["dtmp"])
        A("dve", lambda e: e.tensor_tensor(out=dtmp[:], in0=dtmp[:], in1=vcol(V_MU, 14), op=ALU.mult), ["dtmp", "vecT"], ["dtmp"])
        A("dve", lambda e: e.tensor_tensor(out=Pm[:], in0=dtmp[:], in1=Pcur, op=ALU.add), ["dtmp", ("PTx", "cur")], ["Pm"])
        if ti < NT - 1:
            A("dve", lambda e: e.tensor_copy(out=PTx[:, :, 0:1], in_=PTx[:, :, 128:129]), [("PTx", "cur")], [("PTx", "carry")])
        A("act", lambda e: e.activation(out=la[0:64, :], in_=Pm[0:64, 12, :], func=AF.Tanh), ["Pm"], [("la", 0)])
        A("act", lambda e: e.activation(out=la[64:128, :], in_=Pm[64:128, 12, :], func=AF.Copy), ["Pm"], [("la", 1)])
        A("act", lambda e: e.activation(out=sgl[:], in_=Pm[:, 13, :], func=AF.Sigmoid), ["Pm"], ["sgl"])
        for c in range(4):
            cs = slice(128 * c, 128 * c + 128)
            A("pe", lambda e, cs=cs: e.matmul(ps[0][:, cs], lhsT=w2a2[0:64, cs], rhs=la[0:64, :], start=True, stop=True),
              ["w2a2", ("la", 0)], [("ps", 0)])
            A("pe", lambda e, cs=cs: e.matmul(ps[1][:, cs], lhsT=w2a2[64:128, cs], rhs=la[64:128, :], start=True, stop=True),
              ["w2a2", ("la", 1)], [("ps", 1)])
            A("pe", lambda e, cs=cs: e.matmul(ps[2][:, cs], lhsT=g2w[:, cs], rhs=sgl[:], start=True, stop=True), ["g2w", "sgl"], [("ps", 2)])
        for c in range(4):
            cs = slice(128 * c, 128 * c + 128)
            A("act", lambda e, c=c, cs=cs: e.activation(out=ld[:, c, :], in_=ps[0][:, cs], func=AF.Sigmoid, bias=vecT[:, V_W0 + c:V_W0 + c + 1]),
              [("ps", 0), "vecT"], [("ld", c)])
            A("act", lambda e, c=c, cs=cs: e.activation(out=alpha[:, c, :], in_=ps[1][:, cs], func=AF.Sigmoid, bias=vecT[:, V_A0 + c:V_A0 + c + 1]),
              [("ps", 1), "vecT"], [("alpha", c)])
        A("act", lambda e: e.activation(out=gT[:], in_=ps[2][:].rearrange("p (c t) -> p c t", c=4), func=AF.Copy), [("ps", 2)], ["gT"])
        A("dve", lambda e: e.tensor_scalar(out=ld[:], in0=ld[:], scalar1=-0.6065306597126334, scalar2=None, op0=ALU.mult), ["ld"], ["ld"])
        kc_ = Pm[:, 4:8, :]
        A("dve", lambda e: e.tensor_tensor(out=kxk[:], in0=kc_, in1=vcol(V_KK), op=ALU.mult), ["Pm", "vecT"], ["kxk"])
        A("act", lambda e: e.activation(out=nrm[:], in_=kxk[:], func=AF.Square), ["kxk"], ["nrm"])
        A("pe", lambda e: e.matmul(ps[3][:], lhsT=blk, rhs=nrm[:].rearrange("p c t -> p (c t)"), start=True, stop=True),
          ["cmask", "nrm"], [("ps", 3)])
        A("act", lambda e: e.activation(out=nrm[:], in_=ps[3][:].rearrange("p (c t) -> p c t", c=4), func=AF.Sqrt), [("ps", 3)], ["nrm"])
        A("dve", lambda e: e.tensor_scalar(out=nrm[:], in0=nrm[:], scalar1=1e-12, scalar2=None, op0=ALU.max), ["nrm"], ["nrm"])
        A("dve", lambda e: e.reciprocal(out=nrm[:], in_=nrm[:]), ["nrm"], ["nrm"])
        A("dve", lambda e: e.tensor_tensor(out=kk[:], in0=kxk[:], in1=nrm[:], op=ALU.mult), ["kxk", "nrm"], ["kk"])
        A("dve", lambda e: e.scalar_tensor_tensor(out=tmp4[:], in0=alpha[:], scalar=-1.0, in1=vcol(V_KA), op0=ALU.add, op1=ALU.mult),
          ["alpha", "vecT"], ["tmp4"])
        A("dve", lambda e: e.scalar_tensor_tensor(out=kmod[:], in0=tmp4[:], scalar=1.0, in1=kc_, op0=ALU.add, op1=ALU.mult),
          ["tmp4", "Pm"], ["kmod"])
        A("dve", lambda e: e.tensor_tensor(out=bb[:], in0=kk[:], in1=alpha[:], op=ALU.mult), ["kk", "alpha"], ["bb"])
        A("dve", lambda e: e.tensor_tensor(out=tmp4[:], in0=Pm[:, 0:4, :], in1=kmod[:], op=ALU.mult), ["Pm", "kmod"], ["tmp4"])
        A("dve", lambda e: e.tensor_tensor(out=tmp4[:], in0=tmp4[:], in1=vcol(V_RK), op=ALU.mult), ["tmp4", "vecT"], ["tmp4"])
        A("pe", lambda e: e.matmul(ps[3][:], lhsT=blk, rhs=tmp4[:].rearrange("p c t -> p (c t)"), start=True, stop=True),
          ["cmask", "tmp4"], [("ps", 3)])
        A("act", lambda e: e.activation(out=bsum[:], in_=ps[3][:].rearrange("p (c t) -> p c t", c=4), func=AF.Copy), [("ps", 3)], ["bsum"])

    def rwkv_chunk(ti):
        for c in range(4):
            A("dve", lambda e, c=c: e.tensor_tensor_scan(out=Lc[:, c, :], data0=ones_t, data1=ld[:, c, :], initial=0.0,
                                                         op0=ALU.mult, op1=ALU.add), ["ld", "cmask"], [("Lc", c)])
        A("act", lambda e: e.activation(out=Wt[:], in_=Lc[:], func=AF.Exp), ["Lc"], ["Wt"])
        A("act", lambda e: e.activation(out=Winv[:], in_=Lc[:], func=AF.Exp, scale=-1.0), ["Lc"], ["Winv"])
        A("dve", lambda e: e.tensor_tensor(out=tmp4[:], in0=Lc[:], in1=ld[:], op=ALU.subtract), ["Lc", "ld"], ["tmp4"])
        A("act", lambda e: e.activation(out=tmp4[:], in_=tmp4[:], func=AF.Exp), ["tmp4"], ["tmp4"])
        for c in range(4):
            A("act", lambda e, c=c: e.activation(out=Wend[:, c, :], in_=Lc[:, c, :], func=AF.Exp, scale=-1.0, bias=Lc[:, c, 127:128]),
              ["Lc"], [("Wend", c)])
        A("act", lambda e: e.activation(out=WC[:].unsqueeze(2), in_=Lc[:, :, 127:128], func=AF.Exp), ["Lc"], ["WC"])
        A("dve", lambda e: e.tensor_tensor(out=St32[:], in0=St32[:], in1=WC[:].unsqueeze(2).broadcast_to([128, 4, 64]), op=ALU.mult),
          ["St32", "WC"], ["St32"])
        A("dve", lambda e: e.scalar_tensor_tensor(out=AR[:, :, 0, :], in0=kk[:], scalar=-1.0, in1=tmp4[:], op0=ALU.mult, op1=ALU.mult),
          ["kk", "tmp4"], [("AR", 0)])
        A("dve", lambda e: e.tensor_tensor(out=AR[:, :, 1, :], in0=Pm[:, 0:4, :], in1=Wt[:], op=ALU.mult), ["Pm", "Wt"], [("AR", 1)])
        A("dve", lambda e: e.tensor_tensor(out=Bt[:], in0=bb[:], in1=Winv[:], op=ALU.mult), ["bb", "Winv"], ["Bt"])
        A("dve", lambda e: e.tensor_tensor(out=Kt[:], in0=kmod[:], in1=Winv[:], op=ALU.mult), ["kmod", "Winv"], ["Kt"])
        A("dve", lambda e: e.tensor_tensor(out=Bh[:], in0=bb[:], in1=Wend[:], op=ALU.mult), ["bb", "Wend"], ["kxk"])
        A("dve", lambda e: e.tensor_tensor(out=Kh[:], in0=kmod[:], in1=Wend[:], op=ALU.mult), ["kmod", "Wend"], ["nrm"])
        for (src, skey, dst, dkey) in ((Pm[:, 8:12, :], "Pm", Vtok, "Vtok"), (Bh[:], "kxk", BhTok, "BhTok"), (Kh[:], "nrm", KhTok, "KhTok")):
            for c in range(4):
                A("pe", lambda e, src=src, c=c: e.transpose(out=ps[3][:, 128 * c:128 * c + 128], in_=src[:, c, :], identity=ident[:]),
                  [skey, "ident"], [("ps", 3)])
            A("act", lambda e, dst=dst: e.activation(out=dst[:], in_=ps[3][:], func=AF.Copy), [("ps", 3)], [dkey])
        identb4 = ident[:].unsqueeze(1).broadcast_to([128, 4, 128])
        for g in range(2):
            for hl in range(4):
                h = 4 * g + hl
                hp, hh = h // 2, h % 2
                pr = slice(64 * hh, 64 * hh + 64)
                arv = AR[pr, hp, :, :].rearrange("p a t -> p (a t)")
                ab = 4 if hh == 0 else 6
                lb = 5 if hh == 0 else 7
                lc = slice(128 * (hl // 2), 128 * (hl // 2) + 128)
                A("pe", lambda e, pr=pr, hp=hp, arv=arv, ab=ab: e.matmul(ps[ab][:, 0:256], lhsT=Bt[pr, hp, :], rhs=arv, start=True, stop=True),
                  ["Bt", "AR"], [("ps", ab)])
                A("pe", lambda e, pr=pr, hp=hp, arv=arv, ab=ab: e.matmul(ps[ab][:, 256:512], lhsT=Kt[pr, hp, :], rhs=arv, start=True, stop=True),
                  ["Kt", "AR"], [("ps", ab)])
                A("pe", lambda e, pr=pr, hp=hp, lb=lb, lc=lc: e.matmul(ps[lb][:, lc], lhsT=AR[pr, hp, 0, :], rhs=Bt[pr, hp, :], start=True, stop=True),
                  ["Bt", "AR"], [("ps", lb)])
                A("dve", lambda e, h=h, ab=ab: e.tensor_tensor(out=AX4[:, h, :, :].rearrange("p a t -> p (a t)"), in0=ps[ab][:], in1=MT4, op=ALU.mult),
                  [("ps", ab), "cmask"], [("AX4", h)])
            for hh in range(2):
                lb = 5 if hh == 0 else 7
                A("dve", lambda e, hh=hh, lb=lb: e.tensor_tensor(out=Lk[0][:, hh::2, 0, :], in0=ps[lb][:, 0:256].rearrange("p (a t) -> p a t", a=2),
                                                                 in1=MLs.unsqueeze(1).broadcast_to([128, 2, 128]), op=ALU.mult),
                  [("ps", lb), "cmask"], [("Lk0", ("L", hh))])
            axg = AX4[:, 4 * g:4 * g + 4, 0, :]
            axk = [("AX4", 4 * g + i) for i in range(4)]
            A("act", lambda e, axg=axg: e.activation(out=Lk[0][:, :, 1, :], in_=axg, func=AF.Copy), axk, [("Lk0", "T")])
            A("dve", lambda e, axg=axg: e.tensor_tensor(out=MTb[:], in0=axg, in1=identb4, op=ALU.add), axk + ["ident"], ["MTb"])
            for lv in range(6):
                a_, b_ = Lk[lv % 2], Lk[(lv + 1) % 2]
                ak, bk = "Lk%d" % (lv % 2), "Lk%d" % ((lv + 1) % 2)
                for hl in range(4):
                    bank = 4 if hl < 2 else 6
                    c0 = 256 * (hl % 2)
                    A("pe", lambda e, a_=a_, hl=hl, bank=bank, c0=c0: e.matmul(ps[bank][:, c0:c0 + 128], lhsT=a_[:, hl, 1, :], rhs=a_[:, hl, 0, :],
                                                                               start=True, stop=True), [ak], [("ps", bank)])
                    A("pe", lambda e, a_=a_, hl=hl, bank=bank, c0=c0: e.matmul(ps[bank][:, c0 + 128:c0 + 256], lhsT=a_[:, hl, 0, :], rhs=a_[:, hl, 1, :],
                                                                               start=True, stop=True), [ak], [("ps", bank)])
                A("act", lambda e, b_=b_: e.activation(out=b_[:, 0:2, :, :].rearrange("p h a t -> p (h a t)"), in_=ps[4][:], func=AF.Copy),
                  [("ps", 4)], [(bk, 0)])
                A("dve", lambda e, b_=b_: e.tensor_copy(out=b_[:, 2:4, :, :].rearrange("p h a t -> p (h a t)"), in_=ps[6][:]),
                  [("ps", 6)], [(bk, 1)])
                for hl in range(4):
                    A("pe", lambda e, b_=b_, hl=hl: e.matmul(ps[5][:, 128 * hl:128 * hl + 128], lhsT=b_[:, hl, 0, :], rhs=MTb[:, hl, :],
                                                             start=True, stop=True), [(bk, hl // 2), "MTb"], [("ps", 5)])
                A("dve", lambda e: e.tensor_tensor(out=MTb[:], in0=ps[5][:].rearrange("p (h t) -> p h t", h=4), in1=MTb[:], op=ALU.add),
                  [("ps", 5), "MTb"], ["MTb"])
            A("dve", lambda e, g=g: e.tensor_tensor(out=NTb[:, 4 * g:4 * g + 4, :], in0=MTb[:], in1=identb4, op=ALU.subtract),
              ["MTb", "ident"], [("NTb", g)])
        for h in range(H):
            hp, hh = h // 2, h % 2
            pr = slice(64 * hh, 64 * hh + 64); hs = slice(64 * h, 64 * h + 64)
            A("pe", lambda e, pr=pr, hp=hp, hs=hs: e.matmul(ps[7][:, hs], lhsT=AR[pr, hp, 0, :], rhs=Stb[pr, hp, :], start=True, stop=False),
              ["AR", "Stb"], [("ps", 7)])
            A("pe", lambda e, h=h, hs=hs: e.matmul(ps[7][:, hs], lhsT=AX4[:, h, 2, :], rhs=Vtok[:, hs], start=False, stop=True),
              [("AX4", h), "Vtok"], [("ps", 7)])
        A("act", lambda e: e.activation(out=RHSb[:], in_=ps[7][:], func=AF.Copy), [("ps", 7)], ["RHSb"])
        A("dve", lambda e: e.tensor_copy(out=RHS32[:], in_=ps[7][:]), [("ps", 7)], ["RHS32"])
        for h in range(H):
            hs = slice(64 * h, 64 * h + 64)
            A("pe", lambda e, h=h, hs=hs: e.matmul(ps[6][:, hs], lhsT=NTb[:, h, :], rhs=RHSb[:, hs], start=True, stop=True),
              [("NTb", h // 4), "RHSb"], [("ps", 6)])
        A("dve", lambda e: e.tensor_tensor(out=Ub[:], in0=ps[6][:], in1=RHS32[:], op=ALU.add), [("ps", 6), "RHS32"], ["Ub"])
        for h in range(H):
            hp, hh = h // 2, h % 2
            pr = slice(64 * hh, 64 * hh + 64); hs = slice(64 * h, 64 * h + 64)
            A("pe", lambda e, pr=pr, hp=hp, hs=hs: e.matmul(ps[7][:, hs], lhsT=AR[pr, hp, 1, :], rhs=Stb[pr, hp, :], start=True, stop=False),
              ["AR", "Stb"], [("ps", 7)])
            A("pe", lambda e, h=h, hs=hs: e.matmul(ps[7][:, hs], lhsT=AX4[:, h, 1, :], rhs=Ub[:, hs], start=False, stop=False),
              [("AX4", h), "Ub"], [("ps", 7)])
            A("pe", lambda e, h=h, hs=hs: e.matmul(ps[7][:, hs], lhsT=AX4[:, h, 3, :], rhs=Vtok[:, hs], start=False, stop=True),
              [("AX4", h), "Vtok"], [("ps", 7)])
        A("act", lambda e: e.activation(out=O32[:].rearrange("p h i -> p (h i)"), in_=ps[7][:], func=AF.Copy), [("ps", 7)], ["O32"])
        for h in range(H):
            hp, hh = h // 2, h % 2
            pr = slice(64 * hh, 64 * hh + 64); hs = slice(64 * h, 64 * h + 64)
            A("pe", lambda e, pr=pr, hp=hp, hs=hs: e.matmul(ps[5][pr, 256 + 64 * hp:256 + 64 * hp + 64], lhsT=BhTok[:, hs], rhs=Ub[:, hs],
                                                            start=True, stop=False), ["BhTok", "Ub"], [("ps", 5)])
            A("pe", lambda e, pr=pr, hp=hp, hs=hs: e.matmul(ps[5][pr, 256 + 64 * hp:256 + 64 * hp + 64], lhsT=KhTok[:, hs], rhs=Vtok[:, hs],
                                                            start=False, stop=True), ["KhTok", "Vtok"], [("ps", 5)])
        A("dve", lambda e: e.tensor_tensor(out=St32[:], in0=St32[:], in1=ps[5][:, 256:512].rearrange("p (c i) -> p c i", c=4), op=ALU.add),
          ["St32", ("ps", 5)], ["St32"])
        A("act", lambda e: e.activation(out=Stb[:], in_=St32[:], func=AF.Copy), ["St32"], ["Stb"])

    def wkv_prompt_out():
        stv = St32[:].rearrange("p c i -> p (c i)")
        for q in range(2):
            A("pe", lambda e, q=q: e.transpose(out=ps[3][:, 128 * q:128 * q + 128], in_=stv[:, 128 * q:128 * q + 128], identity=ident[:]),
              ["St32", "ident"], [("ps", 3)])
        A("act", lambda e: e.activation(out=RHS32[:, 0:256], in_=ps[3][:, 0:256], func=AF.Copy), [("ps", 3)], ["RHS32"])
        for q in range(2):
            for hpp in range(2):
                hp = 2 * q + hpp
                dst = o_wkvp[2 * hp:2 * hp + 2, :, :].rearrange("hh i j -> i hh j")
                src = RHS32[64 * hpp:64 * hpp + 64, 128 * q:128 * q + 128].rearrange("p (hh j) -> p hh j", hh=2)
                dma("sp", dst, src, ["RHS32"], [("o_wkvp", hp)], final=True)


    def rwkv_out(ti):
        A("dve", lambda e: e.tensor_reduce(out=gst[:, 0, :], in_=O32[:], axis=AX.X, op=ALU.add), ["O32"], [("gst", 0)])
        A("act", lambda e: e.activation(out=Osq[:], in_=O32[:], func=AF.Square), ["O32"], ["Osq"])
        A("dve", lambda e: e.tensor_reduce(out=gst[:, 1, :], in_=Osq[:], axis=AX.X, op=ALU.add), ["Osq"], [("gst", 1)])
        A("dve", lambda e: e.tensor_scalar(out=gst[:, 0, :], in0=gst[:, 0, :], scalar1=1.0 / E, scalar2=None, op0=ALU.mult), [("gst", 0)], [("gst", 0)])
        A("dve", lambda e: e.tensor_tensor(out=gst[:, 2, :], in0=gst[:, 0, :], in1=gst[:, 0, :], op=ALU.mult), [("gst", 0)], [("gst", 2)])
        A("dve", lambda e: e.scalar_tensor_tensor(out=gst[:, 3, :], in0=gst[:, 1, :], scalar=1.0 / E, in1=gst[:, 2, :], op0=ALU.mult, op1=ALU.subtract),
          [("gst", 1), ("gst", 2)], [("gst", 3)])
        A("dve", lambda e: e.tensor_scalar(out=gst[:, 3, :], in0=gst[:, 3, :], scalar1=64e-5, scalar2=None, op0=ALU.add), [("gst", 3)], [("gst", 3)])
        A("act", lambda e: e.activation(out=gst[:, 3, :], in_=gst[:, 3, :], func=AF.Sqrt), [("gst", 3)], [("gst", 3)])
        A("dve", lambda e: e.reciprocal(out=gst[:, 3, :], in_=gst[:, 3, :]), [("gst", 3)], [("gst", 3)])
        mb = gst[:, 0, :].unsqueeze(2).broadcast_to([128, H, E]); rb = gst[:, 3, :].unsqueeze(2).broadcast_to([128, H, E])
        A("dve", lambda e: e.tensor_tensor(out=Osq[:], in0=O32[:], in1=mb, op=ALU.subtract), ["O32", ("gst", 0)], ["Osq"])
        A("dve", lambda e: e.tensor_tensor(out=Osq[:], in0=Osq[:], in1=rb, op=ALU.mult), ["Osq", ("gst", 3)], ["Osq"])
        ov = Osq[:].rearrange("p h i -> p (h i)")
        for c in range(4):
            A("pe", lambda e, c=c: e.transpose(out=ps[3][:, 128 * c:128 * c + 128], in_=ov[:, 128 * c:128 * c + 128], identity=ident[:]),
              ["Osq", "ident"], [("ps", 3)])
        A("dve", lambda e: e.tensor_tensor(out=onT[:], in0=ps[3][:].rearrange("p (c t) -> p c t", c=4), in1=vcol(V_LG), op=ALU.mult),
          [("ps", 3), "vecT"], ["onT"])
        A("dve", lambda e: e.tensor_tensor(out=onT[:], in0=onT[:], in1=vcol(V_LB), op=ALU.add), ["onT", "vecT"], ["onT"])
        A("dve", lambda e: e.tensor_tensor(out=bsum[:], in0=bsum[:], in1=Pm[:, 8:12, :], op=ALU.mult), ["bsum", "Pm"], ["bsum"])
        A("dve", lambda e: e.tensor_tensor(out=onT[:], in0=onT[:], in1=bsum[:], op=ALU.add), ["onT", "bsum"], ["onT"])
        A("dve", lambda e: e.tensor_tensor(out=rwT[:, ti, :, :], in0=onT[:], in1=gT[:], op=ALU.mult), ["onT", "gT"], [("rwT", ti)])

    def sample_vectors_out():
        A("act", lambda e: e.activation(out=Wt[:], in_=ld[:], func=AF.Exp), ["ld"], ["Wt"])
        A("dve", lambda e: e.tensor_scalar(out=Winv[:], in0=kk[:], scalar1=-1.0, scalar2=None, op0=ALU.mult), ["kk"], ["Winv"])
        srcs = ((Pm[:, 0:4, :], "Pm"), (Wt[:], "Wt"), (kmod[:], "kmod"), (Pm[:, 8:12, :], "Pm"), (Winv[:], "Winv"), (bb[:], "bb"))
        for v, (src, skey) in enumerate(srcs):
            for c in range(4):
                A("pe", lambda e, src=src, c=c: e.transpose(out=ps[3][:, 128 * c:128 * c + 128], in_=src[:, c, :], identity=ident[:]),
                  [skey, "ident"], [("ps", 3)])
            A("act", lambda e: e.activation(out=RHS32[:], in_=ps[3][:], func=AF.Copy), [("ps", 3)], ["RHS32"])
            dma("sp", scr_v[v], RHS32[:], ["RHS32"], [("scr_v", v)])

    for ti in ([0, NT - 1, NT] if probe == "quick" else range(NT + 1)):
        phase1b_proj(ti)
        rwkv_prep(ti)
        if ti < NT:
            rwkv_chunk(ti)
            rwkv_out(ti)
        else:
            sample_vectors_out()
        if ti == NT - 1:
            wkv_prompt_out()

    new_phase(KEEP_1C)
    Ss = sb("Ss", [128, E, E]); tmpS = sb("tmpS", [128, E, E]); vkS = sb("vkS", [128, E, E])
    vec6 = sb("vec6", [128, 6, T, E]); sa = sb("sa", [128, E]); outs = sb("outs", [128, T, E])
    dma("sp", Ss[:].rearrange("p i j -> p (i j)"), swkv, [], ["Ss"])
    for b in range(DB):
        for v in range(6):
            dma("sp", vec6[8 * b:8 * b + 8, v, :, :], scr_v[v, 8 * b:8 * b + 8, :].rearrange("t (h j) -> h t j", h=H),
                [("scr_v", v)], [("vec6", (b, v))])
    bi = lambda ap: ap.unsqueeze(1).broadcast_to([128, E, E])
    bj = lambda ap: ap.unsqueeze(2).broadcast_to([128, E, E])
    rec = []
    NQ = 4; QI = E // NQ
    bq = lambda ap: ap.unsqueeze(1).broadcast_to([128, QI, E])
    for t in range(T):
        r_, w_, k_, v_, nk_, ka_ = (vec6[:, v, t, :] for v in range(6))
        rec.append(lambda v_=v_, k_=k_: A("pool", lambda e: e.tensor_tensor(out=vkS[:], in0=bj(v_), in1=bi(k_), op=ALU.mult), ["vec6"], ["vkS"]))
        def q_ops(kind, t=t, r_=r_, w_=w_, nk_=nk_, ka_=ka_):
            for q in range(NQ):
                isl = slice(QI * q, QI * q + QI)
                S_, T_ = Ss[:, isl, :], tmpS[:, isl, :]
                sk, tk, ak = ("Ss", q), ("tmpS", q), ("sa", q)
                if kind == 0:
                    f = lambda S_=S_, T_=T_, sk=sk, tk=tk: A("dve", lambda e: e.tensor_tensor(out=T_, in0=S_, in1=bq(nk_), op=ALU.mult), [sk, "vec6"], [tk])
                elif kind == 1:
                    f = lambda T_=T_, isl=isl, tk=tk, ak=ak: A("dve", lambda e: e.tensor_reduce(out=sa[:, isl], in_=T_, axis=AX.X, op=ALU.add), [tk], [ak])
                elif kind == 2:
                    f = lambda S_=S_, sk=sk: A("dve", lambda e: e.tensor_tensor(out=S_, in0=S_, in1=bq(w_), op=ALU.mult), [sk, "vec6"], [sk])
                elif kind == 3:
                    f = lambda T_=T_, isl=isl, tk=tk, ak=ak: A("dve", lambda e: e.tensor_tensor(
                        out=T_, in0=sa[:, isl].unsqueeze(2).broadcast_to([128, QI, E]), in1=bq(ka_), op=ALU.mult), [ak, "vec6"], [tk])
                elif kind == 4:
                    f = lambda S_=S_, T_=T_, sk=sk, tk=tk: A("dve", lambda e: e.tensor_tensor(out=S_, in0=S_, in1=T_, op=ALU.add), [sk, tk], [sk])
                elif kind == 5:
                    f = lambda S_=S_, isl=isl, sk=sk: A("dve", lambda e: e.tensor_tensor(out=S_, in0=S_, in1=vkS[:, isl, :], op=ALU.add), [sk, "vkS"], [sk])
                elif kind == 6:
                    f = lambda S_=S_, T_=T_, sk=sk, tk=tk: A("dve", lambda e: e.tensor_tensor(out=T_, in0=S_, in1=bq(r_), op=ALU.mult), [sk, "vec6"], [tk])
                else:
                    f = lambda T_=T_, isl=isl, tk=tk, q=q: A("dve", lambda e: e.tensor_reduce(out=outs[:, t, isl], in_=T_, axis=AX.X, op=ALU.add),
                                                             [tk], [("outs", (t, q))])
                rec.append(f)
        for kind in range(8):
            q_ops(kind)

    units = []
    if SAMPLE_ATTN_DONE:
        NLB = 4
        kcf = [sb("kc_f%d" % i, [128, 512]) for i in range(NLB)]; vcf = [sb("vc_f%d" % i, [128, 512]) for i in range(NLB)]
        NU = 4
        KTcs = [sb("KTc%d" % i, [128, 4, 128], BF16) for i in range(NU)]
        Vcs = [sb("Vc%d" % i, [128, H, 65], BF16) for i in range(NU)]
        PTss = [sb("PTs%d" % i, [128, H, T], BF16) for i in range(NU)]
        QTbd = sb("QTbd", [128, 4, DB, 2 * T], BF16)
        A("dve", lambda e: e.memset(QTbd[:], 0.0), [], ["QTbd"])
        qsv = lambda pr: QT[pr, :, NT * 128:NT * 128 + 128].rearrange("p c (b q) -> p c b q", b=DB)
        A("act", lambda e: e.activation(out=QTbd[0:64, :, :, 0:T], in_=qsv(slice(0, 64)), func=AF.Copy), [("QT", NT), "QTbd"], ["QTbd"])
        A("act", lambda e: e.activation(out=QTbd[64:128, :, :, T:2 * T], in_=qsv(slice(64, 128)), func=AF.Copy), [("QT", NT), "QTbd"], ["QTbd"])
        accB = [sb("accB%d" % i, [64, H * 65]) for i in range(2)]; cnts = sb("cnts", [128, 16, T]); cntn = sb("cntn", [128, DB, T])
        dma("sp", cnts[:].rearrange("p a q -> p (a q)"), cnts_d, [], ["cnts"])
        dma("sp", cntn[:].rearrange("p a q -> p (a q)"), cntn_d, [], ["cntn"])
        for i in range(NU):
            A("dve", lambda e, i=i: e.memset(Vcs[i][:, :, 64:65], 1.0), [], ["Vc%d" % i])
        QS0 = NT * 128
        TB = (0, 2, 4, 6); SVB = (1, 3, 5, 7)

        def sattn_unit(u, b, tl):
            p_ = u % NU
            KTc, Vc, PTs = KTcs[p_], Vcs[p_], PTss[p_]
            ktk, vkk, ptk = "KTc%d" % p_, "Vc%d" % p_, "PTs%d" % p_
            tb, sv = TB[p_], SVB[p_]
            if tl == 16:
                npart = 128
                kT = lambda hp: KT[:, hp, QS0:QS0 + 128]
                vT = lambda g_: Vaug[:, NT, 4 * g_:4 * g_ + 4, :].rearrange("p h e -> p (h e)")
                kkeys = [("KT", NT)]; vkeys = [("Vaug", NT)]
                msk = cntn[:, b, :]; mkey = "cntn"
            else:
                r = tl
                m0, npart = (0, 128) if r < 8 else (96, 32)
                l_ = sattn_unit.nload % NLB; sattn_unit.nload += 1
                kc_f, vc_f = kcf[l_], vcf[l_]
                kck, vck = "kc_f%d" % l_, "vc_f%d" % l_
                dma("sp", kc_f[0:npart, :], ck[b, 16 * m0 + r:2048:16, :], [], [kck])
                dma("sp", vc_f[0:npart, :], cv[b, 16 * m0 + r:2048:16, :], [], [vck])
                for c in range(4):
                    A("pe", lambda e, c=c: e.transpose(out=ps[tb][:, 128 * c:128 * c + npart], in_=kc_f[0:npart, 128 * c:128 * c + 128],
                                                       identity=ident[0:npart, 0:npart]), [kck, "ident"], [("ps", tb)])
                A("act", lambda e: e.activation(out=KTc[:, :, 0:npart], in_=ps[tb][:].rearrange("p (c k) -> p c k", c=4)[:, :, 0:npart], func=AF.Copy),
                  [("ps", tb)], [ktk])
                A("act", lambda e: e.activation(out=Vc[0:npart, :, 0:64], in_=vc_f[0:npart, :].rearrange("p (h e) -> p h e", h=H), func=AF.Copy),
                  [vck], [vkk])
                kT = lambda hp: KTc[:, hp, 0:npart]
                vT = lambda g_: Vc[0:npart, 4 * g_:4 * g_ + 4, :].rearrange("p h e -> p (h e)")
                kkeys = [ktk]; vkeys = [vkk]
                msk = cnts[0:npart, tl, :]; mkey = "cnts"
            for hp in range(4):
                A("pe", lambda e, hp=hp: e.matmul(ps[sv][0:npart, 2 * T * hp:2 * T * hp + 2 * T], lhsT=kT(hp), rhs=QTbd[:, hp, b, :], start=True, stop=True),
                  kkeys + ["QTbd"], [("ps", sv)])
            A("act", lambda e: e.activation(out=PTs[0:npart, :, :].rearrange("p h q -> p (h q)"), in_=ps[sv][0:npart, 0:H * T], func=AF.Exp), [("ps", sv)], [ptk])
            A("dve", lambda e: e.tensor_tensor(out=PTs[0:npart, :, :], in0=PTs[0:npart, :, :], in1=msk.unsqueeze(1).broadcast_to([npart, H, T]), op=ALU.mult),
              [ptk, mkey], [ptk])
            pflat = PTs[0:npart, :, :].rearrange("p h q -> p (h q)")
            A("pe", lambda e: e.matmul(ps[sv][0:64, 64:324], lhsT=pflat, rhs=vT(0), start=True, stop=True), vkeys + [ptk], [("ps", sv)])
            A("pe", lambda e: e.matmul(ps[tb][0:64, 0:260], lhsT=pflat, rhs=vT(1), start=True, stop=True), vkeys + [ptk], [("ps", tb)])
            ab = accB[b % 2]; abk = "accB%d" % (b % 2)
            for g_, (bank, c0) in enumerate(((sv, 64), (tb, 0))):
                dst = ab[:, 260 * g_:260 * g_ + 260]
                if tl == 0:
                    A("dve", lambda e, dst=dst, bank=bank, c0=c0: e.tensor_copy(out=dst, in_=ps[bank][0:64, c0:c0 + 260]), [("ps", bank)], [(abk, g_)])
                else:
                    A("dve", lambda e, dst=dst, bank=bank, c0=c0: e.tensor_tensor(out=dst, in0=dst, in1=ps[bank][0:64, c0:c0 + 260], op=ALU.add),
                      [("ps", bank), (abk, g_)], [(abk, g_)])
            if tl == 16:
                for h in range(H):
                    dma("sp", scr_acc[b, :, h, :], ab[T * h:T * h + T, 65 * h:65 * h + 65], [(abk, h // 4)], [("scr_acc", (b, h))])

        sattn_unit.nload = 0
        u = 0
        for b in (range(2) if probe == "quick" else range(DB)):
            for tl in range(17):
                units.append(lambda u=u, b=b, tl=tl: sattn_unit(u, b, tl))
                u += 1
    done = 0
    for k_, th in enumerate(rec):
        th()
        want = (len(units) * (k_ + 1)) // len(rec)
        while done < want:
            units[done](); done += 1
    while done < len(units):
        units[done](); done += 1
    dma("sp", o_wkvs, Ss[:].rearrange("p i j -> p (i j)"), ["Ss"], ["o_wkvs"], final=True)
    for b in range(DB):
        dma("sp", scr_o[8 * b:8 * b + 8, :].rearrange("t (h i) -> h t i", h=H), outs[8 * b:8 * b + 8, :, :], ["outs"], [("scr_o", b)])
    dma("sp", O32[:].rearrange("p h i -> p (h i)"), scr_o, ["scr_o"], ["O32"])
    rwkv_out(NT)

    new_phase()
    X1_BYTES = (NT + 1) * D * 4
    x1 = sb("x1", [128, NT + 1, D])
    G2p = sb("G2p", [128, D]); G2s = sb("G2s", [128, D])
    KEEP_P3 = ptr["R2"] - R2_0
    G1 = sb("G1", [128, D]); G1s = sb("G1s", [128, D])
    KEEP_P2 = ptr["R2"] - R2_0
    wadah = sb("wadah", [128, 8, 512], BF16); m17 = sb("m17", [17, D]); bgb = sb("bgb", [17, 512])

    def gate_m17(gidx):
        col0 = (2 if gidx == 0 else 5) * D
        for hf in range(2):
            for q in range(2):
                A("pool", lambda e, hf=hf, q=q: e.dma_start(
                    out=wadah[:, 4 * q:4 * q + 4, :],
                    in_=w_ada[512 * q:512 * q + 512, col0 + 512 * hf:col0 + 512 * hf + 512].rearrange("(kc p) n -> p kc n", p=128)),
                  [], ["wadah"], dma=True)
            dma("sp", bgb[:], bgate[gidx, 512 * hf:512 * hf + 512].partition_broadcast(17), [], ["bgb"])
            for kc in range(8):
                A("pe", lambda e, kc=kc: e.matmul(ps[0][0:17, :], lhsT=scT[:, kc, :], rhs=wadah[:, kc, :], start=(kc == 0), stop=(kc == 7)),
                  ["scT", "wadah"], [("ps", 0)])
            A("dve", lambda e, hf=hf: e.tensor_tensor(out=m17[:, 512 * hf:512 * hf + 512], in0=ps[0][0:17, :], in1=bgb[:], op=ALU.add),
              [("ps", 0), "bgb"], [("m17", hf)])

    def gate_bcast(sel, dst, dkey):
        for hf in range(2):
            A("pe", lambda e, hf=hf: e.matmul(ps[1][:], lhsT=Esel[:, 128 * sel:128 * sel + 128], rhs=m17[:, 512 * hf:512 * hf + 512],
                                              start=True, stop=True), ["Esel", ("m17", hf)], [("ps", 1)])
            A("act", lambda e, hf=hf: e.activation(out=dst[:, 512 * hf:512 * hf + 512], in_=ps[1][:], func=AF.Copy), [("ps", 1)], [dkey])

    gate_m17(1); gate_bcast(0, G2p, "G2p"); gate_bcast(1, G2s, "G2s")
    gate_m17(0); gate_bcast(0, G1, "G1"); gate_bcast(1, G1s, "G1s")

    new_phase(KEEP_P2)
    wout = sb("wout", [128, 8, D], BF16)
    for kc in range(8):
        A("pool", lambda e, kc=kc: e.dma_start(out=wout[:, kc, :], in_=w_out[kc * 128:(kc + 1) * 128, :]), [], [("wout", kc)], dma=True)
    attT = sb("attT", [128, 4, 128], BF16); rsum = sb("rsum", [128, H])
    KEEP_2B = ptr["R2"] - R2_0
    cntm = sb("cntm", [128, 16, 128], BF16)
    for hf in range(2):
        A("pool", lambda e, hf=hf: e.dma_start(out=cntm[:, 8 * hf:8 * hf + 8, :].rearrange("p d q -> p (d q)"), in_=cnt_d[:, 1024 * hf:1024 * hf + 1024]),
          [], [("cntm", hf)], dma=True)
    PTb = sb("PTb", [128, NT, 2, 128], BF16)

    oacc = xt[:, 0:520].rearrange("p (h e) -> p h e", h=H)

    def attn_finish(ti, Gt=None, gk="G1", from_psum=True):
        for g in (range(2) if from_psum else ()):
            A("act", lambda e, g=g: e.activation(out=xt[:, 260 * g:260 * g + 260], in_=ps[6 + g][:, 0:260], func=AF.Copy), [("ps", 6 + g)], ["xt"])
        A("dve", lambda e: e.reciprocal(out=rsum[:].unsqueeze(2), in_=oacc[:, :, 64:65]), ["xt"], ["rsum"])
        A("dve", lambda e: e.tensor_tensor(out=xsn[:, 0:512].rearrange("p (h e) -> p h e", h=H), in0=oacc[:, :, 0:64],
                                           in1=rsum[:].unsqueeze(2).broadcast_to([128, H, E]), op=ALU.mult), ["xt", "rsum"], ["xsn"])
        for c in range(4):
            A("pe", lambda e, c=c: e.transpose(out=ps[5][:, 128 * c:128 * c + 128], in_=xsn[:, 128 * c:128 * c + 128], identity=ident[:]),
              ["xsn", "ident"], [("ps", 5)])
        A("act", lambda e: e.activation(out=attT[:], in_=ps[5][:].rearrange("p (c t) -> p c t", c=4), func=AF.Copy), [("ps", 5)], ["attT"])
        for hf in range(2):
            for kc in range(8):
                lhs = attT[:, kc, :] if kc < 4 else rwT[:, ti, kc - 4, :]
                A("pe", lambda e, hf=hf, kc=kc, lhs=lhs: e.matmul(ps[2 + hf][:], lhsT=lhs, rhs=wout[:, kc, 512 * hf:512 * hf + 512],
                                                                  start=(kc == 0), stop=(kc == 7)),
                  ["attT", ("rwT", ti), ("wout", kc)], [("ps", 2 + hf)])
        dma("sp", xt[:], xsm if ti == NT else xp[ti * 128:(ti + 1) * 128, :], [], ["xt"])
        for hf in range(2):
            cs = slice(512 * hf, 512 * hf + 512)
            Gt_ = G1 if Gt is None else Gt
            A("dve", lambda e, hf=hf, cs=cs, Gt_=Gt_: e.tensor_tensor(out=xsn[:, cs], in0=ps[2 + hf][:], in1=Gt_[:, cs], op=ALU.mult),
              [("ps", 2 + hf), gk, "xsn"], ["xsn"])
            A("dve", lambda e, cs=cs: e.tensor_tensor(out=x1[:, ti, cs], in0=xsn[:, cs], in1=xt[:, cs], op=ALU.add), ["xsn", "xt"], [("x1", ti)])

    def attn_prompt(qt):
        qs_ = slice(qt * 128, qt * 128 + 128)
        for hg in range(4):
            for kt0 in range(0, qt + 1, 4):
                n = min(4, qt + 1 - kt0)
                for j in range(n):
                    kt = kt0 + j
                    for hh in range(2):
                        pr = slice(64 * hh, 64 * hh + 64)
                        A("pe", lambda e, j=j, kt=kt, pr=pr, hh=hh, hg=hg: e.matmul(
                            ps[hh][:, 128 * j:128 * j + 128], lhsT=KT[pr, hg, kt * 128:kt * 128 + 128], rhs=QT[pr, hg, qs_], start=True, stop=True),
                          [("KT", kt), ("QT", qt)], [("ps", hh)])
                for hh in range(2):
                    A("act", lambda e, kt0=kt0, n=n, hh=hh: e.activation(
                        out=PTb[:, kt0:kt0 + n, hh, :], in_=ps[hh][:, 0:128 * n].rearrange("p (k q) -> p k q", k=n), func=AF.Exp),
                      [("ps", hh)], [("PTb", kt_) for kt_ in range(kt0, kt0 + n)])
                for kt in range(kt0, kt0 + n):
                    d = qt - kt
                    eng = "dve" if d % 2 == 0 else "pool"
                    A(eng, lambda e, kt=kt, d=d: e.tensor_tensor(out=PTb[:, kt, :, :], in0=PTb[:, kt, :, :],
                                                                 in1=cntm[:, d, :].unsqueeze(1).broadcast_to([128, 2, 128]), op=ALU.mult),
                      [("PTb", kt), ("cntm", d // 8)], [("PTb", kt)])
            for hh in range(2):
                h = 2 * hg + hh
                ob = ps[6 + h // 4][:, 65 * (h % 4):65 * (h % 4) + 65]
                for kt in range(qt + 1):
                    A("pe", lambda e, kt=kt, hh=hh, h=h, ob=ob: e.matmul(ob, lhsT=PTb[:, kt, hh, :], rhs=Vaug[:, kt, h, :],
                                                                         start=(kt == 0), stop=(kt == qt)),
                      [("PTb", kt), ("Vaug", kt)], [("ps", 6 + h // 4)])
        attn_finish(qt)

    for qt in ([0, 1] if probe == "quick" else range(NT)):
        attn_prompt(qt)


    if SAMPLE_ATTN_DONE:
        dma("sp", xt[:, 0:H * 65], scr_acc.rearrange("b q h e -> (b q) (h e)"), ["scr_acc"], ["xt"])
        attn_finish(NT, G1s, "G1s", from_psum=False)

    new_phase(KEEP_P3)
    ptr["R1"] = R1_0
    w1c = [sb("w1c%d" % i, [128, 8, D], BF16, "R1") for i in range(2)]
    w2c = [sb("w2c%d" % i, [128, 8, D], BF16, "R1") for i in range(2)]
    h2T = sb("h2T", [128, 8, (NT + 1) * 128], BF16)
    GT = 3
    hid = sb("hid", [128, 8, 128 * GT], BF16); rl = sb("rl", [128, 128 * GT])
    TILES = [0, 1, NT] if probe == "quick" else list(range(NT + 1))
    if not SAMPLE_ATTN_DONE:
        TILES = [t_ for t_ in TILES if t_ != NT]
    for ti in TILES:
        rms_from(x1[:, ti, :], ("x1", ti), "A2", "B2", ti == NT)
        A("act", lambda e, ti=ti: e.activation(out=h2T[:, :, ti * 128:(ti + 1) * 128], in_=hT[:], func=AF.Copy), ["hT"], [("h2T", ti)])
    groups = [TILES[i:i + GT] for i in range(0, len(TILES), GT)]
    for c in range(4):
        wb = c % 2
        for kc in range(8):
            A("pool", lambda e, kc=kc, c=c, wb=wb: e.dma_start(out=w1c[wb][:, kc, :], in_=w_ff1[kc * 128:(kc + 1) * 128, c * D:(c + 1) * D]),
              [], [("w1c%d" % wb, kc)], dma=True)
            A("pool", lambda e, kc=kc, c=c, wb=wb: e.dma_start(out=w2c[wb][:, kc, :], in_=w_ff2[c * D + kc * 128:c * D + (kc + 1) * 128, :]),
              [], [("w2c%d" % wb, kc)], dma=True)
        for gi, grp in enumerate(groups):
            contiguous = all(grp[i + 1] == grp[i] + 1 for i in range(len(grp) - 1))
            subgroups = [grp] if contiguous else [[t_] for t_ in grp]
            for sg in subgroups:
                ntok = 128 * len(sg)
                t0 = sg[0] * 128
                for fc in range(8):
                    hb = fc % 2
                    for kc in range(8):
                        A("pe", lambda e, fc=fc, kc=kc, hb=hb, t0=t0, ntok=ntok, wb=wb: e.matmul(
                            ps[hb][:, 0:ntok], lhsT=w1c[wb][:, kc, 128 * fc:128 * fc + 128], rhs=h2T[:, kc, t0:t0 + ntok],
                            start=(kc == 0), stop=(kc == 7)), [("w1c%d" % wb, kc)] + [("h2T", t_) for t_ in sg], [("ps", hb)])
                    A("act", lambda e, hb=hb, ntok=ntok: e.activation(out=rl[:, 0:ntok], in_=ps[hb][:, 0:ntok], func=AF.Relu), [("ps", hb)], ["rl"])
                    A("dve", lambda e, fc=fc, ntok=ntok: e.tensor_tensor(out=hid[:, fc, 0:ntok], in0=rl[:, 0:ntok], in1=rl[:, 0:ntok], op=ALU.mult),
                      ["rl"], [("hid", fc)])
                for tl, ti in enumerate(sg):
                    Gt = G2s if ti == NT else G2p
                    gk = "G2s" if ti == NT else "G2p"
                    for hf in range(2):
                        yb = 2 + 2 * tl + hf
                        cs = slice(512 * hf, 512 * hf + 512)
                        for fc in range(8):
                            A("pe", lambda e, fc=fc, tl=tl, yb=yb, cs=cs, wb=wb: e.matmul(ps[yb][:], lhsT=hid[:, fc, 128 * tl:128 * tl + 128], rhs=w2c[wb][:, fc, cs],
                                                                                   start=(fc == 0), stop=(fc == 7)),
                              [("hid", fc), ("w2c%d" % wb, fc)], [("ps", yb)])
                        A("dve", lambda e, yb=yb, cs=cs, Gt=Gt: e.tensor_tensor(out=xsn[:, cs], in0=ps[yb][:], in1=Gt[:, cs], op=ALU.mult),
                          [("ps", yb), gk, "xsn"], ["xsn"])
                        A("dve", lambda e, ti=ti, cs=cs: e.tensor_tensor(out=x1[:, ti, cs], in0=xsn[:, cs], in1=x1[:, ti, cs], op=ALU.add),
                          ["xsn", ("x1", ti)], [("x1", ti)])

    new_phase(X1_BYTES)
    gfb = sb("gfb", [128, D]); yo = [sb("yo%d" % i, [128, D]) for i in range(2)]
    dma("sp", gfb[:], gfin.partition_broadcast(128), [], ["gfb"])
    for n_, ti in enumerate(TILES):
        xa = x1[:, ti, :]
        y_ = yo[n_ % 2]; yk = "yo%d" % (n_ % 2)
        A("act", lambda e, xa=xa: e.activation(out=xsn[:], in_=xa, func=AF.Square, accum_out=ss[:]), [("x1", ti)], ["xsn", "ss"])
        A("dve", lambda e: e.tensor_scalar(out=rstd[:], in0=ss[:], scalar1=1.0 / D, scalar2=1e-6, op0=ALU.mult, op1=ALU.add), ["ss"], ["rstd"])
        A("act", lambda e: e.activation(out=rstd[:], in_=rstd[:], func=AF.Sqrt), ["rstd"], ["rstd"])
        A("dve", lambda e: e.reciprocal(out=rstd[:], in_=rstd[:]), ["rstd"], ["rstd"])
        A("act", lambda e, xa=xa: e.activation(out=xsn[:], in_=xa, func=AF.Copy, scale=rstd[:]), [("x1", ti), "rstd"], ["xsn"])
        A("dve", lambda e, y_=y_: e.tensor_tensor(out=y_[:], in0=xsn[:], in1=gfb[:], op=ALU.mult), ["xsn", "gfb"], [yk])
        dma("sp", o_ys if ti == NT else o_yp[ti * 128:(ti + 1) * 128, :], y_[:], [yk], [("o_y", ti)], final=True)

    P.emit()
    es.close()
    return nc


def _rope_tables():
    half = 8
    inv = (500000.0 ** (-np.arange(half, dtype=np.float32) * np.float32(2.0 / 16))).astype(np.float32)
    tab = np.zeros((17, 128, 128), np.float32)
    for ti in range(17):
        if ti < NT:
            pos = (ti * 128 + np.arange(128)).astype(np.float32)
        else:
            pos = (PAST + (np.arange(128) % T)).astype(np.float32)
        ang = pos[:, None] * inv[None, :]
        tab[ti, :, 0:64] = np.tile(np.cos(ang), (1, H))
        tab[ti, :, 64:128] = np.tile(np.sin(ang), (1, H))
    return tab


def _cmask():
    m = np.zeros((128, 896), np.float32)
    m[0:64, 0:64] = 1.0; m[64:128, 64:128] = 1.0
    i = np.arange(128)
    strictT = (i[:, None] < i[None, :]).astype(np.float32)
    inclT = (i[:, None] <= i[None, :]).astype(np.float32)
    m[:, 128:256] = strictT; m[:, 256:384] = inclT; m[:, 384:512] = strictT; m[:, 512:640] = inclT
    m[:, 640:768] = (i[None, :] < i[:, None]).astype(np.float32)
    m[:, 768:896] = 1.0
    return m


def _cnt_table():
    k = np.arange(128)[:, None, None]; d = np.arange(16)[None, :, None]; q = np.arange(128)[None, None, :]
    dl = 128 * d + q - k
    c = ((dl >= 0) & (dl <= 128)).astype(np.float32) + ((dl >= 0) & (dl <= 512) & (dl % 4 == 0)) + ((dl >= 0) & (dl <= 2048) & (dl % 16 == 0))
    return np.ascontiguousarray(c.reshape(128, 2048).astype(np.float32))


def _cnts():
    c = np.zeros((128, 16, T), np.float32)
    for r in range(16):
        m0, n = (0, 128) if r < 8 else (96, 32)
        for p in range(n):
            R = 16 * (m0 + p) + r
            for i in range(T):
                v = 0
                if R >= 1920 + i: v += 1
                if R % 4 == i % 4 and R >= 1536 + i: v += 1
                if R % 16 == i % 16 and R >= i: v += 1
                c[p, r, i] = v
    return np.ascontiguousarray(c.reshape(128, 16 * T))


def _cntn():
    c = np.zeros((128, DB, T), np.float32)
    for b in range(DB):
        for s_ in range(T):
            for i in range(T):
                d = i - s_
                if d >= 0:
                    c[T * b + s_, b, i] = 1 + (d % 4 == 0) + (d % 16 == 0)
    return np.ascontiguousarray(c.reshape(128, DB * T))


def _esel():
    e = np.zeros((17, 256), np.float32)
    e[0, 0:128] = 1.0
    for b in range(DB):
        e[1 + b, 128 + T * b:128 + T * b + T] = 1.0
    return e


_NC_CACHE = {}


def kernel(**inp):
    f = lambda a: np.ascontiguousarray(np.asarray(a, dtype=np.float32))
    x_prompt = f(inp["x_prompt"]); x_sample = f(inp["x_sample"])
    c_prompt = f(inp["c_prompt"]); c_sample = f(inp["c_sample"])
    b_ada = f(inp["b_ada"])[0]
    vec_rows = [f(inp["mu"])[0].reshape(14, 128)]
    for n in ("w0", "a0", "k_k", "k_a"):
        vec_rows.append(f(inp[n])[0].reshape(4, 128))
    vec_rows.append(f(inp["r_k"])[0].reshape(4, 128))
    for n in ("lnx_g", "lnx_b"):
        vec_rows.append(f(inp[n])[0].reshape(4, 128))
    vec_rows.append(f(inp["norm1_g"])[0].reshape(8, 128))
    vec_rows.append(f(inp["norm2_g"])[0].reshape(8, 128))
    for blk in (0, 1, 3, 4):
        vec_rows.append(b_ada[blk * D:(blk + 1) * D].reshape(8, 128))
    vecs = np.ascontiguousarray(np.concatenate(vec_rows, axis=0))
    assert vecs.shape == (NVEC, 128)
    bgate = np.ascontiguousarray(np.stack([b_ada[2 * D:3 * D], b_ada[5 * D:6 * D]]))
    shared = {
        "vecs": vecs, "bgate": bgate, "w_ada": f(inp["w_ada"])[0], "w_in": f(inp["w_in"])[0],
        "ident": np.eye(128, dtype=np.float32), "rope": _rope_tables(), "cmask": _cmask(),
        "w2a2": np.ascontiguousarray(np.concatenate([f(inp["w2"])[0], f(inp["a2"])[0]], axis=0)),
        "w_out": f(inp["w_out"])[0], "w_ff1": f(inp["w_ff1"])[0], "w_ff2": f(inp["w_ff2"])[0], "gfin": f(inp["normf_g"]),
        "cnt": _cnt_table(), "esel": _esel(), "cnts": _cnts(), "cntn": _cntn(),
        "g2": f(inp["g2"])[0],
    }
    state_shift = f(inp["state_shift"])[0]
    state_wkv = f(inp["state_wkv"])[0]
    cache_k = np.asarray(inp["cache_k"], dtype=np.float32)[0].reshape(128, 2048, 512)
    cache_v = np.asarray(inp["cache_v"], dtype=np.float32)[0].reshape(128, 2048, 512)
    in_maps = []
    for i in range(NCORES):
        m = dict(shared)
        m["xp"] = x_prompt[i]
        m["xs"] = x_sample[DB * i:DB * (i + 1)].reshape(128, D)
        m["c17"] = np.ascontiguousarray(np.concatenate([c_prompt[i:i + 1], c_sample[DB * i:DB * (i + 1)]], axis=0))
        m["sshift"] = state_shift[DB * i:DB * (i + 1)]
        m["swkv"] = state_wkv[DB * i:DB * (i + 1)].reshape(128, E * E)
        m["ck"] = cache_k[DB * i:DB * (i + 1)]
        m["cv"] = cache_v[DB * i:DB * (i + 1)]
        in_maps.append(m)
    if "nc" not in _NC_CACHE:
        _NC_CACHE["nc"] = build_program()
    nc = _NC_CACHE["nc"]
    res = run_bass_kernel_spmd(nc, in_maps, core_ids=list(range(NCORES)))
    R = res.results
    g = lambda name: [np.asarray(R[i][name], dtype=np.float32) for i in range(NCORES)]
    y_prompt = np.stack(g("o_yp"))
    y_sample = np.concatenate(g("o_ys")).reshape(128, T, D) if SAMPLE_ATTN_DONE else np.zeros((128, T, D), np.float32)
    kwin = np.stack(g("o_kwin")).reshape(1, 8, S, H, E)
    vwin = np.stack(g("o_vwin")).reshape(1, 8, S, H, E)
    wkv_p = np.stack(g("o_wkvp")).reshape(1, 8, H, E, E)
    shp = np.stack(g("o_shp")).reshape(1, 8, RIN)
    knew = np.concatenate(g("o_knew")).reshape(1, 128, T, H, E)
    vnew = np.concatenate(g("o_vnew")).reshape(1, 128, T, H, E)
    wkv_s = np.concatenate(g("o_wkvs")).reshape(1, 128, H, E, E)
    shs = np.concatenate(g("o_shs")).reshape(1, 128, RIN)
    return (y_prompt, y_sample, kwin, vwin, wkv_p, shp, knew, vnew, wkv_s, shs)
```

```python
import contextlib
import numpy as np
import concourse.bass as bass
import concourse.mybir as mybir
from concourse.bass_utils import run_bass_kernel_spmd

F32 = mybir.dt.float32
BF16 = mybir.dt.bfloat16
AF = mybir.ActivationFunctionType
ALU = mybir.AluOpType
AX = mybir.AxisListType

NCORES = 8
D = 1024
S = 2048
NT = 16
DB = 16
T = 8
H = 8
E = 64
RIN = 1792
INW = 3328
DFF = 4096
PAST = 8192
ENGS = ("pe", "act", "dve", "pool", "sp")

V_MU, V_W0, V_A0, V_KK, V_KA, V_RK, V_LG, V_LB, V_G1, V_G2, V_BSH1, V_BSC1, V_BSH2, V_BSC2 = (
    0, 14, 18, 22, 26, 30, 34, 38, 42, 50, 58, 66, 74, 82)
NVEC = 90
SAMPLE_ATTN_DONE = True


class Op:
    __slots__ = ("eng", "fn", "deps", "is_dma", "signal", "sem", "target", "name")

    def __init__(s, eng, fn, is_dma, name):
        s.eng = eng; s.fn = fn; s.is_dma = is_dma; s.deps = []
        s.signal = False; s.sem = None; s.target = 0; s.name = name


class Prog:
    def __init__(s, nc, n_dma_sems=24):
        s.nc = nc
        s.ops = []
        s.st = {}
        s.group_of = {}
        s.n_dma_sems = n_dma_sems
        s.final_ops = []
        s.exclusive = {"ps"}

    def _conf(s, g, name, sub):
        ent, idx = g
        if name == "*":
            return list(ent.keys())
        if sub is None:
            keys = [(name, x) for x in idx.get(name, ())]
        else:
            keys = [k for k in ((name, sub), (name, None)) if k in ent]
        if ("*", None) in ent:
            keys.append(("*", None))
        return keys

    def _norm(s, k):
        if not isinstance(k, tuple):
            k = (k, None)
        name, sub = k
        if name.startswith("*@"):
            return name[2:], "*", None
        return s.group_of.get(name, name), name, sub

    def add(s, eng, fn, reads=(), writes=(), dma=False, name="", final=False):
        if getattr(s, "_cap", None) is not None:
            s._cap.append((eng, fn, reads, writes, dma, name, final))
            return None
        op = Op(eng, fn, dma, name)
        reads = [s._norm(k) for k in reads]; writes = [s._norm(k) for k in writes]
        deps = {}
        for gname, name_, sub in reads:
            g = s.st.setdefault(gname, ({}, {}))
            for k in s._conf(g, name_, sub):
                w = g[0][k][0]
                if w is not None:
                    deps[id(w)] = w
                if name_ in s.exclusive:
                    for r in g[0][k][1]:
                        if r.eng != eng:
                            deps[id(r)] = r
        for gname, name_, sub in writes:
            g = s.st.setdefault(gname, ({}, {}))
            for k in s._conf(g, name_, sub):
                w = g[0][k][0]
                if w is not None:
                    deps[id(w)] = w
                for r in g[0][k][1]:
                    deps[id(r)] = r
        for gname, name_, sub in reads:
            ent, idx = s.st[gname]
            rl_ = ent.setdefault((name_, sub), [None, []])[1]
            if not op.is_dma:
                rl_[:] = [r for r in rl_ if r.is_dma or r.eng != op.eng]
            rl_.append(op)
            idx.setdefault(name_, set()).add(sub)
        for gname, name_, sub in writes:
            ent, idx = s.st[gname]
            if name_ == "*":
                ent.clear(); idx.clear()
            elif sub is None:
                for x in idx.get(name_, ()):
                    ent.pop((name_, x), None)
                idx[name_] = set()
            ent[(name_, sub)] = [op, []]
            idx.setdefault(name_, set()).add(sub)
        for w in deps.values():
            if w is op:
                continue
            if (not w.is_dma) and (not op.is_dma) and w.eng == op.eng and op.eng == "pe":
                continue
            op.deps.append(w)
            w.signal = True
        s.ops.append(op)
        if final:
            op.signal = True
            s.final_ops.append(op)
        return op

    def emit(s):
        nc = s.nc
        with contextlib.ExitStack() as es:
            esems = {e: es.enter_context(nc.semaphore("s_" + e)) for e in ("pe", "act", "dve", "pool")}
            dpool = {q: [es.enter_context(nc.semaphore("d%s%d" % (q, i))) for i in range(s.n_dma_sems)]
                     for q in ("sp", "pool")}
            cnt = {e: 0 for e in esems}
            dcum = {q: [0] * s.n_dma_sems for q in dpool}
            dprev = {}
            kq = {q: 0 for q in dpool}
            for op in s.ops:
                if op.is_dma:
                    q = op.eng
                    i = kq[q] % s.n_dma_sems; kq[q] += 1
                    op.sem = dpool[q][i]; dprev[id(op)] = dcum[q][i]
                    dcum[q][i] += 16; op.target = dcum[q][i]
                elif op.signal:
                    cnt[op.eng] += 1
                    op.sem = esems[op.eng]; op.target = cnt[op.eng]
            by_eng = {e: [o for o in s.ops if o.eng == e] for e in ENGS}
            block = es.enter_context(nc.Block())

            def run(engname, eng):
                waited = {}

                def wait(sem, val):
                    if val <= 0:
                        return
                    key = id(sem)
                    if waited.get(key, 0) >= val:
                        return
                    eng.wait_ge(sem, val)
                    waited[key] = val

                for op in by_eng[engname]:
                    for w in op.deps:
                        wait(w.sem, w.target)
                    if op.is_dma:
                        wait(op.sem, dprev[id(op)])
                        op.fn(eng).then_inc(op.sem, 16)
                    else:
                        ins = op.fn(eng)
                        if op.signal:
                            ins.then_inc(op.sem, 1)
                for op in s.final_ops:
                    if op.eng == engname:
                        wait(op.sem, op.target)

            block.tensor(lambda e: run("pe", e))
            block.scalar(lambda e: run("act", e))
            block.vector(lambda e: run("dve", e))
            block.gpsimd(lambda e: run("pool", e))
            block.sync(lambda e: run("sp", e))


def build_program(probe=None):
    nc = bass.Bass("TRN2", target_bir_lowering=False)
    es = contextlib.ExitStack()
    P = Prog(nc)
    A = P.add

    def din(name, shape, dt=F32):
        return nc.dram_tensor(name, list(shape), dt, kind="ExternalInput").ap()

    def dout(name, shape, dt=F32):
        return nc.dram_tensor(name, list(shape), dt, kind="ExternalOutput").ap()

    START = 16512
    G0, R1_0, R2_0, END = START, START + 20480, START + 20480 + 71680, 229344
    ptr = {"G": G0, "R1": R1_0, "R2": R2_0}
    lim = {"G": R1_0, "R1": R2_0, "R2": END}
    cnt_names = [0]

    def sb(name, shape, dt=F32, reg="R2"):
        n = 1
        for d in shape[1:]:
            n *= d
        nbytes = n * (2 if dt == BF16 else 4)
        off = ptr[reg]
        ptr[reg] = off + (nbytes + 31) // 32 * 32
        assert ptr[reg] <= lim[reg], (name, reg, ptr[reg] - lim[reg])
        cnt_names[0] += 1
        if reg != "G":
            P.group_of[name] = "arena"
        return nc.alloc_sbuf_tensor_at("s%d_%s" % (cnt_names[0], name), list(shape), dt, offset=off)

    def new_phase(keep_r2=0):
        A("dve", lambda e: e.memset(bar_t[:], 0.0), [], ["*@arena", "bar_t"])
        ptr["R2"] = R2_0 + keep_r2

    def dma(q, out, in_, reads, writes, final=False):
        return A(q, lambda e: e.dma_start(out=out, in_=in_), reads, writes, dma=True, final=final)

    xp = din("xp", [S, D]); xsm = din("xs", [128, D])
    c17 = din("c17", [17, D]); vecs = din("vecs", [NVEC, 128]); bgate = din("bgate", [2, D])
    w_ada = din("w_ada", [D, 6 * D]); w_in = din("w_in", [D, INW])
    sshift = din("sshift", [DB, RIN])
    ident_d = din("ident", [128, 128]); rope_d = din("rope", [17, 128, 128])
    cmask_d = din("cmask", [128, 896])
    w2a2_d = din("w2a2", [128, 512]); g2_d = din("g2", [128, 512])
    o_kwin = dout("o_kwin", [S, 512]); o_vwin = dout("o_vwin", [S, 512])
    o_shp = dout("o_shp", [1, RIN]); o_knew = dout("o_knew", [128, 512]); o_vnew = dout("o_vnew", [128, 512])
    o_shs = dout("o_shs", [DB, RIN]); o_wkvp = dout("o_wkvp", [H, E, E])
    swkv = din("swkv", [128, E * E]); o_wkvs = dout("o_wkvs", [128, E * E])
    w_out = din("w_out", [D, D]); w_ff1 = din("w_ff1", [D, DFF]); w_ff2 = din("w_ff2", [DFF, D]); gfin = din("gfin", [D])
    cnt_d = din("cnt", [128, 2048]); esel_d = din("esel", [17, 256])
    ck = din("ck", [DB, 2048, 512]); cv = din("cv", [DB, 2048, 512])
    cnts_d = din("cnts", [128, 16 * T]); cntn_d = din("cntn", [128, DB * T])
    o_yp = dout("o_yp", [S, D]); o_ys = dout("o_ys", [128, D])
    scr_v = nc.dram_tensor("scr_v", [6, 128, 512], F32, kind="Internal").ap()
    scr_o = nc.dram_tensor("scr_o", [128, 512], F32, kind="Internal").ap()
    scr_acc = nc.dram_tensor("scr_acc", [65, 1024], F32, kind="Internal").ap()

    ps = [es.enter_context(nc.psum_tensor("ps%d" % i, [128, 512], F32)) for i in range(8)]

    bar_t = sb("bar_t", [128, 8], F32, "G")
    ident = sb("ident", [128, 128], F32, "G"); identb = sb("identb", [128, 128], BF16, "G")
    vecT = sb("vecT", [128, NVEC], F32, "G"); scT = sb("scT", [128, 8, 17], BF16, "G")
    modT = {n: sb("modT_" + n, [128, 8, 17], F32, "G") for n in ("A1", "B1", "A2", "B2")}
    cmask = sb("cmask", [128, 896], F32, "G"); ropet = sb("ropet", [128, 128], F32, "G")
    ss = sb("ss", [128, 1], F32, "G"); rstd = sb("rstd", [128, 1], F32, "G")
    xt = sb("xt", [128, D], F32, "G"); xsn = sb("xsn", [128, D], F32, "G"); hT = sb("hT", [128, 8, 128], BF16, "G")
    sshT = sb("sshT", [128, 14, DB], F32, "G")
    Esel = sb("Esel", [17, 256], F32, "G")
    blk = cmask[:, 0:128]; MT4 = cmask[:, 128:640]; MLs = cmask[:, 640:768]; ones_t = cmask[:, 768:896]
    QT = sb("QT", [128, 4, (NT + 1) * 128], BF16, "R1"); KT = sb("KT", [128, 4, (NT + 1) * 128], BF16, "R1")
    Vaug = sb("Vaug", [128, NT + 1, H, 65], BF16, "R1"); rwT = sb("rwT", [128, NT + 1, 4, 128], BF16, "R1")

    def vcol(c0, n=4, w=128):
        return vecT[:, c0:c0 + n].unsqueeze(2).broadcast_to([128, n, w])

    dma("sp", ident[:], ident_d, [], ["ident"])
    dma("sp", cmask[:], cmask_d, [], ["cmask"])
    dma("sp", Esel[:], esel_d, [], ["Esel"])
    A("dve", lambda e: e.tensor_copy(out=identb[:], in_=ident[:]), ["ident"], ["identb"])
    dma("sp", xt[0:NVEC, 0:128], vecs, [], ["xt"])
    A("pe", lambda e: e.transpose(out=ps[0][:, 0:NVEC], in_=xt[0:NVEC, 0:128], identity=ident[0:NVEC, 0:NVEC]),
      ["xt", "ident"], [("ps", 0)])
    A("dve", lambda e: e.tensor_copy(out=vecT[:], in_=ps[0][:, 0:NVEC]), [("ps", 0)], ["vecT"])
    c_sb = xsn[0:17, :]
    dma("sp", c_sb, c17, [], ["xsn"])
    A("act", lambda e: e.activation(out=c_sb, in_=c_sb, func=AF.Silu), ["xsn"], ["xsn"])
    for kc in range(8):
        A("pe", lambda e, kc=kc: e.transpose(out=ps[1][:, kc * 17:(kc + 1) * 17], in_=xsn[0:17, kc * 128:(kc + 1) * 128],
                                             identity=ident[0:17, 0:17]), ["xsn", "ident"], [("ps", 1)])
    A("dve", lambda e: e.tensor_copy(out=scT[:], in_=ps[1][:, 0:136].rearrange("p (k b) -> p k b", k=8)), [("ps", 1)], ["scT"])
    A("dve", lambda e: e.memset(Vaug[:, :, :, 64:65], 1.0), [], ["Vaug"])

    wada = [sb("wada%d" % i, [128, 8, D], BF16) for i in range(2)]
    ssh_sb = sb("ssh_sb", [DB, RIN])

    def ada_block(col0, buf):
        for half in range(2):
            A("pool", lambda e, half=half: e.dma_start(
                out=wada[buf][:, 4 * half:4 * half + 4, :],
                in_=w_ada[512 * half:512 * half + 512, col0:col0 + D].rearrange("(kc p) n -> p kc n", p=128)),
              [], ["wada%d" % buf], dma=True)

    def ada_fm(buf, bank, dst, bias_col, gain_col):
        wt = wada[buf]
        for c in range(8):
            for kc in range(8):
                A("pe", lambda e, c=c, kc=kc: e.matmul(ps[bank][:, c * 17:(c + 1) * 17], lhsT=wt[:, kc, c * 128:(c + 1) * 128],
                                                       rhs=scT[:, kc, :], start=(kc == 0), stop=(kc == 7)),
                  ["wada%d" % buf, "scT"], [("ps", bank)])
        pv = ps[bank][:, 0:136].rearrange("p (c b) -> p c b", c=8)
        dk = "modT_" + dst
        A("dve", lambda e: e.tensor_tensor(out=modT[dst][:], in0=pv, in1=vcol(bias_col, 8, 17), op=ALU.add), [("ps", bank), "vecT"], [dk])
        if gain_col is not None:
            A("dve", lambda e: e.scalar_tensor_tensor(out=modT[dst][:], in0=modT[dst][:], scalar=1.0, in1=vcol(gain_col, 8, 17),
                                                      op0=ALU.add, op1=ALU.mult), [dk, "vecT"], [dk])

    ada_block(1 * D, 0); ada_block(0 * D, 1)
    ada_fm(0, 2, "A1", V_BSC1, V_G1); ada_fm(1, 3, "B1", V_BSH1, None)
    ada_block(4 * D, 0); ada_block(3 * D, 1)
    ada_fm(0, 2, "A2", V_BSC2, V_G2); ada_fm(1, 3, "B2", V_BSH2, None)
    dma("sp", ssh_sb[:], sshift, [], ["ssh_sb"])
    for c in range(14):
        A("pe", lambda e, c=c: e.transpose(out=ps[4][:, c * 16:(c + 1) * 16], in_=ssh_sb[:, c * 128:(c + 1) * 128],
                                           identity=ident[0:16, 0:16]), ["ssh_sb", "ident"], [("ps", 4)])
    A("dve", lambda e: e.tensor_copy(out=sshT[:], in_=ps[4][:, 0:224].rearrange("p (c b) -> p c b", c=14)), [("ps", 4)], ["sshT"])

    BS0 = dict(xt=xt, xsn=xsn, hT=hT, ss=ss, rstd=rstd, sfx="")

    def rms_hT(ti, modA, modB, bs=None):
        bs = bs or BS0
        sample = (ti == NT)
        dma("sp", bs["xt"][:], xsm if sample else xp[ti * 128:(ti + 1) * 128, :], [], ["xt" + bs["sfx"]])
        rms_from(bs["xt"][:], "xt" + bs["sfx"], modA, modB, sample, bs)

    def rms_from(x_ap, xkey, modA, modB, sample, bs=None):
        bs = bs or BS0
        return _rms_from(x_ap, xkey, modA, modB, sample, bs["xsn"], bs["hT"], bs["ss"], bs["rstd"], bs["sfx"])

    def _rms_from(x_ap, xkey, modA, modB, sample, xsn, hT, ss, rstd, sfx):
        A("act", lambda e: e.activation(out=xsn[:], in_=x_ap, func=AF.Square, accum_out=ss[:]), [xkey], ["xsn" + sfx, "ss" + sfx])
        A("dve", lambda e: e.tensor_scalar(out=rstd[:], in0=ss[:], scalar1=1.0 / D, scalar2=1e-6, op0=ALU.mult, op1=ALU.add),
          ["ss" + sfx], ["rstd" + sfx])
        A("act", lambda e: e.activation(out=rstd[:], in_=rstd[:], func=AF.Sqrt), ["rstd" + sfx], ["rstd" + sfx])
        A("dve", lambda e: e.reciprocal(out=rstd[:], in_=rstd[:]), ["rstd" + sfx], ["rstd" + sfx])
        A("act", lambda e: e.activation(out=xsn[:], in_=x_ap, func=AF.Copy, scale=rstd[:]), [xkey, "rstd" + sfx], ["xsn" + sfx])
        for c in range(8):
            A("pe", lambda e, c=c: e.transpose(out=ps[c // 4][:, (c % 4) * 128:(c % 4 + 1) * 128], in_=xsn[:, c * 128:(c + 1) * 128],
                                               identity=ident[:]), ["xsn" + sfx, "ident"], [("ps", c // 4)])
        mA, mB = modT[modA], modT[modB]
        for g in range(2):
            if sample:
                pv = ps[g][:].rearrange("p (c b t) -> p c b t", c=4, b=DB)
                o = hT[:, 4 * g:4 * g + 4, :].rearrange("p c (b t) -> p c b t", b=DB)
                a_ = mA[:, 4 * g:4 * g + 4, 1:17].unsqueeze(3).broadcast_to([128, 4, DB, T])
                b_ = mB[:, 4 * g:4 * g + 4, 1:17].unsqueeze(3).broadcast_to([128, 4, DB, T])
                tmp = xsn[:, 512 * g:512 * g + 512].rearrange("p (c b t) -> p c b t", c=4, b=DB)
            else:
                pv = ps[g][:].rearrange("p (c t) -> p c t", c=4)
                o = hT[:, 4 * g:4 * g + 4, :]
                a_ = mA[:, 4 * g:4 * g + 4, 0:1].broadcast_to([128, 4, 128])
                b_ = mB[:, 4 * g:4 * g + 4, 0:1].broadcast_to([128, 4, 128])
                tmp = xsn[:, 512 * g:512 * g + 512].rearrange("p (c t) -> p c t", c=4)
            A("dve", lambda e, pv=pv, a_=a_, tmp=tmp: e.tensor_tensor(out=tmp, in0=pv, in1=a_, op=ALU.mult),
              [("ps", g), "modT_" + modA, "xsn" + sfx], ["xsn" + sfx])
            A("dve", lambda e, o=o, b_=b_, tmp=tmp: e.tensor_tensor(out=o, in0=tmp, in1=b_, op=ALU.add),
              ["xsn" + sfx, "modT_" + modB], [("hT" + sfx, g)])

    new_phase()
    winq = sb("winq", [128, 8, 1536], BF16)
    for kc in range(8):
        for c0 in (0, 768):
            A("pool", lambda e, kc=kc, c0=c0: e.dma_start(out=winq[:, kc, c0:c0 + 768], in_=w_in[kc * 128:(kc + 1) * 128, c0:c0 + 768]),
              [], [("winq", kc)], dma=True)
    xtB = sb("xt1", [128, D]); xsnB = sb("xsn1", [128, D]); hTB = sb("hT1", [128, 8, 128], BF16)
    ssB = sb("ss1", [128, 1]); rstdB = sb("rstd1", [128, 1])
    BS = [BS0, dict(xt=xtB, xsn=xsnB, hT=hTB, ss=ssB, rstd=rstdB, sfx="1")]
    QKV = [tuple(sb("%s%d" % (n, i), [128, 512]) for n in ("qs", "ks", "vs")) for i in range(2)]
    ropeB = [ropet, sb("ropet1", [128, 128])]
    rtmp = [sb("rtmp%d" % i, [128, H, 8]) for i in range(2)]

    def rope(bank, dst, dkey, rt, rtk):
        psb = ps[bank]
        A("act", lambda e: e.activation(out=dst[:], in_=psb[:], func=AF.Copy), [("ps", bank)], [dkey])
        pv = psb[:].rearrange("p (h e) -> p h e", h=H)
        dv = dst[:].rearrange("p (h e) -> p h e", h=H)
        cos = rt[:, 0:64].rearrange("p (h e) -> p h e", h=H)
        sin = rt[:, 64:128].rearrange("p (h e) -> p h e", h=H)
        t1 = pv[:, :, 0:8]; t2 = pv[:, :, 8:16]
        rk = [("ps", bank), rtk]
        A("dve", lambda e: e.tensor_tensor(out=rtmp[0][:], in0=t1, in1=cos, op=ALU.mult), rk, ["rtmp0"])
        A("dve", lambda e: e.tensor_tensor(out=rtmp[1][:], in0=t2, in1=sin, op=ALU.mult), rk, ["rtmp1"])
        A("dve", lambda e: e.tensor_tensor(out=dv[:, :, 0:8], in0=rtmp[0][:], in1=rtmp[1][:], op=ALU.subtract),
          ["rtmp0", "rtmp1", dkey], [dkey])
        A("dve", lambda e: e.tensor_tensor(out=rtmp[0][:], in0=t1, in1=sin, op=ALU.mult), rk, ["rtmp0"])
        A("dve", lambda e: e.tensor_tensor(out=rtmp[1][:], in0=t2, in1=cos, op=ALU.mult), rk, ["rtmp1"])
        A("dve", lambda e: e.tensor_tensor(out=dv[:, :, 8:16], in0=rtmp[0][:], in1=rtmp[1][:], op=ALU.add),
          ["rtmp0", "rtmp1", dkey], [dkey])

    def p1a_front(n_, ti):
        bs = BS[n_ % 2]
        dma("sp", ropeB[n_ % 2][:], rope_d[ti], [], ["ropet%d" % (n_ % 2)])
        rms_hT(ti, "A1", "B1", bs)

    def p1a_mm(n_, ti):
        bs = BS[n_ % 2]
        hT_ = bs["hT"]
        for g in range(3):
            for kc in range(8):
                A("pe", lambda e, g=g, kc=kc, hT_=hT_: e.matmul(ps[2 + g][:], lhsT=hT_[:, kc, :], rhs=winq[:, kc, 512 * g:512 * g + 512],
                                                                start=(kc == 0), stop=(kc == 7)), ["hT" + bs["sfx"], ("winq", kc)], [("ps", 2 + g)])

    def p1a_back(n_, ti):
        sample = (ti == NT)
        qs, ks, vs = QKV[n_ % 2]
        qk, kk_, vk = ("qs%d" % (n_ % 2), "ks%d" % (n_ % 2), "vs%d" % (n_ % 2))
        rope(2, qs, qk, ropeB[n_ % 2], "ropet%d" % (n_ % 2)); rope(3, ks, kk_, ropeB[n_ % 2], "ropet%d" % (n_ % 2))
        A("act", lambda e: e.activation(out=vs[:], in_=ps[4][:], func=AF.Copy), [("ps", 4)], [vk])
        if sample:
            dma("sp", o_knew, ks[:], [kk_], ["o_knew"], final=True)
            dma("sp", o_vnew, vs[:], [vk], ["o_vnew"], final=True)
        else:
            dma("sp", o_kwin[ti * 128:(ti + 1) * 128, :], ks[:], [kk_], [("o_kwin", ti)], final=True)
            dma("sp", o_vwin[ti * 128:(ti + 1) * 128, :], vs[:], [vk], [("o_vwin", ti)], final=True)
        for (src, skey, dst, dkey, bank, scl) in ((qs, qk, QT, "QT", 5, 0.125), (ks, kk_, KT, "KT", 6, 1.0)):
            for c in range(4):
                A("pe", lambda e, src=src, c=c, bank=bank: e.transpose(out=ps[bank][:, 128 * c:128 * c + 128], in_=src[:, 128 * c:128 * c + 128],
                                                                       identity=ident[:]), [skey, "ident"], [("ps", bank)])
            A("act", lambda e, dst=dst, bank=bank, scl=scl: e.activation(
                out=dst[:, :, ti * 128:(ti + 1) * 128], in_=ps[bank][:].rearrange("p (c t) -> p c t", c=4), func=AF.Copy, scale=scl),
              [("ps", bank)], [(dkey, ti)])
        A("act", lambda e: e.activation(out=Vaug[:, ti, :, 0:64], in_=vs[:].rearrange("p (h e) -> p h e", h=H), func=AF.Copy),
          [vk], [("Vaug", ti)])

    TL1A = [0, NT] if probe == "quick" else list(range(NT + 1))
    p1a_front(0, TL1A[0])
    for n_, ti in enumerate(TL1A):
        p1a_mm(n_, ti)
        if n_ + 1 < len(TL1A):
            p1a_front(n_ + 1, TL1A[n_ + 1])
        p1a_back(n_, ti)

    new_phase()
    Pm = sb("Pm", [128, 14, 128])

    def t4(name, dt=F32):
        return sb(name, [128, 4, 128], dt)

    gT = t4("gT"); bsum = t4("bsum"); onT = t4("onT")
    O32 = sb("O32", [128, H, 64]); Osq = sb("Osq", [128, H, 64]); gst = sb("gst", [128, 4, H])
    KEEP_1C = ptr["R2"] - R2_0
    winp = sb("winp", [128, 8, RIN], BF16)
    for kc in range(8):
        for c0 in (0, 896):
            A("pool", lambda e, kc=kc, c0=c0: e.dma_start(out=winp[:, kc, c0:c0 + 896], in_=w_in[kc * 128:(kc + 1) * 128, 1536 + c0:1536 + c0 + 896]),
              [], [("winp", kc)], dma=True)
    w2a2 = sb("w2a2", [128, 512], BF16); g2w = sb("g2w", [128, 512], BF16)
    A("pool", lambda e: e.dma_start(out=w2a2[:], in_=w2a2_d), [], ["w2a2"], dma=True)
    A("pool", lambda e: e.dma_start(out=g2w[:], in_=g2_d), [], ["g2w"], dma=True)
    PTx = sb("PTx", [128, 14, 129]); dtmp = sb("dtmp", [128, 14, 128])
    Ptok = dtmp[:].rearrange("p c t -> p (c t)")
    la = sb("la", [128, 128], BF16); sgl = sb("sgl", [128, 128], BF16)
    ld = t4("ld"); alpha = t4("alpha"); Lc = t4("Lc"); kxk = t4("kxk"); nrm = t4("nrm"); kk = t4("kk"); kmod = t4("kmod")
    bb = t4("bb"); tmp4 = t4("tmp4"); Wt = t4("Wt"); Winv = t4("Winv"); Wend = t4("Wend")
    Bh, Kh = kxk, nrm
    AR = sb("AR", [128, 4, 2, 128], BF16); Bt = t4("Bt", BF16); Kt = t4("Kt", BF16)
    Vtok = sb("Vtok", [128, 512], BF16); BhTok = sb("BhTok", [128, 512], BF16); KhTok = sb("KhTok", [128, 512], BF16)
    AX4 = sb("AX4", [128, 8, 4, 128], BF16)
    Lk = [sb("Lk%d" % i, [128, 4, 2, 128], BF16) for i in range(2)]
    MT32 = sb("MT32", [128, 4, 128]); MTb = sb("MTb", [128, 4, 128], BF16); NTb = sb("NTb", [128, 8, 128], BF16)
    St32 = sb("St32", [128, 4, 64]); Stb = sb("Stb", [128, 4, 64], BF16); WC = sb("WC", [128, 4])
    RHS32 = sb("RHS32", [128, 512]); RHSb = sb("RHSb", [128, 512], BF16); Ub = sb("Ub", [128, 512], BF16)
    A("dve", lambda e: e.memset(PTx[:, :, 0:1], 0.0), [], [("PTx", "carry")])
    A("dve", lambda e: e.memset(St32[:], 0.0), [], ["St32"])
    A("dve", lambda e: e.memset(Stb[:], 0.0), [], ["Stb"])

    def ptok_from_PTx():
        for c in range(14):
            A("pe", lambda e, c=c: e.transpose(out=ps[1][:, (c % 4) * 128:(c % 4 + 1) * 128], in_=PTx[:, c, 1:129], identity=ident[:]),
              [("PTx", "cur"), "ident"], [("ps", 1)])
            if c % 4 == 3 or c == 13:
                c0 = (c // 4) * 4
                n = c - c0 + 1
                A("act", lambda e, c0=c0, n=n: e.activation(out=Ptok[:, c0 * 128:(c0 + n) * 128], in_=ps[1][:, 0:128 * n], func=AF.Copy),
                  [("ps", 1)], ["dtmp"])

    def phase1b_proj(ti):
        sample = (ti == NT)
        rms_hT(ti, "A1", "B1")
        for g in range(4):
            bank = (2, 0, 1, 2)[g]
            n = 4 if g < 3 else 2
            for c in range(4 * g, 4 * g + n):
                for kc in range(8):
                    A("pe", lambda e, c=c, kc=kc, bank=bank: e.matmul(
                        ps[bank][:, (c % 4) * 128:(c % 4 + 1) * 128], lhsT=winp[:, kc, c * 128:(c + 1) * 128],
                        rhs=hT[:, kc, :], start=(kc == 0), stop=(kc == 7)), ["hT", ("winp", kc)], [("ps", bank)])
            A("act", lambda e, g=g, bank=bank, n=n: e.activation(
                out=PTx[:, 4 * g:4 * g + n, 1:129], in_=ps[bank][:, 0:128 * n].rearrange("p (c t) -> p c t", c=n), func=AF.Copy),
              [("ps", bank)], [("PTx", "cur")])
        if sample or ti == NT - 1:
            ptok_from_PTx()
            if sample:
                dma("sp", o_shs, Ptok[T - 1:128:T, :], ["dtmp"], ["o_shs"], final=True)
            else:
                dma("sp", o_shp, Ptok[127:128, :], ["dtmp"], ["o_shp"], final=True)

    def rwkv_prep(ti):
        sample = (ti == NT)
        Pcur = PTx[:, :, 1:129]
        if sample:
            d4 = dtmp[:].rearrange("p c (b t) -> p c b t", b=DB); c4 = Pcur.rearrange("p c (b t) -> p c b t", b=DB)
            A("dve", lambda e: e.tensor_tensor(out=d4[:, :, :, 1:T], in0=c4[:, :, :, 0:T - 1], in1=c4[:, :, :, 1:T], op=ALU.subtract),
              [("PTx", "cur")], ["dtmp"])
            A("dve", lambda e: e.tensor_tensor(out=d4[:, :, :, 0:1], in0=sshT[:].unsqueeze(3), in1=c4[:, :, :, 0:1], op=ALU.subtract),
              [("PTx", "cur"), "sshT", "dtmp"], ["dtmp"])
        else:
            A("dve", lambda e: e.tensor_tensor(out=dtmp[:], in0=PTx[:, :, 0:128], in1=Pcur, op=ALU.subtract),
              [("PTx", "cur"), ("PTx", "carry")], ["dtmp"])
        A("dve", lambda e: e.tensor_tensor(out=dtmp[:], in0=dtmp[:], in1=vcol(V_MU, 14), op=ALU.mult), ["dtmp", "vecT"], ["dtmp"])
        A("dve", lambda e: e.tensor_tensor(out=Pm[:], in0=dtmp[:], in1=Pcur, op=ALU.add), ["dtmp", ("PTx", "cur")], ["Pm"])
        if ti < NT - 1:
            A("dve", lambda e: e.tensor_copy(out=PTx[:, :, 0:1], in_=PTx[:, :, 128:129]), [("PTx", "cur")], [("PTx", "carry")])
        A("act", lambda e: e.activation(out=la[0:64, :], in_=Pm[0:64, 12, :], func=AF.Tanh), ["Pm"], [("la", 0)])
        A("act", lambda e: e.activation(out=la[64:128, :], in_=Pm[64:128, 12, :], func=AF.Copy), ["Pm"], [("la", 1)])
        A("act", lambda e: e.activation(out=sgl[:], in_=Pm[:, 13, :], func=AF.Sigmoid), ["Pm"], ["sgl"])
        for c in range(4):
            cs = slice(128 * c, 128 * c + 128)
            A("pe", lambda e, cs=cs: e.matmul(ps[0][:, cs], lhsT=w2a2[0:64, cs], rhs=la[0:64, :], start=True, stop=True),
              ["w2a2", ("la", 0)], [("ps", 0)])
            A("pe", lambda e, cs=cs: e.matmul(ps[1][:, cs], lhsT=w2a2[64:128, cs], rhs=la[64:128, :], start=True, stop=True),
              ["w2a2", ("la", 1)], [("ps", 1)])
            A("pe", lambda e, cs=cs: e.matmul(ps[2][:, cs], lhsT=g2w[:, cs], rhs=sgl[:], start=True, stop=True), ["g2w", "sgl"], [("ps", 2)])
        for c in range(4):
            cs = slice(128 * c, 128 * c + 128)
            A("act", lambda e, c=c, cs=cs: e.activation(out=ld[:, c, :], in_=ps[0][:, cs], func=AF.Sigmoid, bias=vecT[:, V_W0 + c:V_W0 + c + 1]),
              [("ps", 0), "vecT"], [("ld", c)])
            A("act", lambda e, c=c, cs=cs: e.activation(out=alpha[:, c, :], in_=ps[1][:, cs], func=AF.Sigmoid, bias=vecT[:, V_A0 + c:V_A0 + c + 1]),
              [("ps", 1), "vecT"], [("alpha", c)])
        A("act", lambda e: e.activation(out=gT[:], in_=ps[2][:].rearrange("p (c t) -> p c t", c=4), func=AF.Copy), [("ps", 2)], ["gT"])
        A("dve", lambda e: e.tensor_scalar(out=ld[:], in0=ld[:], scalar1=-0.6065306597126334, scalar2=None, op0=ALU.mult), ["ld"], ["ld"])
        kc_ = Pm[:, 4:8, :]
        A("dve", lambda e: e.tensor_tensor(out=kxk[:], in0=kc_, in1=vcol(V_KK), op=ALU.mult), ["Pm", "vecT"], ["kxk"])
        A("act", lambda e: e.activation(out=nrm[:], in_=kxk[:], func=AF.Square), ["kxk"], ["nrm"])
        A("pe", lambda e: e.matmul(ps[3][:], lhsT=blk, rhs=nrm[:].rearrange("p c t -> p (c t)"), start=True, stop=True),
          ["cmask", "nrm"], [("ps", 3)])
        A("act", lambda e: e.activation(out=nrm[:], in_=ps[3][:].rearrange("p (c t) -> p c t", c=4), func=AF.Sqrt), [("ps", 3)], ["nrm"])
        A("dve", lambda e: e.tensor_scalar(out=nrm[:], in0=nrm[:], scalar1=1e-12, scalar2=None, op0=ALU.max), ["nrm"], ["nrm"])
        A("dve", lambda e: e.reciprocal(out=nrm[:], in_=nrm[:]), ["nrm"], ["nrm"])
        A("dve", lambda e: e.tensor_tensor(out=kk[:], in0=kxk[:], in1=nrm[:], op=ALU.mult), ["kxk", "nrm"], ["kk"])
        A("dve", lambda e: e.scalar_tensor_tensor(out=tmp4[:], in0=alpha[:], scalar=-1.0, in1=vcol(V_KA), op0=ALU.add, op1=ALU.mult),
          ["alpha", "vecT"], ["tmp4"])
        A("dve", lambda e: e.scalar_tensor_tensor(out=kmod[:], in0=tmp4[:], scalar=1.0, in1=kc_, op0=ALU.add, op1=ALU.mult),
          ["tmp4", "Pm"], ["kmod"])
        A("dve", lambda e: e.tensor_tensor(out=bb[:], in0=kk[:], in1=alpha[:], op=ALU.mult), ["kk", "alpha"], ["bb"])
        A("dve", lambda e: e.tensor_tensor(out=tmp4[:], in0=Pm[:, 0:4, :], in1=kmod[:], op=ALU.mult), ["Pm", "kmod"], ["tmp4"])
        A("dve", lambda e: e.tensor_tensor(out=tmp4[:], in0=tmp4[:], in1=vcol(V_RK), op=ALU.mult), ["tmp4", "vecT"], ["tmp4"])
        A("pe", lambda e: e.matmul(ps[3][:], lhsT=blk, rhs=tmp4[:].rearrange("p c t -> p (c t)"), start=True, stop=True),
          ["cmask", "tmp4"], [("ps", 3)])
        A("act", lambda e: e.activation(out=bsum[:], in_=ps[3][:].rearrange("p (c t) -> p c t", c=4), func=AF.Copy), [("ps", 3)], ["bsum"])

    def rwkv_chunk(ti):
        for c in range(4):
            A("dve", lambda e, c=c: e.tensor_tensor_scan(out=Lc[:, c, :], data0=ones_t, data1=ld[:, c, :], initial=0.0,
                                                         op0=ALU.mult, op1=ALU.add), ["ld", "cmask"], [("Lc", c)])
        A("act", lambda e: e.activation(out=Wt[:], in_=Lc[:], func=AF.Exp), ["Lc"], ["Wt"])
        A("act", lambda e: e.activation(out=Winv[:], in_=Lc[:], func=AF.Exp, scale=-1.0), ["Lc"], ["Winv"])
        A("dve", lambda e: e.tensor_tensor(out=tmp4[:], in0=Lc[:], in1=ld[:], op=ALU.subtract), ["Lc", "ld"], ["tmp4"])
        A("act", lambda e: e.activation(out=tmp4[:], in_=tmp4[:], func=AF.Exp), ["tmp4"], ["tmp4"])
        for c in range(4):
            A("act", lambda e, c=c: e.activation(out=Wend[:, c, :], in_=Lc[:, c, :], func=AF.Exp, scale=-1.0, bias=Lc[:, c, 127:128]),
              ["Lc"], [("Wend", c)])
        A("act", lambda e: e.activation(out=WC[:].unsqueeze(2), in_=Lc[:, :, 127:128], func=AF.Exp), ["Lc"], ["WC"])
        A("dve", lambda e: e.tensor_tensor(out=St32[:], in0=St32[:], in1=WC[:].unsqueeze(2).broadcast_to([128, 4, 64]), op=ALU.mult),
          ["St32", "WC"], ["St32"])
        A("dve", lambda e: e.scalar_tensor_tensor(out=AR[:, :, 0, :], in0=kk[:], scalar=-1.0, in1=tmp4[:], op0=ALU.mult, op1=ALU.mult),
          ["kk", "tmp4"], [("AR", 0)])
        A("dve", lambda e: e.tensor_tensor(out=AR[:, :, 1, :], in0=Pm[:, 0:4, :], in1=Wt[:], op=ALU.mult), ["Pm", "Wt"], [("AR", 1)])
        A("dve", lambda e: e.tensor_tensor(out=Bt[:], in0=bb[:], in1=Winv[:], op=ALU.mult), ["bb", "Winv"], ["Bt"])
        A("dve", lambda e: e.tensor_tensor(out=Kt[:], in0=kmod[:], in1=Winv[:], op=ALU.mult), ["kmod", "Winv"], ["Kt"])
        A("dve", lambda e: e.tensor_tensor(out=Bh[:], in0=bb[:], in1=Wend[:], op=ALU.mult), ["bb", "Wend"], ["kxk"])
        A("dve", lambda e: e.tensor_tensor(out=Kh[:], in0=kmod[:], in1=Wend[:], op=ALU.mult), ["kmod", "Wend"], ["nrm"])
        for (src, skey, dst, dkey) in ((Pm[:, 8:12, :], "Pm", Vtok, "Vtok"), (Bh[:], "kxk", BhTok, "BhTok"), (Kh[:], "nrm", KhTok, "KhTok")):
            for c in range(4):
                A("pe", lambda e, src=src, c=c: e.transpose(out=ps[3][:, 128 * c:128 * c + 128], in_=src[:, c, :], identity=ident[:]),
                  [skey, "ident"], [("ps", 3)])
            A("act", lambda e, dst=dst: e.activation(out=dst[:], in_=ps[3][:], func=AF.Copy), [("ps", 3)], [dkey])
        identb4 = ident[:].unsqueeze(1).broadcast_to([128, 4, 128])
        for g in range(2):
            for hl in range(4):
                h = 4 * g + hl
                hp, hh = h // 2, h % 2
                pr = slice(64 * hh, 64 * hh + 64)
                arv = AR[pr, hp, :, :].rearrange("p a t -> p (a t)")
                ab = 4 if hh == 0 else 6
                lb = 5 if hh == 0 else 7
                lc = slice(128 * (hl // 2), 128 * (hl // 2) + 128)
                A("pe", lambda e, pr=pr, hp=hp, arv=arv, ab=ab: e.matmul(ps[ab][:, 0:256], lhsT=Bt[pr, hp, :], rhs=arv, start=True, stop=True),
                  ["Bt", "AR"], [("ps", ab)])
                A("pe", lambda e, pr=pr, hp=hp, arv=arv, ab=ab: e.matmul(ps[ab][:, 256:512], lhsT=Kt[pr, hp, :], rhs=arv, start=True, stop=True),
                  ["Kt", "AR"], [("ps", ab)])
                A("pe", lambda e, pr=pr, hp=hp, lb=lb, lc=lc: e.matmul(ps[lb][:, lc], lhsT=AR[pr, hp, 0, :], rhs=Bt[pr, hp, :], start=True, stop=True),
                  ["Bt", "AR"], [("ps", lb)])
                A("dve", lambda e, h=h, ab=ab: e.tensor_tensor(out=AX4[:, h, :, :].rearrange("p a t -> p (a t)"), in0=ps[ab][:], in1=MT4, op=ALU.mult),
                  [("ps", ab), "cmask"], [("AX4", h)])
            for hh in range(2):
                lb = 5 if hh == 0 else 7
                A("dve", lambda e, hh=hh, lb=lb: e.tensor_tensor(out=Lk[0][:, hh::2, 0, :], in0=ps[lb][:, 0:256].rearrange("p (a t) -> p a t", a=2),
                                                                 in1=MLs.unsqueeze(1).broadcast_to([128, 2, 128]), op=ALU.mult),
                  [("ps", lb), "cmask"], [("Lk0", ("L", hh))])
            axg = AX4[:, 4 * g:4 * g + 4, 0, :]
            axk = [("AX4", 4 * g + i) for i in range(4)]
            A("act", lambda e, axg=axg: e.activation(out=Lk[0][:, :, 1, :], in_=axg, func=AF.Copy), axk, [("Lk0", "T")])
            A("dve", lambda e, axg=axg: e.tensor_tensor(out=MTb[:], in0=axg, in1=identb4, op=ALU.add), axk + ["ident"], ["MTb"])
            for lv in range(6):
                a_, b_ = Lk[lv % 2], Lk[(lv + 1) % 2]
                ak, bk = "Lk%d" % (lv % 2), "Lk%d" % ((lv + 1) % 2)
                for hl in range(4):
                    bank = 4 if hl < 2 else 6
                    c0 = 256 * (hl % 2)
                    A("pe", lambda e, a_=a_, hl=hl, bank=bank, c0=c0: e.matmul(ps[bank][:, c0:c0 + 128], lhsT=a_[:, hl, 1, :], rhs=a_[:, hl, 0, :],
                                                                               start=True, stop=True), [ak], [("ps", bank)])
                    A("pe", lambda e, a_=a_, hl=hl, bank=bank, c0=c0: e.matmul(ps[bank][:, c0 + 128:c0 + 256], lhsT=a_[:, hl, 0, :], rhs=a_[:, hl, 1, :],
                                                                               start=True, stop=True), [ak], [("ps", bank)])
                A("act", lambda e, b_=b_: e.activation(out=b_[:, 0:2, :, :].rearrange("p h a t -> p (h a t)"), in_=ps[4][:], func=AF.Copy),
                  [("ps", 4)], [(bk, 0)])
                A("dve", lambda e, b_=b_: e.tensor_copy(out=b_[:, 2:4, :, :].rearrange("p h a t -> p (h a t)"), in_=ps[6][:]),
                  [("ps", 6)], [(bk, 1)])
                for hl in range(4):
                    A("pe", lambda e, b_=b_, hl=hl: e.matmul(ps[5][:, 128 * hl:128 * hl + 128], lhsT=b_[:, hl, 0, :], rhs=MTb[:, hl, :],
                                                             start=True, stop=True), [(bk, hl // 2), "MTb"], [("ps", 5)])
                A("dve", lambda e: e.tensor_tensor(out=MTb[:], in0=ps[5][:].rearrange("p (h t) -> p h t", h=4), in1=MTb[:], op=ALU.add),
                  [("ps", 5), "MTb"], ["MTb"])
            A("dve", lambda e, g=g: e.tensor_tensor(out=NTb[:, 4 * g:4 * g + 4, :], in0=MTb[:], in1=identb4, op=ALU.subtract),
              ["MTb", "ident"], [("NTb", g)])
        for h in range(H):
            hp, hh = h // 2, h % 2
            pr = slice(64 * hh, 64 * hh + 64); hs = slice(64 * h, 64 * h + 64)
            A("pe", lambda e, pr=pr, hp=hp, hs=hs: e.matmul(ps[7][:, hs], lhsT=AR[pr, hp, 0, :], rhs=Stb[pr, hp, :], start=True, stop=False),
              ["AR", "Stb"], [("ps", 7)])
            A("pe", lambda e, h=h, hs=hs: e.matmul(ps[7][:, hs], lhsT=AX4[:, h, 2, :], rhs=Vtok[:, hs], start=False, stop=True),
              [("AX4", h), "Vtok"], [("ps", 7)])
        A("act", lambda e: e.activation(out=RHSb[:], in_=ps[7][:], func=AF.Copy), [("ps", 7)], ["RHSb"])
        A("dve", lambda e: e.tensor_copy(out=RHS32[:], in_=ps[7][:]), [("ps", 7)], ["RHS32"])
        for h in range(H):
            hs = slice(64 * h, 64 * h + 64)
            A("pe", lambda e, h=h, hs=hs: e.matmul(ps[6][:, hs], lhsT=NTb[:, h, :], rhs=RHSb[:, hs], start=True, stop=True),
              [("NTb", h // 4), "RHSb"], [("ps", 6)])
        A("dve", lambda e: e.tensor_tensor(out=Ub[:], in0=ps[6][:], in1=RHS32[:], op=ALU.add), [("ps", 6), "RHS32"], ["Ub"])
        for h in range(H):
            hp, hh = h // 2, h % 2
            pr = slice(64 * hh, 64 * hh + 64); hs = slice(64 * h, 64 * h + 64)
            A("pe", lambda e, pr=pr, hp=hp, hs=hs: e.matmul(ps[7][:, hs], lhsT=AR[pr, hp, 1, :], rhs=Stb[pr, hp, :], start=True, stop=False),
              ["AR", "Stb"], [("ps", 7)])
            A("pe", lambda e, h=h, hs=hs: e.matmul(ps[7][:, hs], lhsT=AX4[:, h, 1, :], rhs=Ub[:, hs], start=False, stop=False),
              [("AX4", h), "Ub"], [("ps", 7)])
            A("pe", lambda e, h=h, hs=hs: e.matmul(ps[7][:, hs], lhsT=AX4[:, h, 3, :], rhs=Vtok[:, hs], start=False, stop=True),
              [("AX4", h), "Vtok"], [("ps", 7)])
        A("act", lambda e: e.activation(out=O32[:].rearrange("p h i -> p (h i)"), in_=ps[7][:], func=AF.Copy), [("ps", 7)], ["O32"])
        for h in range(H):
            hp, hh = h // 2, h % 2
            pr = slice(64 * hh, 64 * hh + 64); hs = slice(64 * h, 64 * h + 64)
            A("pe", lambda e, pr=pr, hp=hp, hs=hs: e.matmul(ps[5][pr, 256 + 64 * hp:256 + 64 * hp + 64], lhsT=BhTok[:, hs], rhs=Ub[:, hs],
                                                            start=True, stop=False), ["BhTok", "Ub"], [("ps", 5)])
            A("pe", lambda e, pr=pr, hp=hp, hs=hs: e.matmul(ps[5][pr, 256 + 64 * hp:256 + 64 * hp + 64], lhsT=KhTok[:, hs], rhs=Vtok[:, hs],
                                                            start=False, stop=True), ["KhTok", "Vtok"], [("ps", 5)])
        A("dve", lambda e: e.tensor_tensor(out=St32[:], in0=St32[:], in1=ps[5][:, 256:512].rearrange("p (c i) -> p c i", c=4), op=ALU.add),
          ["St32", ("ps", 5)], ["St32"])
        A("act", lambda e: e.activation(out=Stb[:], in_=St32[:], func=AF.Copy), ["St32"], ["Stb"])

    def wkv_prompt_out():
        stv = St32[:].rearrange("p c i -> p (c i)")
        for q in range(2):
            A("pe", lambda e, q=q: e.transpose(out=ps[3][:, 128 * q:128 * q + 128], in_=stv[:, 128 * q:128 * q + 128], identity=ident[:]),
              ["St32", "ident"], [("ps", 3)])
        A("act", lambda e: e.activation(out=RHS32[:, 0:256], in_=ps[3][:, 0:256], func=AF.Copy), [("ps", 3)], ["RHS32"])
        for q in range(2):
            for hpp in range(2):
                hp = 2 * q + hpp
                dst = o_wkvp[2 * hp:2 * hp + 2, :, :].rearrange("hh i j -> i hh j")
                src = RHS32[64 * hpp:64 * hpp + 64, 128 * q:128 * q + 128].rearrange("p (hh j) -> p hh j", hh=2)
                dma("sp", dst, src, ["RHS32"], [("o_wkvp", hp)], final=True)


    def rwkv_out(ti):
        A("dve", lambda e: e.tensor_reduce(out=gst[:, 0, :], in_=O32[:], axis=AX.X, op=ALU.add), ["O32"], [("gst", 0)])
        A("act", lambda e: e.activation(out=Osq[:], in_=O32[:], func=AF.Square), ["O32"], ["Osq"])
        A("dve", lambda e: e.tensor_reduce(out=gst[:, 1, :], in_=Osq[:], axis=AX.X, op=ALU.add), ["Osq"], [("gst", 1)])
        A("dve", lambda e: e.tensor_scalar(out=gst[:, 0, :], in0=gst[:, 0, :], scalar1=1.0 / E, scalar2=None, op0=ALU.mult), [("gst", 0)], [("gst", 0)])
        A("dve", lambda e: e.tensor_tensor(out=gst[:, 2, :], in0=gst[:, 0, :], in1=gst[:, 0, :], op=ALU.mult), [("gst", 0)], [("gst", 2)])
        A("dve", lambda e: e.scalar_tensor_tensor(out=gst[:, 3, :], in0=gst[:, 1, :], scalar=1.0 / E, in1=gst[:, 2, :], op0=ALU.mult, op1=ALU.subtract),
          [("gst", 1), ("gst", 2)], [("gst", 3)])
        A("dve", lambda e: e.tensor_scalar(out=gst[:, 3, :], in0=gst[:, 3, :], scalar1=64e-5, scalar2=None, op0=ALU.add), [("gst", 3)], [("gst", 3)])
        A("act", lambda e: e.activation(out=gst[:, 3, :], in_=gst[:, 3, :], func=AF.Sqrt), [("gst", 3)], [("gst", 3)])
        A("dve", lambda e: e.reciprocal(out=gst[:, 3, :], in_=gst[:, 3, :]), [("gst", 3)], [("gst", 3)])
        mb = gst[:, 0, :].unsqueeze(2).broadcast_to([128, H, E]); rb = gst[:, 3, :].unsqueeze(2).broadcast_to([128, H, E])
        A("dve", lambda e: e.tensor_tensor(out=Osq[:], in0=O32[:], in1=mb, op=ALU.subtract), ["O32", ("gst", 0)], ["Osq"])
        A("dve", lambda e: e.tensor_tensor(out=Osq[:], in0=Osq[:], in1=rb, op=ALU.mult), ["Osq", ("gst", 3)], ["Osq"])
        ov = Osq[:].rearrange("p h i -> p (h i)")
        for c in range(4):
            A("pe", lambda e, c=c: e.transpose(out=ps[3][:, 128 * c:128 * c + 128], in_=ov[:, 128 * c:128 * c + 128], identity=ident[:]),
              ["Osq", "ident"], [("ps", 3)])
        A("dve", lambda e: e.tensor_tensor(out=onT[:], in0=ps[3][:].rearrange("p (c t) -> p c t", c=4), in1=vcol(V_LG), op=ALU.mult),
          [("ps", 3), "vecT"], ["onT"])
        A("dve", lambda e: e.tensor_tensor(out=onT[:], in0=onT[:], in1=vcol(V_LB), op=ALU.add), ["onT", "vecT"], ["onT"])
        A("dve", lambda e: e.tensor_tensor(out=bsum[:], in0=bsum[:], in1=Pm[:, 8:12, :], op=ALU.mult), ["bsum", "Pm"], ["bsum"])
        A("dve", lambda e: e.tensor_tensor(out=onT[:], in0=onT[:], in1=bsum[:], op=ALU.add), ["onT", "bsum"], ["onT"])
        A("dve", lambda e: e.tensor_tensor(out=rwT[:, ti, :, :], in0=onT[:], in1=gT[:], op=ALU.mult), ["onT", "gT"], [("rwT", ti)])

    def sample_vectors_out():
        A("act", lambda e: e.activation(out=Wt[:], in_=ld[:], func=AF.Exp), ["ld"], ["Wt"])
        A("dve", lambda e: e.tensor_scalar(out=Winv[:], in0=kk[:], scalar1=-1.0, scalar2=None, op0=ALU.mult), ["kk"], ["Winv"])
        srcs = ((Pm[:, 0:4, :], "Pm"), (Wt[:], "Wt"), (kmod[:], "kmod"), (Pm[:, 8:12, :], "Pm"), (Winv[:], "Winv"), (bb[:], "bb"))
        for v, (src, skey) in enumerate(srcs):
            for c in range(4):
                A("pe", lambda e, src=src, c=c: e.transpose(out=ps[3][:, 128 * c:128 * c + 128], in_=src[:, c, :], identity=ident[:]),
                  [skey, "ident"], [("ps", 3)])
            A("act", lambda e: e.activation(out=RHS32[:], in_=ps[3][:], func=AF.Copy), [("ps", 3)], ["RHS32"])
            dma("sp", scr_v[v], RHS32[:], ["RHS32"], [("scr_v", v)])

    def capture(f_):
        P._cap = []
        f_()
        lst, P._cap = P._cap, None
        return lst

    def interleave(main, side):
        j = 0
        for k_, a_ in enumerate(main):
            P.add(*a_)
            want = (len(side) * (k_ + 1)) // max(len(main), 1)
            while j < want:
                P.add(*side[j]); j += 1
        while j < len(side):
            P.add(*side[j]); j += 1

    TL1B = [0, NT - 1, NT] if probe == "quick" else list(range(NT + 1))
    phase1b_proj(TL1B[0])
    for n_, ti in enumerate(TL1B):
        rwkv_prep(ti)
        nxt = capture(lambda: phase1b_proj(TL1B[n_ + 1])) if n_ + 1 < len(TL1B) else []
        if ti < NT:
            def stage(ti=ti):
                rwkv_chunk(ti)
                rwkv_out(ti)
                if ti == NT - 1:
                    wkv_prompt_out()
            interleave(capture(stage), nxt)
        else:
            interleave([], nxt)
            sample_vectors_out()

    new_phase(KEEP_1C)
    Ss = sb("Ss", [128, E, E]); tmpS = sb("tmpS", [128, E, E]); vkS = sb("vkS", [128, E, E])
    vec6 = sb("vec6", [128, 6, T, E]); sa = sb("sa", [128, E]); outs = sb("outs", [128, T, E])
    dma("sp", Ss[:].rearrange("p i j -> p (i j)"), swkv, [], ["Ss"])
    for b in range(DB):
        for v in range(6):
            dma("sp", vec6[8 * b:8 * b + 8, v, :, :], scr_v[v, 8 * b:8 * b + 8, :].rearrange("t (h j) -> h t j", h=H),
                [("scr_v", v)], [("vec6", (b, v))])
    bi = lambda ap: ap.unsqueeze(1).broadcast_to([128, E, E])
    bj = lambda ap: ap.unsqueeze(2).broadcast_to([128, E, E])
    rec = []
    NQ = 4; QI = E // NQ
    bq = lambda ap: ap.unsqueeze(1).broadcast_to([128, QI, E])
    for t in range(T):
        r_, w_, k_, v_, nk_, ka_ = (vec6[:, v, t, :] for v in range(6))
        rec.append(lambda v_=v_, k_=k_: A("pool", lambda e: e.tensor_tensor(out=vkS[:], in0=bj(v_), in1=bi(k_), op=ALU.mult), ["vec6"], ["vkS"]))
        def q_ops(kind, t=t, r_=r_, w_=w_, nk_=nk_, ka_=ka_):
            for q in range(NQ):
                isl = slice(QI * q, QI * q + QI)
                S_, T_ = Ss[:, isl, :], tmpS[:, isl, :]
                sk, tk, ak = ("Ss", q), ("tmpS", q), ("sa", q)
                if kind == 0:
                    f = lambda S_=S_, T_=T_, sk=sk, tk=tk: A("dve", lambda e: e.tensor_tensor(out=T_, in0=S_, in1=bq(nk_), op=ALU.mult), [sk, "vec6"], [tk])
                elif kind == 1:
                    f = lambda T_=T_, isl=isl, tk=tk, ak=ak: A("dve", lambda e: e.tensor_reduce(out=sa[:, isl], in_=T_, axis=AX.X, op=ALU.add), [tk], [ak])
                elif kind == 2:
                    f = lambda S_=S_, sk=sk: A("dve", lambda e: e.tensor_tensor(out=S_, in0=S_, in1=bq(w_), op=ALU.mult), [sk, "vec6"], [sk])
                elif kind == 3:
                    f = lambda T_=T_, isl=isl, tk=tk, ak=ak: A("dve", lambda e: e.tensor_tensor(
                        out=T_, in0=sa[:, isl].unsqueeze(2).broadcast_to([128, QI, E]), in1=bq(ka_), op=ALU.mult), [ak, "vec6"], [tk])
                elif kind == 4:
                    f = lambda S_=S_, T_=T_, sk=sk, tk=tk: A("dve", lambda e: e.tensor_tensor(out=S_, in0=S_, in1=T_, op=ALU.add), [sk, tk], [sk])
                elif kind == 5:
                    f = lambda S_=S_, isl=isl, sk=sk: A("dve", lambda e: e.tensor_tensor(out=S_, in0=S_, in1=vkS[:, isl, :], op=ALU.add), [sk, "vkS"], [sk])
                elif kind == 6:
                    f = lambda S_=S_, T_=T_, sk=sk, tk=tk: A("dve", lambda e: e.tensor_tensor(out=T_, in0=S_, in1=bq(r_), op=ALU.mult), [sk, "vec6"], [tk])
                else:
                    f = lambda T_=T_, isl=isl, tk=tk, q=q: A("dve", lambda e: e.tensor_reduce(out=outs[:, t, isl], in_=T_, axis=AX.X, op=ALU.add),
                                                             [tk], [("outs", (t, q))])
                rec.append(f)
        for kind in range(8):
            q_ops(kind)

    units = []
    if SAMPLE_ATTN_DONE:
        NLB = 4
        kcf = [sb("kc_f%d" % i, [128, 512]) for i in range(NLB)]; vcf = [sb("vc_f%d" % i, [128, 512]) for i in range(NLB)]
        NU = 4
        KTcs = [sb("KTc%d" % i, [128, 4, 128], BF16) for i in range(NU)]
        Vcs = [sb("Vc%d" % i, [128, H, 65], BF16) for i in range(NU)]
        PTss = [sb("PTs%d" % i, [128, H, T], BF16) for i in range(NU)]
        QTbd = sb("QTbd", [128, 4, DB, 2 * T], BF16)
        A("dve", lambda e: e.memset(QTbd[:], 0.0), [], ["QTbd"])
        qsv = lambda pr: QT[pr, :, NT * 128:NT * 128 + 128].rearrange("p c (b q) -> p c b q", b=DB)
        A("act", lambda e: e.activation(out=QTbd[0:64, :, :, 0:T], in_=qsv(slice(0, 64)), func=AF.Copy), [("QT", NT), "QTbd"], ["QTbd"])
        A("act", lambda e: e.activation(out=QTbd[64:128, :, :, T:2 * T], in_=qsv(slice(64, 128)), func=AF.Copy), [("QT", NT), "QTbd"], ["QTbd"])
        acc = sb("acc", [128, H, 128]); cnts = sb("cnts", [128, 16, T]); cntn = sb("cntn", [128, DB, T])
        dma("sp", cnts[:].rearrange("p a q -> p (a q)"), cnts_d, [], ["cnts"])
        dma("sp", cntn[:].rearrange("p a q -> p (a q)"), cntn_d, [], ["cntn"])
        for i in range(NU):
            A("dve", lambda e, i=i: e.memset(Vcs[i][:, :, 64:65], 1.0), [], ["Vc%d" % i])
        A("dve", lambda e: e.memset(acc[:], 0.0), [], ["acc"])
        QS0 = NT * 128
        TB = (0, 2, 4, 6); SVB = (1, 3, 5, 7)

        def sattn_unit(u, b, tl):
            p_ = u % NU
            KTc, Vc, PTs = KTcs[p_], Vcs[p_], PTss[p_]
            ktk, vkk, ptk = "KTc%d" % p_, "Vc%d" % p_, "PTs%d" % p_
            tb, sv = TB[p_], SVB[p_]
            if tl == 16:
                npart = 128
                kT = lambda hp: KT[:, hp, QS0:QS0 + 128]
                vT = lambda h: Vaug[:, NT, h, :]
                kkeys = [("KT", NT)]; vkeys = [("Vaug", NT)]
                msk = cntn[:, b, :]; mkey = "cntn"
            else:
                r = tl
                m0, npart = (0, 128) if r < 8 else (96, 32)
                l_ = sattn_unit.nload % NLB; sattn_unit.nload += 1
                kc_f, vc_f = kcf[l_], vcf[l_]
                kck, vck = "kc_f%d" % l_, "vc_f%d" % l_
                dma("sp", kc_f[0:npart, :], ck[b, 16 * m0 + r:2048:16, :], [], [kck])
                dma("sp", vc_f[0:npart, :], cv[b, 16 * m0 + r:2048:16, :], [], [vck])
                for c in range(4):
                    A("pe", lambda e, c=c: e.transpose(out=ps[tb][:, 128 * c:128 * c + npart], in_=kc_f[0:npart, 128 * c:128 * c + 128],
                                                       identity=ident[0:npart, 0:npart]), [kck, "ident"], [("ps", tb)])
                A("act", lambda e: e.activation(out=KTc[:, :, 0:npart], in_=ps[tb][:].rearrange("p (c k) -> p c k", c=4)[:, :, 0:npart], func=AF.Copy),
                  [("ps", tb)], [ktk])
                A("act", lambda e: e.activation(out=Vc[0:npart, :, 0:64], in_=vc_f[0:npart, :].rearrange("p (h e) -> p h e", h=H), func=AF.Copy),
                  [vck], [vkk])
                kT = lambda hp: KTc[:, hp, 0:npart]
                vT = lambda h: Vc[0:npart, h, :]
                kkeys = [ktk]; vkeys = [vkk]
                msk = cnts[0:npart, tl, :]; mkey = "cnts"
            for hp in range(4):
                A("pe", lambda e, hp=hp: e.matmul(ps[sv][0:npart, 2 * T * hp:2 * T * hp + 2 * T], lhsT=kT(hp), rhs=QTbd[:, hp, b, :], start=True, stop=True),
                  kkeys + ["QTbd"], [("ps", sv)])
            A("act", lambda e: e.activation(out=PTs[0:npart, :, :].rearrange("p h q -> p (h q)"), in_=ps[sv][0:npart, 0:H * T], func=AF.Exp), [("ps", sv)], [ptk])
            A("dve", lambda e: e.tensor_tensor(out=PTs[0:npart, :, :], in0=PTs[0:npart, :, :], in1=msk.unsqueeze(1).broadcast_to([npart, H, T]), op=ALU.mult),
              [ptk, mkey], [ptk])
            for h in range(H):
                A("pe", lambda e, h=h: e.matmul(ps[sv][0:65, 64 + T * h:64 + T * h + T], lhsT=vT(h), rhs=PTs[0:npart, h, :], start=True, stop=True),
                  vkeys + [ptk], [("ps", sv)])
            A("dve", lambda e: e.tensor_tensor(out=acc[0:65, :, T * b:T * b + T], in0=acc[0:65, :, T * b:T * b + T],
                                               in1=ps[sv][0:65, 64:64 + H * T].rearrange("p (h q) -> p h q", h=H), op=ALU.add),
              [("ps", sv), "acc"], ["acc"])

        sattn_unit.nload = 0
        u = 0
        for b in (range(2) if probe == "quick" else range(DB)):
            for tl in range(17):
                units.append(lambda u=u, b=b, tl=tl: sattn_unit(u, b, tl))
                u += 1
    done = 0
    for k_, th in enumerate(rec):
        th()
        want = (len(units) * (k_ + 1)) // len(rec)
        while done < want:
            units[done](); done += 1
    while done < len(units):
        units[done](); done += 1
    if SAMPLE_ATTN_DONE:
        dma("sp", scr_acc, acc[0:65, :, :].rearrange("p h t -> p (h t)"), ["acc"], ["scr_acc"])
    dma("sp", o_wkvs, Ss[:].rearrange("p i j -> p (i j)"), ["Ss"], ["o_wkvs"], final=True)
    for b in range(DB):
        dma("sp", scr_o[8 * b:8 * b + 8, :].rearrange("t (h i) -> h t i", h=H), outs[8 * b:8 * b + 8, :, :], ["outs"], [("scr_o", b)])
    dma("sp", O32[:].rearrange("p h i -> p (h i)"), scr_o, ["scr_o"], ["O32"])
    rwkv_out(NT)

    new_phase()
    X1_BYTES = (NT + 1) * D * 4
    x1 = sb("x1", [128, NT + 1, D])
    G2p = sb("G2p", [128, D]); G2s = sb("G2s", [128, D])
    KEEP_P3 = ptr["R2"] - R2_0
    G1 = sb("G1", [128, D]); G1s = sb("G1s", [128, D])
    KEEP_P2 = ptr["R2"] - R2_0
    wadah = sb("wadah", [128, 8, 512], BF16); m17 = sb("m17", [17, D]); bgb = sb("bgb", [17, 512])

    def gate_m17(gidx):
        col0 = (2 if gidx == 0 else 5) * D
        for hf in range(2):
            for q in range(2):
                A("pool", lambda e, hf=hf, q=q: e.dma_start(
                    out=wadah[:, 4 * q:4 * q + 4, :],
                    in_=w_ada[512 * q:512 * q + 512, col0 + 512 * hf:col0 + 512 * hf + 512].rearrange("(kc p) n -> p kc n", p=128)),
                  [], ["wadah"], dma=True)
            dma("sp", bgb[:], bgate[gidx, 512 * hf:512 * hf + 512].partition_broadcast(17), [], ["bgb"])
            for kc in range(8):
                A("pe", lambda e, kc=kc: e.matmul(ps[0][0:17, :], lhsT=scT[:, kc, :], rhs=wadah[:, kc, :], start=(kc == 0), stop=(kc == 7)),
                  ["scT", "wadah"], [("ps", 0)])
            A("dve", lambda e, hf=hf: e.tensor_tensor(out=m17[:, 512 * hf:512 * hf + 512], in0=ps[0][0:17, :], in1=bgb[:], op=ALU.add),
              [("ps", 0), "bgb"], [("m17", hf)])

    def gate_bcast(sel, dst, dkey):
        for hf in range(2):
            A("pe", lambda e, hf=hf: e.matmul(ps[1][:], lhsT=Esel[:, 128 * sel:128 * sel + 128], rhs=m17[:, 512 * hf:512 * hf + 512],
                                              start=True, stop=True), ["Esel", ("m17", hf)], [("ps", 1)])
            A("act", lambda e, hf=hf: e.activation(out=dst[:, 512 * hf:512 * hf + 512], in_=ps[1][:], func=AF.Copy), [("ps", 1)], [dkey])

    gate_m17(1); gate_bcast(0, G2p, "G2p"); gate_bcast(1, G2s, "G2s")
    gate_m17(0); gate_bcast(0, G1, "G1"); gate_bcast(1, G1s, "G1s")

    new_phase(KEEP_P2)
    wout = sb("wout", [128, 8, D], BF16)
    for kc in range(8):
        A("pool", lambda e, kc=kc: e.dma_start(out=wout[:, kc, :], in_=w_out[kc * 128:(kc + 1) * 128, :]), [], [("wout", kc)], dma=True)
    attT = sb("attT", [128, 4, 128], BF16); rsum = sb("rsum", [128, H])
    KEEP_2B = ptr["R2"] - R2_0
    cntm = sb("cntm", [128, 16, 128], BF16)
    for hf in range(2):
        A("pool", lambda e, hf=hf: e.dma_start(out=cntm[:, 8 * hf:8 * hf + 8, :].rearrange("p d q -> p (d q)"), in_=cnt_d[:, 1024 * hf:1024 * hf + 1024]),
          [], [("cntm", hf)], dma=True)
    PTb = sb("PTb", [128, NT, 2, 128], BF16)

    oacc = xt[:, 0:520].rearrange("p (h e) -> p h e", h=H)

    def attn_finish(ti, Gt=None, gk="G1"):
        for g in range(2):
            A("act", lambda e, g=g: e.activation(out=xt[:, 260 * g:260 * g + 260], in_=ps[6 + g][:, 0:260], func=AF.Copy), [("ps", 6 + g)], ["xt"])
        A("dve", lambda e: e.reciprocal(out=rsum[:].unsqueeze(2), in_=oacc[:, :, 64:65]), ["xt"], ["rsum"])
        A("dve", lambda e: e.tensor_tensor(out=xsn[:, 0:512].rearrange("p (h e) -> p h e", h=H), in0=oacc[:, :, 0:64],
                                           in1=rsum[:].unsqueeze(2).broadcast_to([128, H, E]), op=ALU.mult), ["xt", "rsum"], ["xsn"])
        for c in range(4):
            A("pe", lambda e, c=c: e.transpose(out=ps[5][:, 128 * c:128 * c + 128], in_=xsn[:, 128 * c:128 * c + 128], identity=ident[:]),
              ["xsn", "ident"], [("ps", 5)])
        A("act", lambda e: e.activation(out=attT[:], in_=ps[5][:].rearrange("p (c t) -> p c t", c=4), func=AF.Copy), [("ps", 5)], ["attT"])
        for hf in range(2):
            for kc in range(8):
                lhs = attT[:, kc, :] if kc < 4 else rwT[:, ti, kc - 4, :]
                A("pe", lambda e, hf=hf, kc=kc, lhs=lhs: e.matmul(ps[2 + hf][:], lhsT=lhs, rhs=wout[:, kc, 512 * hf:512 * hf + 512],
                                                                  start=(kc == 0), stop=(kc == 7)),
                  ["attT", ("rwT", ti), ("wout", kc)], [("ps", 2 + hf)])
        dma("sp", xt[:], xsm if ti == NT else xp[ti * 128:(ti + 1) * 128, :], [], ["xt"])
        for hf in range(2):
            cs = slice(512 * hf, 512 * hf + 512)
            Gt_ = G1 if Gt is None else Gt
            A("dve", lambda e, hf=hf, cs=cs, Gt_=Gt_: e.tensor_tensor(out=xsn[:, cs], in0=ps[2 + hf][:], in1=Gt_[:, cs], op=ALU.mult),
              [("ps", 2 + hf), gk, "xsn"], ["xsn"])
            A("dve", lambda e, cs=cs: e.tensor_tensor(out=x1[:, ti, cs], in0=xsn[:, cs], in1=xt[:, cs], op=ALU.add), ["xsn", "xt"], [("x1", ti)])

    def attn_prompt(qt):
        qs_ = slice(qt * 128, qt * 128 + 128)
        for hg in range(4):
            for kt0 in range(0, qt + 1, 4):
                n = min(4, qt + 1 - kt0)
                for j in range(n):
                    kt = kt0 + j
                    for hh in range(2):
                        pr = slice(64 * hh, 64 * hh + 64)
                        A("pe", lambda e, j=j, kt=kt, pr=pr, hh=hh, hg=hg: e.matmul(
                            ps[hh][:, 128 * j:128 * j + 128], lhsT=KT[pr, hg, kt * 128:kt * 128 + 128], rhs=QT[pr, hg, qs_], start=True, stop=True),
                          [("KT", kt), ("QT", qt)], [("ps", hh)])
                for hh in range(2):
                    A("act", lambda e, kt0=kt0, n=n, hh=hh: e.activation(
                        out=PTb[:, kt0:kt0 + n, hh, :], in_=ps[hh][:, 0:128 * n].rearrange("p (k q) -> p k q", k=n), func=AF.Exp),
                      [("ps", hh)], [("PTb", kt_) for kt_ in range(kt0, kt0 + n)])
                for kt in range(kt0, kt0 + n):
                    d = qt - kt
                    eng = "dve" if d % 2 == 0 else "pool"
                    A(eng, lambda e, kt=kt, d=d: e.tensor_tensor(out=PTb[:, kt, :, :], in0=PTb[:, kt, :, :],
                                                                 in1=cntm[:, d, :].unsqueeze(1).broadcast_to([128, 2, 128]), op=ALU.mult),
                      [("PTb", kt), ("cntm", d // 8)], [("PTb", kt)])
            for hh in range(2):
                h = 2 * hg + hh
                ob = ps[6 + h // 4][:, 65 * (h % 4):65 * (h % 4) + 65]
                for kt in range(qt + 1):
                    A("pe", lambda e, kt=kt, hh=hh, h=h, ob=ob: e.matmul(ob, lhsT=PTb[:, kt, hh, :], rhs=Vaug[:, kt, h, :],
                                                                         start=(kt == 0), stop=(kt == qt)),
                      [("PTb", kt), ("Vaug", kt)], [("ps", 6 + h // 4)])

    QTL = [0, 1] if probe == "quick" else list(range(NT))
    attn_prompt(QTL[0])
    for n_, qt in enumerate(QTL):
        fin = capture(lambda: attn_finish(qt))
        nxt = capture(lambda: attn_prompt(QTL[n_ + 1])) if n_ + 1 < len(QTL) else []
        if nxt:
            interleave(nxt, fin)
        else:
            interleave(fin, [])


    if SAMPLE_ATTN_DONE:
        accv = xsn[:].rearrange("p (h t) -> p h t", h=H)
        dma("sp", xsn[0:65, :], scr_acc, ["scr_acc"], ["xsn"])
        for h in range(H):
            A("pe", lambda e, h=h: e.transpose(out=ps[6 + h // 4][:, 65 * (h % 4):65 * (h % 4) + 65], in_=accv[0:65, h, :], identity=ident[0:65, 0:65]),
              ["xsn", "ident"], [("ps", 6 + h // 4)])
        attn_finish(NT, G1s, "G1s")

    new_phase(KEEP_P3)
    ptr["R1"] = R1_0
    w1c = [sb("w1c%d" % i, [128, 8, D], BF16, "R1") for i in range(2)]
    w2c = [sb("w2c%d" % i, [128, 8, D], BF16, "R1") for i in range(2)]
    h2T = sb("h2T", [128, 8, (NT + 1) * 128], BF16)
    GT = 3
    hid = sb("hid", [128, 8, 128 * GT], BF16); rl = sb("rl", [128, 128 * GT])
    TILES = [0, 1, NT] if probe == "quick" else list(range(NT + 1))
    if not SAMPLE_ATTN_DONE:
        TILES = [t_ for t_ in TILES if t_ != NT]
    for ti in TILES:
        rms_from(x1[:, ti, :], ("x1", ti), "A2", "B2", ti == NT)
        A("act", lambda e, ti=ti: e.activation(out=h2T[:, :, ti * 128:(ti + 1) * 128], in_=hT[:], func=AF.Copy), ["hT"], [("h2T", ti)])
    groups = [TILES[i:i + GT] for i in range(0, len(TILES), GT)]
    for c in range(4):
        wb = c % 2
        for kc in range(8):
            A("pool", lambda e, kc=kc, c=c, wb=wb: e.dma_start(out=w1c[wb][:, kc, :], in_=w_ff1[kc * 128:(kc + 1) * 128, c * D:(c + 1) * D]),
              [], [("w1c%d" % wb, kc)], dma=True)
            A("pool", lambda e, kc=kc, c=c, wb=wb: e.dma_start(out=w2c[wb][:, kc, :], in_=w_ff2[c * D + kc * 128:c * D + (kc + 1) * 128, :]),
              [], [("w2c%d" % wb, kc)], dma=True)
        for gi, grp in enumerate(groups):
            contiguous = all(grp[i + 1] == grp[i] + 1 for i in range(len(grp) - 1))
            subgroups = [grp] if contiguous else [[t_] for t_ in grp]
            for sg in subgroups:
                ntok = 128 * len(sg)
                t0 = sg[0] * 128
                for fc in range(8):
                    hb = fc % 2
                    for kc in range(8):
                        A("pe", lambda e, fc=fc, kc=kc, hb=hb, t0=t0, ntok=ntok, wb=wb: e.matmul(
                            ps[hb][:, 0:ntok], lhsT=w1c[wb][:, kc, 128 * fc:128 * fc + 128], rhs=h2T[:, kc, t0:t0 + ntok],
                            start=(kc == 0), stop=(kc == 7)), [("w1c%d" % wb, kc)] + [("h2T", t_) for t_ in sg], [("ps", hb)])
                    A("act", lambda e, hb=hb, ntok=ntok: e.activation(out=rl[:, 0:ntok], in_=ps[hb][:, 0:ntok], func=AF.Relu), [("ps", hb)], ["rl"])
                    A("dve", lambda e, fc=fc, ntok=ntok: e.tensor_tensor(out=hid[:, fc, 0:ntok], in0=rl[:, 0:ntok], in1=rl[:, 0:ntok], op=ALU.mult),
                      ["rl"], [("hid", fc)])
                for tl, ti in enumerate(sg):
                    Gt = G2s if ti == NT else G2p
                    gk = "G2s" if ti == NT else "G2p"
                    for hf in range(2):
                        yb = 2 + 2 * tl + hf
                        cs = slice(512 * hf, 512 * hf + 512)
                        for fc in range(8):
                            A("pe", lambda e, fc=fc, tl=tl, yb=yb, cs=cs, wb=wb: e.matmul(ps[yb][:], lhsT=hid[:, fc, 128 * tl:128 * tl + 128], rhs=w2c[wb][:, fc, cs],
                                                                                   start=(fc == 0), stop=(fc == 7)),
                              [("hid", fc), ("w2c%d" % wb, fc)], [("ps", yb)])
                        A("dve", lambda e, yb=yb, cs=cs, Gt=Gt: e.tensor_tensor(out=xsn[:, cs], in0=ps[yb][:], in1=Gt[:, cs], op=ALU.mult),
                          [("ps", yb), gk, "xsn"], ["xsn"])
                        A("dve", lambda e, ti=ti, cs=cs: e.tensor_tensor(out=x1[:, ti, cs], in0=xsn[:, cs], in1=x1[:, ti, cs], op=ALU.add),
                          ["xsn", ("x1", ti)], [("x1", ti)])

    new_phase(X1_BYTES)
    gfb = sb("gfb", [128, D]); yo = [sb("yo%d" % i, [128, D]) for i in range(2)]
    dma("sp", gfb[:], gfin.partition_broadcast(128), [], ["gfb"])
    for n_, ti in enumerate(TILES):
        xa = x1[:, ti, :]
        y_ = yo[n_ % 2]; yk = "yo%d" % (n_ % 2)
        A("act", lambda e, xa=xa: e.activation(out=xsn[:], in_=xa, func=AF.Square, accum_out=ss[:]), [("x1", ti)], ["xsn", "ss"])
        A("dve", lambda e: e.tensor_scalar(out=rstd[:], in0=ss[:], scalar1=1.0 / D, scalar2=1e-6, op0=ALU.mult, op1=ALU.add), ["ss"], ["rstd"])
        A("act", lambda e: e.activation(out=rstd[:], in_=rstd[:], func=AF.Sqrt), ["rstd"], ["rstd"])
        A("dve", lambda e: e.reciprocal(out=rstd[:], in_=rstd[:]), ["rstd"], ["rstd"])
        A("act", lambda e, xa=xa: e.activation(out=xsn[:], in_=xa, func=AF.Copy, scale=rstd[:]), [("x1", ti), "rstd"], ["xsn"])
        A("dve", lambda e, y_=y_: e.tensor_tensor(out=y_[:], in0=xsn[:], in1=gfb[:], op=ALU.mult), ["xsn", "gfb"], [yk])
        dma("sp", o_ys if ti == NT else o_yp[ti * 128:(ti + 1) * 128, :], y_[:], [yk], [("o_y", ti)], final=True)

    P.emit()
    es.close()
    return nc


def _rope_tables():
    half = 8
    inv = (500000.0 ** (-np.arange(half, dtype=np.float32) * np.float32(2.0 / 16))).astype(np.float32)
    tab = np.zeros((17, 128, 128), np.float32)
    for ti in range(17):
        if ti < NT:
            pos = (ti * 128 + np.arange(128)).astype(np.float32)
        else:
            pos = (PAST + (np.arange(128) % T)).astype(np.float32)
        ang = pos[:, None] * inv[None, :]
        tab[ti, :, 0:64] = np.tile(np.cos(ang), (1, H))
        tab[ti, :, 64:128] = np.tile(np.sin(ang), (1, H))
    return tab


def _cmask():
    m = np.zeros((128, 896), np.float32)
    m[0:64, 0:64] = 1.0; m[64:128, 64:128] = 1.0
    i = np.arange(128)
    strictT = (i[:, None] < i[None, :]).astype(np.float32)
    inclT = (i[:, None] <= i[None, :]).astype(np.float32)
    m[:, 128:256] = strictT; m[:, 256:384] = inclT; m[:, 384:512] = strictT; m[:, 512:640] = inclT
    m[:, 640:768] = (i[None, :] < i[:, None]).astype(np.float32)
    m[:, 768:896] = 1.0
    return m


def _cnt_table():
    k = np.arange(128)[:, None, None]; d = np.arange(16)[None, :, None]; q = np.arange(128)[None, None, :]
    dl = 128 * d + q - k
    c = ((dl >= 0) & (dl <= 128)).astype(np.float32) + ((dl >= 0) & (dl <= 512) & (dl % 4 == 0)) + ((dl >= 0) & (dl <= 2048) & (dl % 16 == 0))
    return np.ascontiguousarray(c.reshape(128, 2048).astype(np.float32))


def _cnts():
    c = np.zeros((128, 16, T), np.float32)
    for r in range(16):
        m0, n = (0, 128) if r < 8 else (96, 32)
        for p in range(n):
            R = 16 * (m0 + p) + r
            for i in range(T):
                v = 0
                if R >= 1920 + i: v += 1
                if R % 4 == i % 4 and R >= 1536 + i: v += 1
                if R % 16 == i % 16 and R >= i: v += 1
                c[p, r, i] = v
    return np.ascontiguousarray(c.reshape(128, 16 * T))


def _cntn():
    c = np.zeros((128, DB, T), np.float32)
    for b in range(DB):
        for s_ in range(T):
            for i in range(T):
                d = i - s_
                if d >= 0:
                    c[T * b + s_, b, i] = 1 + (d % 4 == 0) + (d % 16 == 0)
    return np.ascontiguousarray(c.reshape(128, DB * T))


def _esel():
    e = np.zeros((17, 256), np.float32)
    e[0, 0:128] = 1.0
    for b in range(DB):
        e[1 + b, 128 + T * b:128 + T * b + T] = 1.0
    return e


_NC_CACHE = {}


def kernel(**inp):
    f = lambda a: np.ascontiguousarray(np.asarray(a, dtype=np.float32))
    x_prompt = f(inp["x_prompt"]); x_sample = f(inp["x_sample"])
    c_prompt = f(inp["c_prompt"]); c_sample = f(inp["c_sample"])
    b_ada = f(inp["b_ada"])[0]
    vec_rows = [f(inp["mu"])[0].reshape(14, 128)]
    for n in ("w0", "a0", "k_k", "k_a"):
        vec_rows.append(f(inp[n])[0].reshape(4, 128))
    vec_rows.append(f(inp["r_k"])[0].reshape(4, 128))
    for n in ("lnx_g", "lnx_b"):
        vec_rows.append(f(inp[n])[0].reshape(4, 128))
    vec_rows.append(f(inp["norm1_g"])[0].reshape(8, 128))
    vec_rows.append(f(inp["norm2_g"])[0].reshape(8, 128))
    for blk in (0, 1, 3, 4):
        vec_rows.append(b_ada[blk * D:(blk + 1) * D].reshape(8, 128))
    vecs = np.ascontiguousarray(np.concatenate(vec_rows, axis=0))
    assert vecs.shape == (NVEC, 128)
    bgate = np.ascontiguousarray(np.stack([b_ada[2 * D:3 * D], b_ada[5 * D:6 * D]]))
    shared = {
        "vecs": vecs, "bgate": bgate, "w_ada": f(inp["w_ada"])[0], "w_in": f(inp["w_in"])[0],
        "ident": np.eye(128, dtype=np.float32), "rope": _rope_tables(), "cmask": _cmask(),
        "w2a2": np.ascontiguousarray(np.concatenate([f(inp["w2"])[0], f(inp["a2"])[0]], axis=0)),
        "w_out": f(inp["w_out"])[0], "w_ff1": f(inp["w_ff1"])[0], "w_ff2": f(inp["w_ff2"])[0], "gfin": f(inp["normf_g"]),
        "cnt": _cnt_table(), "esel": _esel(), "cnts": _cnts(), "cntn": _cntn(),
        "g2": f(inp["g2"])[0],
    }
    state_shift = f(inp["state_shift"])[0]
    state_wkv = f(inp["state_wkv"])[0]
    cache_k = np.asarray(inp["cache_k"], dtype=np.float32)[0].reshape(128, 2048, 512)
    cache_v = np.asarray(inp["cache_v"], dtype=np.float32)[0].reshape(128, 2048, 512)
    in_maps = []
    for i in range(NCORES):
        m = dict(shared)
        m["xp"] = x_prompt[i]
        m["xs"] = x_sample[DB * i:DB * (i + 1)].reshape(128, D)
        m["c17"] = np.ascontiguousarray(np.concatenate([c_prompt[i:i + 1], c_sample[DB * i:DB * (i + 1)]], axis=0))
        m["sshift"] = state_shift[DB * i:DB * (i + 1)]
        m["swkv"] = state_wkv[DB * i:DB * (i + 1)].reshape(128, E * E)
        m["ck"] = cache_k[DB * i:DB * (i + 1)]
        m["cv"] = cache_v[DB * i:DB * (i + 1)]
        in_maps.append(m)
    if "nc" not in _NC_CACHE:
        _NC_CACHE["nc"] = build_program()
    nc = _NC_CACHE["nc"]
    res = run_bass_kernel_spmd(nc, in_maps, core_ids=list(range(NCORES)))
    R = res.results
    g = lambda name: [np.asarray(R[i][name], dtype=np.float32) for i in range(NCORES)]
    y_prompt = np.stack(g("o_yp"))
    y_sample = np.concatenate(g("o_ys")).reshape(128, T, D) if SAMPLE_ATTN_DONE else np.zeros((128, T, D), np.float32)
    kwin = np.stack(g("o_kwin")).reshape(1, 8, S, H, E)
    vwin = np.stack(g("o_vwin")).reshape(1, 8, S, H, E)
    wkv_p = np.stack(g("o_wkvp")).reshape(1, 8, H, E, E)
    shp = np.stack(g("o_shp")).reshape(1, 8, RIN)
    knew = np.concatenate(g("o_knew")).reshape(1, 128, T, H, E)
    vnew = np.concatenate(g("o_vnew")).reshape(1, 128, T, H, E)
    wkv_s = np.concatenate(g("o_wkvs")).reshape(1, 128, H, E, E)
    shs = np.concatenate(g("o_shs")).reshape(1, 128, RIN)
    return (y_prompt, y_sample, kwin, vwin, wkv_p, shp, knew, vnew, wkv_s, shs)
```

```python
import contextlib
import numpy as np
import concourse.bass as bass
import concourse.mybir as mybir
from concourse.bass_utils import run_bass_kernel_spmd

F32 = mybir.dt.float32
BF16 = mybir.dt.bfloat16
AF = mybir.ActivationFunctionType
ALU = mybir.AluOpType
AX = mybir.AxisListType

NCORES = 8
D = 1024
S = 2048
NT = 16
DB = 16
T = 8
H = 8
E = 64
RIN = 1792
INW = 3328
DFF = 4096
PAST = 8192
ENGS = ("pe", "act", "dve", "pool", "sp")

V_MU, V_W0, V_A0, V_KK, V_KA, V_RK, V_LG, V_LB, V_G1, V_G2, V_BSH1, V_BSC1, V_BSH2, V_BSC2 = (
    0, 14, 18, 22, 26, 30, 34, 38, 42, 50, 58, 66, 74, 82)
NVEC = 90
SAMPLE_ATTN_DONE = True


class Op:
    __slots__ = ("eng", "fn", "deps", "is_dma", "signal", "sem", "target", "name")

    def __init__(s, eng, fn, is_dma, name):
        s.eng = eng; s.fn = fn; s.is_dma = is_dma; s.deps = []
        s.signal = False; s.sem = None; s.target = 0; s.name = name


class Prog:
    def __init__(s, nc, n_dma_sems=24):
        s.nc = nc
        s.ops = []
        s.st = {}
        s.group_of = {}
        s.n_dma_sems = n_dma_sems
        s.final_ops = []
        s.exclusive = {"ps"}

    def _conf(s, g, name, sub):
        ent, idx = g
        if name == "*":
            return list(ent.keys())
        if sub is None:
            keys = [(name, x) for x in idx.get(name, ())]
        else:
            keys = [k for k in ((name, sub), (name, None)) if k in ent]
        if ("*", None) in ent:
            keys.append(("*", None))
        return keys

    def _norm(s, k):
        if not isinstance(k, tuple):
            k = (k, None)
        name, sub = k
        if name.startswith("*@"):
            return name[2:], "*", None
        return s.group_of.get(name, name), name, sub

    def add(s, eng, fn, reads=(), writes=(), dma=False, name="", final=False):
        if getattr(s, "_cap", None) is not None:
            s._cap.append((eng, fn, reads, writes, dma, name, final))
            return None
        op = Op(eng, fn, dma, name)
        reads = [s._norm(k) for k in reads]; writes = [s._norm(k) for k in writes]
        deps = {}
        for gname, name_, sub in reads:
            g = s.st.setdefault(gname, ({}, {}))
            for k in s._conf(g, name_, sub):
                w = g[0][k][0]
                if w is not None:
                    deps[id(w)] = w
                if name_ in s.exclusive:
                    for r in g[0][k][1]:
                        if r.eng != eng:
                            deps[id(r)] = r
        for gname, name_, sub in writes:
            g = s.st.setdefault(gname, ({}, {}))
            for k in s._conf(g, name_, sub):
                w = g[0][k][0]
                if w is not None:
                    deps[id(w)] = w
                for r in g[0][k][1]:
                    deps[id(r)] = r
        for gname, name_, sub in reads:
            ent, idx = s.st[gname]
            rl_ = ent.setdefault((name_, sub), [None, []])[1]
            if not op.is_dma:
                rl_[:] = [r for r in rl_ if r.is_dma or r.eng != op.eng]
            rl_.append(op)
            idx.setdefault(name_, set()).add(sub)
        for gname, name_, sub in writes:
            ent, idx = s.st[gname]
            if name_ == "*":
                ent.clear(); idx.clear()
            elif sub is None:
                for x in idx.get(name_, ()):
                    ent.pop((name_, x), None)
                idx[name_] = set()
            ent[(name_, sub)] = [op, []]
            idx.setdefault(name_, set()).add(sub)
        for w in deps.values():
            if w is op:
                continue
            if (not w.is_dma) and (not op.is_dma) and w.eng == op.eng and op.eng == "pe":
                continue
            op.deps.append(w)
            w.signal = True
        s.ops.append(op)
        if final:
            op.signal = True
            s.final_ops.append(op)
        return op

    def emit(s):
        nc = s.nc
        with contextlib.ExitStack() as es:
            esems = {e: es.enter_context(nc.semaphore("s_" + e)) for e in ("pe", "act", "dve", "pool")}
            dpool = {q: [es.enter_context(nc.semaphore("d%s%d" % (q, i))) for i in range(s.n_dma_sems)]
                     for q in ("sp", "pool")}
            cnt = {e: 0 for e in esems}
            dcum = {q: [0] * s.n_dma_sems for q in dpool}
            dprev = {}
            kq = {q: 0 for q in dpool}
            for op in s.ops:
                if op.is_dma:
                    q = op.eng
                    i = kq[q] % s.n_dma_sems; kq[q] += 1
                    op.sem = dpool[q][i]; dprev[id(op)] = dcum[q][i]
                    dcum[q][i] += 16; op.target = dcum[q][i]
                elif op.signal:
                    cnt[op.eng] += 1
                    op.sem = esems[op.eng]; op.target = cnt[op.eng]
            by_eng = {e: [o for o in s.ops if o.eng == e] for e in ENGS}
            block = es.enter_context(nc.Block())

            def run(engname, eng):
                waited = {}

                def wait(sem, val):
                    if val <= 0:
                        return
                    key = id(sem)
                    if waited.get(key, 0) >= val:
                        return
                    eng.wait_ge(sem, val)
                    waited[key] = val

                for op in by_eng[engname]:
                    for w in op.deps:
                        wait(w.sem, w.target)
                    if op.is_dma:
                        wait(op.sem, dprev[id(op)])
                        op.fn(eng).then_inc(op.sem, 16)
                    else:
                        ins = op.fn(eng)
                        if op.signal:
                            ins.then_inc(op.sem, 1)
                for op in s.final_ops:
                    if op.eng == engname:
                        wait(op.sem, op.target)

            block.tensor(lambda e: run("pe", e))
            block.scalar(lambda e: run("act", e))
            block.vector(lambda e: run("dve", e))
            block.gpsimd(lambda e: run("pool", e))
            block.sync(lambda e: run("sp", e))


def build_program(probe=None):
    nc = bass.Bass("TRN2", target_bir_lowering=False)
    es = contextlib.ExitStack()
    P = Prog(nc)
    A = P.add

    def din(name, shape, dt=F32):
        return nc.dram_tensor(name, list(shape), dt, kind="ExternalInput").ap()

    def dout(name, shape, dt=F32):
        return nc.dram_tensor(name, list(shape), dt, kind="ExternalOutput").ap()

    START = 16512
    G0, R1_0, R2_0, END = START, START + 20480, START + 20480 + 71680, 229344
    ptr = {"G": G0, "R1": R1_0, "R2": R2_0}
    lim = {"G": R1_0, "R1": R2_0, "R2": END}
    cnt_names = [0]

    def sb(name, shape, dt=F32, reg="R2"):
        n = 1
        for d in shape[1:]:
            n *= d
        nbytes = n * (2 if dt == BF16 else 4)
        off = ptr[reg]
        ptr[reg] = off + (nbytes + 31) // 32 * 32
        assert ptr[reg] <= lim[reg], (name, reg, ptr[reg] - lim[reg])
        cnt_names[0] += 1
        if reg != "G":
            P.group_of[name] = "arena"
        return nc.alloc_sbuf_tensor_at("s%d_%s" % (cnt_names[0], name), list(shape), dt, offset=off)

    def new_phase(keep_r2=0):
        A("dve", lambda e: e.memset(bar_t[:], 0.0), [], ["*@arena", "bar_t"])
        ptr["R2"] = R2_0 + keep_r2

    def dma(q, out, in_, reads, writes, final=False):
        return A(q, lambda e: e.dma_start(out=out, in_=in_), reads, writes, dma=True, final=final)

    xp = din("xp", [S, D]); xsm = din("xs", [128, D])
    c17 = din("c17", [17, D]); vecs = din("vecs", [NVEC, 128]); bgate = din("bgate", [2, D])
    w_ada = din("w_ada", [D, 6 * D]); w_in = din("w_in", [D, INW])
    sshift = din("sshift", [DB, RIN])
    ident_d = din("ident", [128, 128]); rope_d = din("rope", [17, 128, 128])
    cmask_d = din("cmask", [128, 896])
    w2a2_d = din("w2a2", [128, 512]); g2_d = din("g2", [128, 512])
    o_kwin = dout("o_kwin", [S, 512]); o_vwin = dout("o_vwin", [S, 512])
    o_shp = dout("o_shp", [1, RIN]); o_knew = dout("o_knew", [128, 512]); o_vnew = dout("o_vnew", [128, 512])
    o_shs = dout("o_shs", [DB, RIN]); o_wkvp = dout("o_wkvp", [H, E, E])
    swkv = din("swkv", [128, E * E]); o_wkvs = dout("o_wkvs", [128, E * E])
    w_out = din("w_out", [D, D]); w_ff1 = din("w_ff1", [D, DFF]); w_ff2 = din("w_ff2", [DFF, D]); gfin = din("gfin", [D])
    cnt_d = din("cnt", [128, 2048]); esel_d = din("esel", [17, 256])
    ck = din("ck", [DB, 2048, 512]); cv = din("cv", [DB, 2048, 512])
    cnts_d = din("cnts", [128, 16 * T]); cntn_d = din("cntn", [128, DB * T])
    o_yp = dout("o_yp", [S, D]); o_ys = dout("o_ys", [128, D])
    scr_v = nc.dram_tensor("scr_v", [6, 128, 512], F32, kind="Internal").ap()
    scr_o = nc.dram_tensor("scr_o", [128, 512], F32, kind="Internal").ap()
    scr_acc = nc.dram_tensor("scr_acc", [65, 1024], F32, kind="Internal").ap()

    ps = [es.enter_context(nc.psum_tensor("ps%d" % i, [128, 512], F32)) for i in range(8)]

    bar_t = sb("bar_t", [128, 8], F32, "G")
    ident = sb("ident", [128, 128], F32, "G"); identb = sb("identb", [128, 128], BF16, "G")
    vecT = sb("vecT", [128, NVEC], F32, "G"); scT = sb("scT", [128, 8, 17], BF16, "G")
    modT = {n: sb("modT_" + n, [128, 8, 17], F32, "G") for n in ("A1", "B1", "A2", "B2")}
    cmask = sb("cmask", [128, 896], F32, "G"); ropet = sb("ropet", [128, 128], F32, "G")
    ss = sb("ss", [128, 1], F32, "G"); rstd = sb("rstd", [128, 1], F32, "G")
    xt = sb("xt", [128, D], F32, "G"); xsn = sb("xsn", [128, D], F32, "G"); hT = sb("hT", [128, 8, 128], BF16, "G")
    sshT = sb("sshT", [128, 14, DB], F32, "G")
    Esel = sb("Esel", [17, 256], F32, "G")
    blk = cmask[:, 0:128]; MT4 = cmask[:, 128:640]; MLs = cmask[:, 640:768]; ones_t = cmask[:, 768:896]
    QT = sb("QT", [128, 4, (NT + 1) * 128], BF16, "R1"); KT = sb("KT", [128, 4, (NT + 1) * 128], BF16, "R1")
    Vaug = sb("Vaug", [128, NT + 1, H, 65], BF16, "R1"); rwT = sb("rwT", [128, NT + 1, 4, 128], BF16, "R1")

    def vcol(c0, n=4, w=128):
        return vecT[:, c0:c0 + n].unsqueeze(2).broadcast_to([128, n, w])

    dma("sp", ident[:], ident_d, [], ["ident"])
    dma("sp", cmask[:], cmask_d, [], ["cmask"])
    dma("sp", Esel[:], esel_d, [], ["Esel"])
    A("dve", lambda e: e.tensor_copy(out=identb[:], in_=ident[:]), ["ident"], ["identb"])
    dma("sp", xt[0:NVEC, 0:128], vecs, [], ["xt"])
    A("pe", lambda e: e.transpose(out=ps[0][:, 0:NVEC], in_=xt[0:NVEC, 0:128], identity=ident[0:NVEC, 0:NVEC]),
      ["xt", "ident"], [("ps", 0)])
    A("dve", lambda e: e.tensor_copy(out=vecT[:], in_=ps[0][:, 0:NVEC]), [("ps", 0)], ["vecT"])
    c_sb = xsn[0:17, :]
    dma("sp", c_sb, c17, [], ["xsn"])
    A("act", lambda e: e.activation(out=c_sb, in_=c_sb, func=AF.Silu), ["xsn"], ["xsn"])
    for kc in range(8):
        A("pe", lambda e, kc=kc: e.transpose(out=ps[1][:, kc * 17:(kc + 1) * 17], in_=xsn[0:17, kc * 128:(kc + 1) * 128],
                                             identity=ident[0:17, 0:17]), ["xsn", "ident"], [("ps", 1)])
    A("dve", lambda e: e.tensor_copy(out=scT[:], in_=ps[1][:, 0:136].rearrange("p (k b) -> p k b", k=8)), [("ps", 1)], ["scT"])
    A("dve", lambda e: e.memset(Vaug[:, :, :, 64:65], 1.0), [], ["Vaug"])

    wada = [sb("wada%d" % i, [128, 8, D], BF16) for i in range(2)]
    ssh_sb = sb("ssh_sb", [DB, RIN])

    def ada_block(col0, buf):
        for half in range(2):
            A("pool", lambda e, half=half: e.dma_start(
                out=wada[buf][:, 4 * half:4 * half + 4, :],
                in_=w_ada[512 * half:512 * half + 512, col0:col0 + D].rearrange("(kc p) n -> p kc n", p=128)),
              [], ["wada%d" % buf], dma=True)

    def ada_fm(buf, bank, dst, bias_col, gain_col):
        wt = wada[buf]
        for c in range(8):
            for kc in range(8):
                A("pe", lambda e, c=c, kc=kc: e.matmul(ps[bank][:, c * 17:(c + 1) * 17], lhsT=wt[:, kc, c * 128:(c + 1) * 128],
                                                       rhs=scT[:, kc, :], start=(kc == 0), stop=(kc == 7)),
                  ["wada%d" % buf, "scT"], [("ps", bank)])
        pv = ps[bank][:, 0:136].rearrange("p (c b) -> p c b", c=8)
        dk = "modT_" + dst
        A("dve", lambda e: e.tensor_tensor(out=modT[dst][:], in0=pv, in1=vcol(bias_col, 8, 17), op=ALU.add), [("ps", bank), "vecT"], [dk])
        if gain_col is not None:
            A("dve", lambda e: e.scalar_tensor_tensor(out=modT[dst][:], in0=modT[dst][:], scalar=1.0, in1=vcol(gain_col, 8, 17),
                                                      op0=ALU.add, op1=ALU.mult), [dk, "vecT"], [dk])

    ada_block(1 * D, 0); ada_block(0 * D, 1)
    ada_fm(0, 2, "A1", V_BSC1, V_G1); ada_fm(1, 3, "B1", V_BSH1, None)
    ada_block(4 * D, 0); ada_block(3 * D, 1)
    ada_fm(0, 2, "A2", V_BSC2, V_G2); ada_fm(1, 3, "B2", V_BSH2, None)
    dma("sp", ssh_sb[:], sshift, [], ["ssh_sb"])
    for c in range(14):
        A("pe", lambda e, c=c: e.transpose(out=ps[4][:, c * 16:(c + 1) * 16], in_=ssh_sb[:, c * 128:(c + 1) * 128],
                                           identity=ident[0:16, 0:16]), ["ssh_sb", "ident"], [("ps", 4)])
    A("dve", lambda e: e.tensor_copy(out=sshT[:], in_=ps[4][:, 0:224].rearrange("p (c b) -> p c b", c=14)), [("ps", 4)], ["sshT"])

    BS0 = dict(xt=xt, xsn=xsn, hT=hT, ss=ss, rstd=rstd, sfx="")

    def rms_hT(ti, modA, modB, bs=None):
        bs = bs or BS0
        sample = (ti == NT)
        dma("sp", bs["xt"][:], xsm if sample else xp[ti * 128:(ti + 1) * 128, :], [], ["xt" + bs["sfx"]])
        rms_from(bs["xt"][:], "xt" + bs["sfx"], modA, modB, sample, bs)

    def rms_from(x_ap, xkey, modA, modB, sample, bs=None):
        bs = bs or BS0
        return _rms_from(x_ap, xkey, modA, modB, sample, bs["xsn"], bs["hT"], bs["ss"], bs["rstd"], bs["sfx"])

    def _rms_from(x_ap, xkey, modA, modB, sample, xsn, hT, ss, rstd, sfx):
        A("act", lambda e: e.activation(out=xsn[:], in_=x_ap, func=AF.Square, accum_out=ss[:]), [xkey], ["xsn" + sfx, "ss" + sfx])
        A("dve", lambda e: e.tensor_scalar(out=rstd[:], in0=ss[:], scalar1=1.0 / D, scalar2=1e-6, op0=ALU.mult, op1=ALU.add),
          ["ss" + sfx], ["rstd" + sfx])
        A("act", lambda e: e.activation(out=rstd[:], in_=rstd[:], func=AF.Sqrt), ["rstd" + sfx], ["rstd" + sfx])
        A("dve", lambda e: e.reciprocal(out=rstd[:], in_=rstd[:]), ["rstd" + sfx], ["rstd" + sfx])
        A("act", lambda e: e.activation(out=xsn[:], in_=x_ap, func=AF.Copy, scale=rstd[:]), [xkey, "rstd" + sfx], ["xsn" + sfx])
        for c in range(8):
            A("pe", lambda e, c=c: e.transpose(out=ps[c // 4][:, (c % 4) * 128:(c % 4 + 1) * 128], in_=xsn[:, c * 128:(c + 1) * 128],
                                               identity=ident[:]), ["xsn" + sfx, "ident"], [("ps", c // 4)])
        mA, mB = modT[modA], modT[modB]
        for g in range(2):
            if sample:
                pv = ps[g][:].rearrange("p (c b t) -> p c b t", c=4, b=DB)
                o = hT[:, 4 * g:4 * g + 4, :].rearrange("p c (b t) -> p c b t", b=DB)
                a_ = mA[:, 4 * g:4 * g + 4, 1:17].unsqueeze(3).broadcast_to([128, 4, DB, T])
                b_ = mB[:, 4 * g:4 * g + 4, 1:17].unsqueeze(3).broadcast_to([128, 4, DB, T])
                tmp = xsn[:, 512 * g:512 * g + 512].rearrange("p (c b t) -> p c b t", c=4, b=DB)
            else:
                pv = ps[g][:].rearrange("p (c t) -> p c t", c=4)
                o = hT[:, 4 * g:4 * g + 4, :]
                a_ = mA[:, 4 * g:4 * g + 4, 0:1].broadcast_to([128, 4, 128])
                b_ = mB[:, 4 * g:4 * g + 4, 0:1].broadcast_to([128, 4, 128])
                tmp = xsn[:, 512 * g:512 * g + 512].rearrange("p (c t) -> p c t", c=4)
            A("dve", lambda e, pv=pv, a_=a_, tmp=tmp: e.tensor_tensor(out=tmp, in0=pv, in1=a_, op=ALU.mult),
              [("ps", g), "modT_" + modA, "xsn" + sfx], ["xsn" + sfx])
            A("dve", lambda e, o=o, b_=b_, tmp=tmp: e.tensor_tensor(out=o, in0=tmp, in1=b_, op=ALU.add),
              ["xsn" + sfx, "modT_" + modB], [("hT" + sfx, g)])

    new_phase()
    winq = sb("winq", [128, 8, 1536], BF16)
    for kc in range(8):
        for c0 in (0, 768):
            A("pool", lambda e, kc=kc, c0=c0: e.dma_start(out=winq[:, kc, c0:c0 + 768], in_=w_in[kc * 128:(kc + 1) * 128, c0:c0 + 768]),
              [], [("winq", kc)], dma=True)
    xtB = sb("xt1", [128, D]); xsnB = sb("xsn1", [128, D]); hTB = sb("hT1", [128, 8, 128], BF16)
    ssB = sb("ss1", [128, 1]); rstdB = sb("rstd1", [128, 1])
    BS = [BS0, dict(xt=xtB, xsn=xsnB, hT=hTB, ss=ssB, rstd=rstdB, sfx="1")]
    QKV = [tuple(sb("%s%d" % (n, i), [128, 512]) for n in ("qs", "ks", "vs")) for i in range(2)]
    ropeB = [ropet, sb("ropet1", [128, 128])]
    rtmp = [sb("rtmp%d" % i, [128, H, 8]) for i in range(2)]

    def rope(bank, dst, dkey, rt, rtk):
        psb = ps[bank]
        A("act", lambda e: e.activation(out=dst[:], in_=psb[:], func=AF.Copy), [("ps", bank)], [dkey])
        pv = psb[:].rearrange("p (h e) -> p h e", h=H)
        dv = dst[:].rearrange("p (h e) -> p h e", h=H)
        cos = rt[:, 0:64].rearrange("p (h e) -> p h e", h=H)
        sin = rt[:, 64:128].rearrange("p (h e) -> p h e", h=H)
        t1 = pv[:, :, 0:8]; t2 = pv[:, :, 8:16]
        rk = [("ps", bank), rtk]
        A("dve", lambda e: e.tensor_tensor(out=rtmp[0][:], in0=t1, in1=cos, op=ALU.mult), rk, ["rtmp0"])
        A("dve", lambda e: e.tensor_tensor(out=rtmp[1][:], in0=t2, in1=sin, op=ALU.mult), rk, ["rtmp1"])
        A("dve", lambda e: e.tensor_tensor(out=dv[:, :, 0:8], in0=rtmp[0][:], in1=rtmp[1][:], op=ALU.subtract),
          ["rtmp0", "rtmp1", dkey], [dkey])
        A("dve", lambda e: e.tensor_tensor(out=rtmp[0][:], in0=t1, in1=sin, op=ALU.mult), rk, ["rtmp0"])
        A("dve", lambda e: e.tensor_tensor(out=rtmp[1][:], in0=t2, in1=cos, op=ALU.mult), rk, ["rtmp1"])
        A("dve", lambda e: e.tensor_tensor(out=dv[:, :, 8:16], in0=rtmp[0][:], in1=rtmp[1][:], op=ALU.add),
          ["rtmp0", "rtmp1", dkey], [dkey])

    def p1a_front(n_, ti):
        bs = BS[n_ % 2]
        dma("sp", ropeB[n_ % 2][:], rope_d[ti], [], ["ropet%d" % (n_ % 2)])
        rms_hT(ti, "A1", "B1", bs)

    def p1a_mm(n_, ti):
        bs = BS[n_ % 2]
        hT_ = bs["hT"]
        for g in range(3):
            for kc in range(8):
                A("pe", lambda e, g=g, kc=kc, hT_=hT_: e.matmul(ps[2 + g][:], lhsT=hT_[:, kc, :], rhs=winq[:, kc, 512 * g:512 * g + 512],
                                                                start=(kc == 0), stop=(kc == 7)), ["hT" + bs["sfx"], ("winq", kc)], [("ps", 2 + g)])

    def p1a_back(n_, ti):
        sample = (ti == NT)
        qs, ks, vs = QKV[n_ % 2]
        qk, kk_, vk = ("qs%d" % (n_ % 2), "ks%d" % (n_ % 2), "vs%d" % (n_ % 2))
        rope(2, qs, qk, ropeB[n_ % 2], "ropet%d" % (n_ % 2)); rope(3, ks, kk_, ropeB[n_ % 2], "ropet%d" % (n_ % 2))
        A("act", lambda e: e.activation(out=vs[:], in_=ps[4][:], func=AF.Copy), [("ps", 4)], [vk])
        if sample:
            dma("sp", o_knew, ks[:], [kk_], ["o_knew"], final=True)
            dma("sp", o_vnew, vs[:], [vk], ["o_vnew"], final=True)
        else:
            dma("sp", o_kwin[ti * 128:(ti + 1) * 128, :], ks[:], [kk_], [("o_kwin", ti)], final=True)
            dma("sp", o_vwin[ti * 128:(ti + 1) * 128, :], vs[:], [vk], [("o_vwin", ti)], final=True)
        for (src, skey, dst, dkey, bank, scl) in ((qs, qk, QT, "QT", 5, 0.125), (ks, kk_, KT, "KT", 6, 1.0)):
            for c in range(4):
                A("pe", lambda e, src=src, c=c, bank=bank: e.transpose(out=ps[bank][:, 128 * c:128 * c + 128], in_=src[:, 128 * c:128 * c + 128],
                                                                       identity=ident[:]), [skey, "ident"], [("ps", bank)])
            A("act", lambda e, dst=dst, bank=bank, scl=scl: e.activation(
                out=dst[:, :, ti * 128:(ti + 1) * 128], in_=ps[bank][:].rearrange("p (c t) -> p c t", c=4), func=AF.Copy, scale=scl),
              [("ps", bank)], [(dkey, ti)])
        A("act", lambda e: e.activation(out=Vaug[:, ti, :, 0:64], in_=vs[:].rearrange("p (h e) -> p h e", h=H), func=AF.Copy),
          [vk], [("Vaug", ti)])

    TL1A = [0, NT] if probe == "quick" else list(range(NT + 1))
    p1a_front(0, TL1A[0])
    for n_, ti in enumerate(TL1A):
        p1a_mm(n_, ti)
        if n_ + 1 < len(TL1A):
            p1a_front(n_ + 1, TL1A[n_ + 1])
        p1a_back(n_, ti)

    new_phase()
    Pm = sb("Pm", [128, 14, 128])

    def t4(name, dt=F32):
        return sb(name, [128, 4, 128], dt)

    gT = t4("gT"); bsum = t4("bsum"); onT = t4("onT")
    O32 = sb("O32", [128, H, 64]); Osq = sb("Osq", [128, H, 64]); gst = sb("gst", [128, 4, H])
    KEEP_1C = ptr["R2"] - R2_0
    winp = sb("winp", [128, 8, RIN], BF16)
    for kc in range(8):
        for c0 in (0, 896):
            A("pool", lambda e, kc=kc, c0=c0: e.dma_start(out=winp[:, kc, c0:c0 + 896], in_=w_in[kc * 128:(kc + 1) * 128, 1536 + c0:1536 + c0 + 896]),
              [], [("winp", kc)], dma=True)
    w2a2 = sb("w2a2", [128, 512], BF16); g2w = sb("g2w", [128, 512], BF16)
    A("pool", lambda e: e.dma_start(out=w2a2[:], in_=w2a2_d), [], ["w2a2"], dma=True)
    A("pool", lambda e: e.dma_start(out=g2w[:], in_=g2_d), [], ["g2w"], dma=True)
    PTx = sb("PTx", [128, 14, 129]); dtmp = sb("dtmp", [128, 14, 128])
    Ptok = dtmp[:].rearrange("p c t -> p (c t)")
    la = sb("la", [128, 128], BF16); sgl = sb("sgl", [128, 128], BF16)
    ld = t4("ld"); alpha = t4("alpha"); Lc = t4("Lc"); kxk = t4("kxk"); nrm = t4("nrm"); kk = t4("kk"); kmod = t4("kmod")
    bb = t4("bb"); tmp4 = t4("tmp4"); Wt = t4("Wt"); Winv = t4("Winv"); Wend = t4("Wend")
    Bh, Kh = kxk, nrm
    AR = sb("AR", [128, 4, 2, 128], BF16); Bt = t4("Bt", BF16); Kt = t4("Kt", BF16)
    Vtok = sb("Vtok", [128, 512], BF16); BhTok = sb("BhTok", [128, 512], BF16); KhTok = sb("KhTok", [128, 512], BF16)
    AX4 = sb("AX4", [128, 8, 4, 128], BF16)
    Lk = [sb("Lk%d" % i, [128, 4, 2, 128], BF16) for i in range(2)]
    MT32 = sb("MT32", [128, 4, 128]); MTb = sb("MTb", [128, 4, 128], BF16); NTb = sb("NTb", [128, 8, 128], BF16)
    St32 = sb("St32", [128, 4, 64]); Stb = sb("Stb", [128, 4, 64], BF16); WC = sb("WC", [128, 4])
    RHS32 = sb("RHS32", [128, 512]); RHSb = sb("RHSb", [128, 512], BF16); Ub = sb("Ub", [128, 512], BF16)
    A("dve", lambda e: e.memset(PTx[:, :, 0:1], 0.0), [], [("PTx", "carry")])
    A("dve", lambda e: e.memset(St32[:], 0.0), [], ["St32"])
    A("dve", lambda e: e.memset(Stb[:], 0.0), [], ["Stb"])

    def ptok_from_PTx():
        for c in range(14):
            A("pe", lambda e, c=c: e.transpose(out=ps[1][:, (c % 4) * 128:(c % 4 + 1) * 128], in_=PTx[:, c, 1:129], identity=ident[:]),
              [("PTx", "cur"), "ident"], [("ps", 1)])
            if c % 4 == 3 or c == 13:
                c0 = (c // 4) * 4
                n = c - c0 + 1
                A("act", lambda e, c0=c0, n=n: e.activation(out=Ptok[:, c0 * 128:(c0 + n) * 128], in_=ps[1][:, 0:128 * n], func=AF.Copy),
                  [("ps", 1)], ["dtmp"])

    def phase1b_proj(ti):
        sample = (ti == NT)
        rms_hT(ti, "A1", "B1")
        for g in range(4):
            bank = (2, 0, 1, 2)[g]
            n = 4 if g < 3 else 2
            for c in range(4 * g, 4 * g + n):
                for kc in range(8):
                    A("pe", lambda e, c=c, kc=kc, bank=bank: e.matmul(
                        ps[bank][:, (c % 4) * 128:(c % 4 + 1) * 128], lhsT=winp[:, kc, c * 128:(c + 1) * 128],
                        rhs=hT[:, kc, :], start=(kc == 0), stop=(kc == 7)), ["hT", ("winp", kc)], [("ps", bank)])
            A("act", lambda e, g=g, bank=bank, n=n: e.activation(
                out=PTx[:, 4 * g:4 * g + n, 1:129], in_=ps[bank][:, 0:128 * n].rearrange("p (c t) -> p c t", c=n), func=AF.Copy),
              [("ps", bank)], [("PTx", "cur")])
        if sample or ti == NT - 1:
            ptok_from_PTx()
            if sample:
                dma("sp", o_shs, Ptok[T - 1:128:T, :], ["dtmp"], ["o_shs"], final=True)
            else:
                dma("sp", o_shp, Ptok[127:128, :], ["dtmp"], ["o_shp"], final=True)

    def rwkv_prep(ti):
        sample = (ti == NT)
        Pcur = PTx[:, :, 1:129]
        if sample:
            d4 = dtmp[:].rearrange("p c (b t) -> p c b t", b=DB); c4 = Pcur.rearrange("p c (b t) -> p c b t", b=DB)
            A("dve", lambda e: e.tensor_tensor(out=d4[:, :, :, 1:T], in0=c4[:, :, :, 0:T - 1], in1=c4[:, :, :, 1:T], op=ALU.subtract),
              [("PTx", "cur")], ["dtmp"])
            A("dve", lambda e: e.tensor_tensor(out=d4[:, :, :, 0:1], in0=sshT[:].unsqueeze(3), in1=c4[:, :, :, 0:1], op=ALU.subtract),
              [("PTx", "cur"), "sshT", "dtmp"], ["dtmp"])
        else:
            A("dve", lambda e: e.tensor_tensor(out=dtmp[:], in0=PTx[:, :, 0:128], in1=Pcur, op=ALU.subtract),
              [("PTx", "cur"), ("PTx", "carry")], ["dtmp"])
        A("dve", lambda e: e.tensor_tensor(out=dtmp[:], in0=dtmp[:], in1=vcol(V_MU, 14), op=ALU.mult), ["dtmp", "vecT"], ["dtmp"])
        A("dve", lambda e: e.tensor_tensor(out=Pm[:], in0=dtmp[:], in1=Pcur, op=ALU.add), ["dtmp", ("PTx", "cur")], ["Pm"])
        if ti < NT - 1:
            A("dve", lambda e: e.tensor_copy(out=PTx[:, :, 0:1], in_=PTx[:, :, 128:129]), [("PTx", "cur")], [("PTx", "carry")])
        A("act", lambda e: e.activation(out=la[0:64, :], in_=Pm[0:64, 12, :], func=AF.Tanh), ["Pm"], [("la", 0)])
        A("act", lambda e: e.activation(out=la[64:128, :], in_=Pm[64:128, 12, :], func=AF.Copy), ["Pm"], [("la", 1)])
        A("act", lambda e: e.activation(out=sgl[:], in_=Pm[:, 13, :], func=AF.Sigmoid), ["Pm"], ["sgl"])
        for c in range(4):
            cs = slice(128 * c, 128 * c + 128)
            A("pe", lambda e, cs=cs: e.matmul(ps[0][:, cs], lhsT=w2a2[0:64, cs], rhs=la[0:64, :], start=True, stop=True),
              ["w2a2", ("la", 0)], [("ps", 0)])
            A("pe", lambda e, cs=cs: e.matmul(ps[1][:, cs], lhsT=w2a2[64:128, cs], rhs=la[64:128, :], start=True, stop=True),
              ["w2a2", ("la", 1)], [("ps", 1)])
            A("pe", lambda e, cs=cs: e.matmul(ps[2][:, cs], lhsT=g2w[:, cs], rhs=sgl[:], start=True, stop=True), ["g2w", "sgl"], [("ps", 2)])
        for c in range(4):
            cs = slice(128 * c, 128 * c + 128)
            A("act", lambda e, c=c, cs=cs: e.activation(out=ld[:, c, :], in_=ps[0][:, cs], func=AF.Sigmoid, bias=vecT[:, V_W0 + c:V_W0 + c + 1]),
              [("ps", 0), "vecT"], [("ld", c)])
            A("act", lambda e, c=c, cs=cs: e.activation(out=alpha[:, c, :], in_=ps[1][:, cs], func=AF.Sigmoid, bias=vecT[:, V_A0 + c:V_A0 + c + 1]),
              [("ps", 1), "vecT"], [("alpha", c)])
        A("act", lambda e: e.activation(out=gT[:], in_=ps[2][:].rearrange("p (c t) -> p c t", c=4), func=AF.Copy), [("ps", 2)], ["gT"])
        A("dve", lambda e: e.tensor_scalar(out=ld[:], in0=ld[:], scalar1=-0.6065306597126334, scalar2=None, op0=ALU.mult), ["ld"], ["ld"])
        kc_ = Pm[:, 4:8, :]
        A("dve", lambda e: e.tensor_tensor(out=kxk[:], in0=kc_, in1=vcol(V_KK), op=ALU.mult), ["Pm", "vecT"], ["kxk"])
        A("act", lambda e: e.activation(out=nrm[:], in_=kxk[:], func=AF.Square), ["kxk"], ["nrm"])
        A("pe", lambda e: e.matmul(ps[3][:], lhsT=blk, rhs=nrm[:].rearrange("p c t -> p (c t)"), start=True, stop=True),
          ["cmask", "nrm"], [("ps", 3)])
        A("act", lambda e: e.activation(out=nrm[:], in_=ps[3][:].rearrange("p (c t) -> p c t", c=4), func=AF.Sqrt), [("ps", 3)], ["nrm"])
        A("dve", lambda e: e.tensor_scalar(out=nrm[:], in0=nrm[:], scalar1=1e-12, scalar2=None, op0=ALU.max), ["nrm"], ["nrm"])
        A("dve", lambda e: e.reciprocal(out=nrm[:], in_=nrm[:]), ["nrm"], ["nrm"])
        A("dve", lambda e: e.tensor_tensor(out=kk[:], in0=kxk[:], in1=nrm[:], op=ALU.mult), ["kxk", "nrm"], ["kk"])
        A("dve", lambda e: e.scalar_tensor_tensor(out=tmp4[:], in0=alpha[:], scalar=-1.0, in1=vcol(V_KA), op0=ALU.add, op1=ALU.mult),
          ["alpha", "vecT"], ["tmp4"])
        A("dve", lambda e: e.scalar_tensor_tensor(out=kmod[:], in0=tmp4[:], scalar=1.0, in1=kc_, op0=ALU.add, op1=ALU.mult),
          ["tmp4", "Pm"], ["kmod"])
        A("dve", lambda e: e.tensor_tensor(out=bb[:], in0=kk[:], in1=alpha[:], op=ALU.mult), ["kk", "alpha"], ["bb"])
        A("dve", lambda e: e.tensor_tensor(out=tmp4[:], in0=Pm[:, 0:4, :], in1=kmod[:], op=ALU.mult), ["Pm", "kmod"], ["tmp4"])
        A("dve", lambda e: e.tensor_tensor(out=tmp4[:], in0=tmp4[:], in1=vcol(V_RK), op=ALU.mult), ["tmp4", "vecT"], ["tmp4"])
        A("pe", lambda e: e.matmul(ps[3][:], lhsT=blk, rhs=tmp4[:].rearrange("p c t -> p (c t)"), start=True, stop=True),
          ["cmask", "tmp4"], [("ps", 3)])
        A("act", lambda e: e.activation(out=bsum[:], in_=ps[3][:].rearrange("p (c t) -> p c t", c=4), func=AF.Copy), [("ps", 3)], ["bsum"])

    def rwkv_chunk(ti):
        for c in range(4):
            A("dve", lambda e, c=c: e.tensor_tensor_scan(out=Lc[:, c, :], data0=ones_t, data1=ld[:, c, :], initial=0.0,
                                                         op0=ALU.mult, op1=ALU.add), ["ld", "cmask"], [("Lc", c)])
        A("act", lambda e: e.activation(out=Wt[:], in_=Lc[:], func=AF.Exp), ["Lc"], ["Wt"])
        A("act", lambda e: e.activation(out=Winv[:], in_=Lc[:], func=AF.Exp, scale=-1.0), ["Lc"], ["Winv"])
        A("dve", lambda e: e.tensor_tensor(out=tmp4[:], in0=Lc[:], in1=ld[:], op=ALU.subtract), ["Lc", "ld"], ["tmp4"])
        A("act", lambda e: e.activation(out=tmp4[:], in_=tmp4[:], func=AF.Exp), ["tmp4"], ["tmp4"])
        for c in range(4):
            A("act", lambda e, c=c: e.activation(out=Wend[:, c, :], in_=Lc[:, c, :], func=AF.Exp, scale=-1.0, bias=Lc[:, c, 127:128]),
              ["Lc"], [("Wend", c)])
        A("act", lambda e: e.activation(out=WC[:].unsqueeze(2), in_=Lc[:, :, 127:128], func=AF.Exp), ["Lc"], ["WC"])
        A("dve", lambda e: e.tensor_tensor(out=St32[:], in0=St32[:], in1=WC[:].unsqueeze(2).broadcast_to([128, 4, 64]), op=ALU.mult),
          ["St32", "WC"], ["St32"])
        A("dve", lambda e: e.scalar_tensor_tensor(out=AR[:, :, 0, :], in0=kk[:], scalar=-1.0, in1=tmp4[:], op0=ALU.mult, op1=ALU.mult),
          ["kk", "tmp4"], [("AR", 0)])
        A("dve", lambda e: e.tensor_tensor(out=AR[:, :, 1, :], in0=Pm[:, 0:4, :], in1=Wt[:], op=ALU.mult), ["Pm", "Wt"], [("AR", 1)])
        A("dve", lambda e: e.tensor_tensor(out=Bt[:], in0=bb[:], in1=Winv[:], op=ALU.mult), ["bb", "Winv"], ["Bt"])
        A("dve", lambda e: e.tensor_tensor(out=Kt[:], in0=kmod[:], in1=Winv[:], op=ALU.mult), ["kmod", "Winv"], ["Kt"])
        A("dve", lambda e: e.tensor_tensor(out=Bh[:], in0=bb[:], in1=Wend[:], op=ALU.mult), ["bb", "Wend"], ["kxk"])
        A("dve", lambda e: e.tensor_tensor(out=Kh[:], in0=kmod[:], in1=Wend[:], op=ALU.mult), ["kmod", "Wend"], ["nrm"])
        for (src, skey, dst, dkey) in ((Pm[:, 8:12, :], "Pm", Vtok, "Vtok"), (Bh[:], "kxk", BhTok, "BhTok"), (Kh[:], "nrm", KhTok, "KhTok")):
            for c in range(4):
                A("pe", lambda e, src=src, c=c: e.transpose(out=ps[3][:, 128 * c:128 * c + 128], in_=src[:, c, :], identity=ident[:]),
                  [skey, "ident"], [("ps", 3)])
            A("act", lambda e, dst=dst: e.activation(out=dst[:], in_=ps[3][:], func=AF.Copy), [("ps", 3)], [dkey])
        identb4 = ident[:].unsqueeze(1).broadcast_to([128, 4, 128])
        for g in range(2):
            for hl in range(4):
                h = 4 * g + hl
                hp, hh = h // 2, h % 2
                pr = slice(64 * hh, 64 * hh + 64)
                arv = AR[pr, hp, :, :].rearrange("p a t -> p (a t)")
                ab = 4 if hh == 0 else 6
                lb = 5 if hh == 0 else 7
                lc = slice(128 * (hl // 2), 128 * (hl // 2) + 128)
                A("pe", lambda e, pr=pr, hp=hp, arv=arv, ab=ab: e.matmul(ps[ab][:, 0:256], lhsT=Bt[pr, hp, :], rhs=arv, start=True, stop=True),
                  ["Bt", "AR"], [("ps", ab)])
                A("pe", lambda e, pr=pr, hp=hp, arv=arv, ab=ab: e.matmul(ps[ab][:, 256:512], lhsT=Kt[pr, hp, :], rhs=arv, start=True, stop=True),
                  ["Kt", "AR"], [("ps", ab)])
                A("pe", lambda e, pr=pr, hp=hp, lb=lb, lc=lc: e.matmul(ps[lb][:, lc], lhsT=AR[pr, hp, 0, :], rhs=Bt[pr, hp, :], start=True, stop=True),
                  ["Bt", "AR"], [("ps", lb)])
                A("dve", lambda e, h=h, ab=ab: e.tensor_tensor(out=AX4[:, h, :, :].rearrange("p a t -> p (a t)"), in0=ps[ab][:], in1=MT4, op=ALU.mult),
                  [("ps", ab), "cmask"], [("AX4", h)])
            for hh in range(2):
                lb = 5 if hh == 0 else 7
                A("dve", lambda e, hh=hh, lb=lb: e.tensor_tensor(out=Lk[0][:, hh::2, 0, :], in0=ps[lb][:, 0:256].rearrange("p (a t) -> p a t", a=2),
                                                                 in1=MLs.unsqueeze(1).broadcast_to([128, 2, 128]), op=ALU.mult),
                  [("ps", lb), "cmask"], [("Lk0", ("L", hh))])
            axg = AX4[:, 4 * g:4 * g + 4, 0, :]
            axk = [("AX4", 4 * g + i) for i in range(4)]
            A("act", lambda e, axg=axg: e.activation(out=Lk[0][:, :, 1, :], in_=axg, func=AF.Copy), axk, [("Lk0", "T")])
            A("dve", lambda e, axg=axg: e.tensor_tensor(out=MTb[:], in0=axg, in1=identb4, op=ALU.add), axk + ["ident"], ["MTb"])
            for lv in range(6):
                a_, b_ = Lk[lv % 2], Lk[(lv + 1) % 2]
                ak, bk = "Lk%d" % (lv % 2), "Lk%d" % ((lv + 1) % 2)
                for hl in range(4):
                    bank = 4 if hl < 2 else 6
                    c0 = 256 * (hl % 2)
                    A("pe", lambda e, a_=a_, hl=hl, bank=bank, c0=c0: e.matmul(ps[bank][:, c0:c0 + 128], lhsT=a_[:, hl, 1, :], rhs=a_[:, hl, 0, :],
                                                                               start=True, stop=True), [ak], [("ps", bank)])
                    A("pe", lambda e, a_=a_, hl=hl, bank=bank, c0=c0: e.matmul(ps[bank][:, c0 + 128:c0 + 256], lhsT=a_[:, hl, 0, :], rhs=a_[:, hl, 1, :],
                                                                               start=True, stop=True), [ak], [("ps", bank)])
                A("act", lambda e, b_=b_: e.activation(out=b_[:, 0:2, :, :].rearrange("p h a t -> p (h a t)"), in_=ps[4][:], func=AF.Copy),
                  [("ps", 4)], [(bk, 0)])
                A("dve", lambda e, b_=b_: e.tensor_copy(out=b_[:, 2:4, :, :].rearrange("p h a t -> p (h a t)"), in_=ps[6][:]),
                  [("ps", 6)], [(bk, 1)])
                for hl in range(4):
                    A("pe", lambda e, b_=b_, hl=hl: e.matmul(ps[5][:, 128 * hl:128 * hl + 128], lhsT=b_[:, hl, 0, :], rhs=MTb[:, hl, :],
                                                             start=True, stop=True), [(bk, hl // 2), "MTb"], [("ps", 5)])
                A("dve", lambda e: e.tensor_tensor(out=MTb[:], in0=ps[5][:].rearrange("p (h t) -> p h t", h=4), in1=MTb[:], op=ALU.add),
                  [("ps", 5), "MTb"], ["MTb"])
            A("dve", lambda e, g=g: e.tensor_tensor(out=NTb[:, 4 * g:4 * g + 4, :], in0=MTb[:], in1=identb4, op=ALU.subtract),
              ["MTb", "ident"], [("NTb", g)])
        for h in range(H):
            hp, hh = h // 2, h % 2
            pr = slice(64 * hh, 64 * hh + 64); hs = slice(64 * h, 64 * h + 64)
            A("pe", lambda e, pr=pr, hp=hp, hs=hs: e.matmul(ps[7][:, hs], lhsT=AR[pr, hp, 0, :], rhs=Stb[pr, hp, :], start=True, stop=False),
              ["AR", "Stb"], [("ps", 7)])
            A("pe", lambda e, h=h, hs=hs: e.matmul(ps[7][:, hs], lhsT=AX4[:, h, 2, :], rhs=Vtok[:, hs], start=False, stop=True),
              [("AX4", h), "Vtok"], [("ps", 7)])
        A("act", lambda e: e.activation(out=RHSb[:], in_=ps[7][:], func=AF.Copy), [("ps", 7)], ["RHSb"])
        A("dve", lambda e: e.tensor_copy(out=RHS32[:], in_=ps[7][:]), [("ps", 7)], ["RHS32"])
        for h in range(H):
            hs = slice(64 * h, 64 * h + 64)
            A("pe", lambda e, h=h, hs=hs: e.matmul(ps[6][:, hs], lhsT=NTb[:, h, :], rhs=RHSb[:, hs], start=True, stop=True),
              [("NTb", h // 4), "RHSb"], [("ps", 6)])
        A("dve", lambda e: e.tensor_tensor(out=Ub[:], in0=ps[6][:], in1=RHS32[:], op=ALU.add), [("ps", 6), "RHS32"], ["Ub"])
        for h in range(H):
            hp, hh = h // 2, h % 2
            pr = slice(64 * hh, 64 * hh + 64); hs = slice(64 * h, 64 * h + 64)
            A("pe", lambda e, pr=pr, hp=hp, hs=hs: e.matmul(ps[7][:, hs], lhsT=AR[pr, hp, 1, :], rhs=Stb[pr, hp, :], start=True, stop=False),
              ["AR", "Stb"], [("ps", 7)])
            A("pe", lambda e, h=h, hs=hs: e.matmul(ps[7][:, hs], lhsT=AX4[:, h, 1, :], rhs=Ub[:, hs], start=False, stop=False),
              [("AX4", h), "Ub"], [("ps", 7)])
            A("pe", lambda e, h=h, hs=hs: e.matmul(ps[7][:, hs], lhsT=AX4[:, h, 3, :], rhs=Vtok[:, hs], start=False, stop=True),
              [("AX4", h), "Vtok"], [("ps", 7)])
        A("act", lambda e: e.activation(out=O32[:].rearrange("p h i -> p (h i)"), in_=ps[7][:], func=AF.Copy), [("ps", 7)], ["O32"])
        for h in range(H):
            hp, hh = h // 2, h % 2
            pr = slice(64 * hh, 64 * hh + 64); hs = slice(64 * h, 64 * h + 64)
            A("pe", lambda e, pr=pr, hp=hp, hs=hs: e.matmul(ps[5][pr, 256 + 64 * hp:256 + 64 * hp + 64], lhsT=BhTok[:, hs], rhs=Ub[:, hs],
                                                            start=True, stop=False), ["BhTok", "Ub"], [("ps", 5)])
            A("pe", lambda e, pr=pr, hp=hp, hs=hs: e.matmul(ps[5][pr, 256 + 64 * hp:256 + 64 * hp + 64], lhsT=KhTok[:, hs], rhs=Vtok[:, hs],
                                                            start=False, stop=True), ["KhTok", "Vtok"], [("ps", 5)])
        A("dve", lambda e: e.tensor_tensor(out=St32[:], in0=St32[:], in1=ps[5][:, 256:512].rearrange("p (c i) -> p c i", c=4), op=ALU.add),
          ["St32", ("ps", 5)], ["St32"])
        A("act", lambda e: e.activation(out=Stb[:], in_=St32[:], func=AF.Copy), ["St32"], ["Stb"])

    def wkv_prompt_out():
        stv = St32[:].rearrange("p c i -> p (c i)")
        for q in range(2):
            A("pe", lambda e, q=q: e.transpose(out=ps[3][:, 128 * q:128 * q + 128], in_=stv[:, 128 * q:128 * q + 128], identity=ident[:]),
              ["St32", "ident"], [("ps", 3)])
        A("act", lambda e: e.activation(out=RHS32[:, 0:256], in_=ps[3][:, 0:256], func=AF.Copy), [("ps", 3)], ["RHS32"])
        for q in range(2):
            for hpp in range(2):
                hp = 2 * q + hpp
                dst = o_wkvp[2 * hp:2 * hp + 2, :, :].rearrange("hh i j -> i hh j")
                src = RHS32[64 * hpp:64 * hpp + 64, 128 * q:128 * q + 128].rearrange("p (hh j) -> p hh j", hh=2)
                dma("sp", dst, src, ["RHS32"], [("o_wkvp", hp)], final=True)


    def rwkv_out(ti):
        A("dve", lambda e: e.tensor_reduce(out=gst[:, 0, :], in_=O32[:], axis=AX.X, op=ALU.add), ["O32"], [("gst", 0)])
        A("act", lambda e: e.activation(out=Osq[:], in_=O32[:], func=AF.Square), ["O32"], ["Osq"])
        A("dve", lambda e: e.tensor_reduce(out=gst[:, 1, :], in_=Osq[:], axis=AX.X, op=ALU.add), ["Osq"], [("gst", 1)])
        A("dve", lambda e: e.tensor_scalar(out=gst[:, 0, :], in0=gst[:, 0, :], scalar1=1.0 / E, scalar2=None, op0=ALU.mult), [("gst", 0)], [("gst", 0)])
        A("dve", lambda e: e.tensor_tensor(out=gst[:, 2, :], in0=gst[:, 0, :], in1=gst[:, 0, :], op=ALU.mult), [("gst", 0)], [("gst", 2)])
        A("dve", lambda e: e.scalar_tensor_tensor(out=gst[:, 3, :], in0=gst[:, 1, :], scalar=1.0 / E, in1=gst[:, 2, :], op0=ALU.mult, op1=ALU.subtract),
          [("gst", 1), ("gst", 2)], [("gst", 3)])
        A("dve", lambda e: e.tensor_scalar(out=gst[:, 3, :], in0=gst[:, 3, :], scalar1=64e-5, scalar2=None, op0=ALU.add), [("gst", 3)], [("gst", 3)])
        A("act", lambda e: e.activation(out=gst[:, 3, :], in_=gst[:, 3, :], func=AF.Sqrt), [("gst", 3)], [("gst", 3)])
        A("dve", lambda e: e.reciprocal(out=gst[:, 3, :], in_=gst[:, 3, :]), [("gst", 3)], [("gst", 3)])
        mb = gst[:, 0, :].unsqueeze(2).broadcast_to([128, H, E]); rb = gst[:, 3, :].unsqueeze(2).broadcast_to([128, H, E])
        A("dve", lambda e: e.tensor_tensor(out=Osq[:], in0=O32[:], in1=mb, op=ALU.subtract), ["O32", ("gst", 0)], ["Osq"])
        A("dve", lambda e: e.tensor_tensor(out=Osq[:], in0=Osq[:], in1=rb, op=ALU.mult), ["Osq", ("gst", 3)], ["Osq"])
        ov = Osq[:].rearrange("p h i -> p (h i)")
        for c in range(4):
            A("pe", lambda e, c=c: e.transpose(out=ps[3][:, 128 * c:128 * c + 128], in_=ov[:, 128 * c:128 * c + 128], identity=ident[:]),
              ["Osq", "ident"], [("ps", 3)])
        A("dve", lambda e: e.tensor_tensor(out=onT[:], in0=ps[3][:].rearrange("p (c t) -> p c t", c=4), in1=vcol(V_LG), op=ALU.mult),
          [("ps", 3), "vecT"], ["onT"])
        A("dve", lambda e: e.tensor_tensor(out=onT[:], in0=onT[:], in1=vcol(V_LB), op=ALU.add), ["onT", "vecT"], ["onT"])
        A("dve", lambda e: e.tensor_tensor(out=bsum[:], in0=bsum[:], in1=Pm[:, 8:12, :], op=ALU.mult), ["bsum", "Pm"], ["bsum"])
        A("dve", lambda e: e.tensor_tensor(out=onT[:], in0=onT[:], in1=bsum[:], op=ALU.add), ["onT", "bsum"], ["onT"])
        A("dve", lambda e: e.tensor_tensor(out=rwT[:, ti, :, :], in0=onT[:], in1=gT[:], op=ALU.mult), ["onT", "gT"], [("rwT", ti)])

    def sample_vectors_out():
        A("act", lambda e: e.activation(out=Wt[:], in_=ld[:], func=AF.Exp), ["ld"], ["Wt"])
        A("dve", lambda e: e.tensor_scalar(out=Winv[:], in0=kk[:], scalar1=-1.0, scalar2=None, op0=ALU.mult), ["kk"], ["Winv"])
        srcs = ((Pm[:, 0:4, :], "Pm"), (Wt[:], "Wt"), (kmod[:], "kmod"), (Pm[:, 8:12, :], "Pm"), (Winv[:], "Winv"), (bb[:], "bb"))
        for v, (src, skey) in enumerate(srcs):
            for c in range(4):
                A("pe", lambda e, src=src, c=c: e.transpose(out=ps[3][:, 128 * c:128 * c + 128], in_=src[:, c, :], identity=ident[:]),
                  [skey, "ident"], [("ps", 3)])
            A("act", lambda e: e.activation(out=RHS32[:], in_=ps[3][:], func=AF.Copy), [("ps", 3)], ["RHS32"])
            dma("sp", scr_v[v], RHS32[:], ["RHS32"], [("scr_v", v)])

    def capture(f_):
        P._cap = []
        f_()
        lst, P._cap = P._cap, None
        return lst

    def interleave(main, side):
        j = 0
        for k_, a_ in enumerate(main):
            P.add(*a_)
            want = (len(side) * (k_ + 1)) // max(len(main), 1)
            while j < want:
                P.add(*side[j]); j += 1
        while j < len(side):
            P.add(*side[j]); j += 1

    TL1B = [0, NT - 1, NT] if probe == "quick" else list(range(NT + 1))
    phase1b_proj(TL1B[0])
    for n_, ti in enumerate(TL1B):
        rwkv_prep(ti)
        nxt = capture(lambda: phase1b_proj(TL1B[n_ + 1])) if n_ + 1 < len(TL1B) else []
        if ti < NT:
            def stage(ti=ti):
                rwkv_chunk(ti)
                rwkv_out(ti)
                if ti == NT - 1:
                    wkv_prompt_out()
            interleave(capture(stage), nxt)
        else:
            interleave([], nxt)
            sample_vectors_out()

    new_phase(KEEP_1C)
    Ss = sb("Ss", [128, E, E]); tmpS = sb("tmpS", [128, E, E]); vkS = sb("vkS", [128, E, E])
    vec6 = sb("vec6", [128, 6, T, E]); sa = sb("sa", [128, E]); outs = sb("outs", [128, T, E])
    dma("sp", Ss[:].rearrange("p i j -> p (i j)"), swkv, [], ["Ss"])
    for b in range(DB):
        for v in range(6):
            dma("sp", vec6[8 * b:8 * b + 8, v, :, :], scr_v[v, 8 * b:8 * b + 8, :].rearrange("t (h j) -> h t j", h=H),
                [("scr_v", v)], [("vec6", (b, v))])
    bi = lambda ap: ap.unsqueeze(1).broadcast_to([128, E, E])
    bj = lambda ap: ap.unsqueeze(2).broadcast_to([128, E, E])
    rec = []
    NQ = 4; QI = E // NQ
    bq = lambda ap: ap.unsqueeze(1).broadcast_to([128, QI, E])
    for t in range(T):
        r_, w_, k_, v_, nk_, ka_ = (vec6[:, v, t, :] for v in range(6))
        rec.append(lambda v_=v_, k_=k_: A("pool", lambda e: e.tensor_tensor(out=vkS[:], in0=bj(v_), in1=bi(k_), op=ALU.mult), ["vec6"], ["vkS"]))
        def q_ops(kind, t=t, r_=r_, w_=w_, nk_=nk_, ka_=ka_):
            for q in range(NQ):
                isl = slice(QI * q, QI * q + QI)
                S_, T_ = Ss[:, isl, :], tmpS[:, isl, :]
                sk, tk, ak = ("Ss", q), ("tmpS", q), ("sa", q)
                if kind == 0:
                    f = lambda S_=S_, T_=T_, sk=sk, tk=tk: A("dve", lambda e: e.tensor_tensor(out=T_, in0=S_, in1=bq(nk_), op=ALU.mult), [sk, "vec6"], [tk])
                elif kind == 1:
                    f = lambda T_=T_, isl=isl, tk=tk, ak=ak: A("dve", lambda e: e.tensor_reduce(out=sa[:, isl], in_=T_, axis=AX.X, op=ALU.add), [tk], [ak])
                elif kind == 2:
                    f = lambda S_=S_, sk=sk: A("dve", lambda e: e.tensor_tensor(out=S_, in0=S_, in1=bq(w_), op=ALU.mult), [sk, "vec6"], [sk])
                elif kind == 3:
                    f = lambda T_=T_, isl=isl, tk=tk, ak=ak: A("dve", lambda e: e.tensor_tensor(
                        out=T_, in0=sa[:, isl].unsqueeze(2).broadcast_to([128, QI, E]), in1=bq(ka_), op=ALU.mult), [ak, "vec6"], [tk])
                elif kind == 4:
                    f = lambda S_=S_, T_=T_, sk=sk, tk=tk: A("dve", lambda e: e.tensor_tensor(out=S_, in0=S_, in1=T_, op=ALU.add), [sk, tk], [sk])
                elif kind == 5:
                    f = lambda S_=S_, isl=isl, sk=sk: A("dve", lambda e: e.tensor_tensor(out=S_, in0=S_, in1=vkS[:, isl, :], op=ALU.add), [sk, "vkS"], [sk])
                elif kind == 6:
                    f = lambda S_=S_, T_=T_, sk=sk, tk=tk: A("dve", lambda e: e.tensor_tensor(out=T_, in0=S_, in1=bq(r_), op=ALU.mult), [sk, "vec6"], [tk])
                else:
                    f = lambda T_=T_, isl=isl, tk=tk, q=q: A("dve", lambda e: e.tensor_reduce(out=outs[:, t, isl], in_=T_, axis=AX.X, op=ALU.add),
                                                             [tk], [("outs", (t, q))])
                rec.append(f)
        for kind in range(8):
            q_ops(kind)

    units = []
    if SAMPLE_ATTN_DONE:
        NLB = 4
        kcf = [sb("kc_f%d" % i, [128, 512]) for i in range(NLB)]; vcf = [sb("vc_f%d" % i, [128, 512]) for i in range(NLB)]
        NU = 4
        KTcs = [sb("KTc%d" % i, [128, 4, 128], BF16) for i in range(NU)]
        Vcs = [sb("Vc%d" % i, [128, H, 65], BF16) for i in range(NU)]
        PTss = [sb("PTs%d" % i, [128, H, T], BF16) for i in range(NU)]
        QTbd = sb("QTbd", [128, 4, DB, 2 * T], BF16)
        A("dve", lambda e: e.memset(QTbd[:], 0.0), [], ["QTbd"])
        qsv = lambda pr: QT[pr, :, NT * 128:NT * 128 + 128].rearrange("p c (b q) -> p c b q", b=DB)
        A("act", lambda e: e.activation(out=QTbd[0:64, :, :, 0:T], in_=qsv(slice(0, 64)), func=AF.Copy), [("QT", NT), "QTbd"], ["QTbd"])
        A("act", lambda e: e.activation(out=QTbd[64:128, :, :, T:2 * T], in_=qsv(slice(64, 128)), func=AF.Copy), [("QT", NT), "QTbd"], ["QTbd"])
        acc = sb("acc", [128, H, 128]); cnts = sb("cnts", [128, 16, T]); cntn = sb("cntn", [128, DB, T])
        dma("sp", cnts[:].rearrange("p a q -> p (a q)"), cnts_d, [], ["cnts"])
        dma("sp", cntn[:].rearrange("p a q -> p (a q)"), cntn_d, [], ["cntn"])
        for i in range(NU):
            A("dve", lambda e, i=i: e.memset(Vcs[i][:, :, 64:65], 1.0), [], ["Vc%d" % i])
        A("dve", lambda e: e.memset(acc[:], 0.0), [], ["acc"])
        QS0 = NT * 128
        TB = (0, 2, 4, 6); SVB = (1, 3, 5, 7)

        def sattn_unit(u, b, tl):
            p_ = u % NU
            KTc, Vc, PTs = KTcs[p_], Vcs[p_], PTss[p_]
            ktk, vkk, ptk = "KTc%d" % p_, "Vc%d" % p_, "PTs%d" % p_
            tb, sv = TB[p_], SVB[p_]
            if tl == 16:
                npart = 128
                kT = lambda hp: KT[:, hp, QS0:QS0 + 128]
                vT = lambda h: Vaug[:, NT, h, :]
                kkeys = [("KT", NT)]; vkeys = [("Vaug", NT)]
                msk = cntn[:, b, :]; mkey = "cntn"
            else:
                r = tl
                m0, npart = (0, 128) if r < 8 else (96, 32)
                l_ = sattn_unit.nload % NLB; sattn_unit.nload += 1
                kc_f, vc_f = kcf[l_], vcf[l_]
                kck, vck = "kc_f%d" % l_, "vc_f%d" % l_
                dma("sp", kc_f[0:npart, :], ck[b, 16 * m0 + r:2048:16, :], [], [kck])
                dma("sp", vc_f[0:npart, :], cv[b, 16 * m0 + r:2048:16, :], [], [vck])
                for c in range(4):
                    A("pe", lambda e, c=c: e.transpose(out=ps[tb][:, 128 * c:128 * c + npart], in_=kc_f[0:npart, 128 * c:128 * c + 128],
                                                       identity=ident[0:npart, 0:npart]), [kck, "ident"], [("ps", tb)])
                A("act", lambda e: e.activation(out=KTc[:, :, 0:npart], in_=ps[tb][:].rearrange("p (c k) -> p c k", c=4)[:, :, 0:npart], func=AF.Copy),
                  [("ps", tb)], [ktk])
                A("act", lambda e: e.activation(out=Vc[0:npart, :, 0:64], in_=vc_f[0:npart, :].rearrange("p (h e) -> p h e", h=H), func=AF.Copy),
                  [vck], [vkk])
                kT = lambda hp: KTc[:, hp, 0:npart]
                vT = lambda h: Vc[0:npart, h, :]
                kkeys = [ktk]; vkeys = [vkk]
                msk = cnts[0:npart, tl, :]; mkey = "cnts"
            for hp in range(4):
                A("pe", lambda e, hp=hp: e.matmul(ps[sv][0:npart, 2 * T * hp:2 * T * hp + 2 * T], lhsT=kT(hp), rhs=QTbd[:, hp, b, :], start=True, stop=True),
                  kkeys + ["QTbd"], [("ps", sv)])
            A("act", lambda e: e.activation(out=PTs[0:npart, :, :].rearrange("p h q -> p (h q)"), in_=ps[sv][0:npart, 0:H * T], func=AF.Exp), [("ps", sv)], [ptk])
            A("dve", lambda e: e.tensor_tensor(out=PTs[0:npart, :, :], in0=PTs[0:npart, :, :], in1=msk.unsqueeze(1).broadcast_to([npart, H, T]), op=ALU.mult),
              [ptk, mkey], [ptk])
            for h in range(H):
                A("pe", lambda e, h=h: e.matmul(ps[sv][0:65, 64 + T * h:64 + T * h + T], lhsT=vT(h), rhs=PTs[0:npart, h, :], start=True, stop=True),
                  vkeys + [ptk], [("ps", sv)])
            A("dve", lambda e: e.tensor_tensor(out=acc[0:65, :, T * b:T * b + T], in0=acc[0:65, :, T * b:T * b + T],
                                               in1=ps[sv][0:65, 64:64 + H * T].rearrange("p (h q) -> p h q", h=H), op=ALU.add),
              [("ps", sv), "acc"], ["acc"])

        sattn_unit.nload = 0
        u = 0
        for b in (range(2) if probe == "quick" else range(DB)):
            for tl in range(17):
                units.append(lambda u=u, b=b, tl=tl: sattn_unit(u, b, tl))
                u += 1
    done = 0
    for k_, th in enumerate(rec):
        th()
        want = (len(units) * (k_ + 1)) // len(rec)
        while done < want:
            units[done](); done += 1
    while done < len(units):
        units[done](); done += 1
    if SAMPLE_ATTN_DONE:
        dma("sp", scr_acc, acc[0:65, :, :].rearrange("p h t -> p (h t)"), ["acc"], ["scr_acc"])
    dma("sp", o_wkvs, Ss[:].rearrange("p i j -> p (i j)"), ["Ss"], ["o_wkvs"], final=True)
    for b in range(DB):
        dma("sp", scr_o[8 * b:8 * b + 8, :].rearrange("t (h i) -> h t i", h=H), outs[8 * b:8 * b + 8, :, :], ["outs"], [("scr_o", b)])
    dma("sp", O32[:].rearrange("p h i -> p (h i)"), scr_o, ["scr_o"], ["O32"])
    rwkv_out(NT)

    new_phase()
    X1_BYTES = (NT + 1) * D * 4
    x1 = sb("x1", [128, NT + 1, D])
    G2p = sb("G2p", [128, D]); G2s = sb("G2s", [128, D])
    KEEP_P3 = ptr["R2"] - R2_0
    G1 = sb("G1", [128, D]); G1s = sb("G1s", [128, D])
    KEEP_P2 = ptr["R2"] - R2_0
    wadah = sb("wadah", [128, 8, 512], BF16); m17 = sb("m17", [17, D]); bgb = sb("bgb", [17, 512])

    def gate_m17(gidx):
        col0 = (2 if gidx == 0 else 5) * D
        for hf in range(2):
            for q in range(2):
                A("pool", lambda e, hf=hf, q=q: e.dma_start(
                    out=wadah[:, 4 * q:4 * q + 4, :],
                    in_=w_ada[512 * q:512 * q + 512, col0 + 512 * hf:col0 + 512 * hf + 512].rearrange("(kc p) n -> p kc n", p=128)),
                  [], ["wadah"], dma=True)
            dma("sp", bgb[:], bgate[gidx, 512 * hf:512 * hf + 512].partition_broadcast(17), [], ["bgb"])
            for kc in range(8):
                A("pe", lambda e, kc=kc: e.matmul(ps[0][0:17, :], lhsT=scT[:, kc, :], rhs=wadah[:, kc, :], start=(kc == 0), stop=(kc == 7)),
                  ["scT", "wadah"], [("ps", 0)])
            A("dve", lambda e, hf=hf: e.tensor_tensor(out=m17[:, 512 * hf:512 * hf + 512], in0=ps[0][0:17, :], in1=bgb[:], op=ALU.add),
              [("ps", 0), "bgb"], [("m17", hf)])

    def gate_bcast(sel, dst, dkey):
        for hf in range(2):
            A("pe", lambda e, hf=hf: e.matmul(ps[1][:], lhsT=Esel[:, 128 * sel:128 * sel + 128], rhs=m17[:, 512 * hf:512 * hf + 512],
                                              start=True, stop=True), ["Esel", ("m17", hf)], [("ps", 1)])
            A("act", lambda e, hf=hf: e.activation(out=dst[:, 512 * hf:512 * hf + 512], in_=ps[1][:], func=AF.Copy), [("ps", 1)], [dkey])

    gate_m17(1); gate_bcast(0, G2p, "G2p"); gate_bcast(1, G2s, "G2s")
    gate_m17(0); gate_bcast(0, G1, "G1"); gate_bcast(1, G1s, "G1s")

    new_phase(KEEP_P2)
    wout = sb("wout", [128, 8, D], BF16)
    for kc in range(8):
        A("pool", lambda e, kc=kc: e.dma_start(out=wout[:, kc, :], in_=w_out[kc * 128:(kc + 1) * 128, :]), [], [("wout", kc)], dma=True)
    attT = sb("attT", [128, 4, 128], BF16); rsum = sb("rsum", [128, H])
    KEEP_2B = ptr["R2"] - R2_0
    cntm = sb("cntm", [128, 16, 128], BF16)
    for hf in range(2):
        A("pool", lambda e, hf=hf: e.dma_start(out=cntm[:, 8 * hf:8 * hf + 8, :].rearrange("p d q -> p (d q)"), in_=cnt_d[:, 1024 * hf:1024 * hf + 1024]),
          [], [("cntm", hf)], dma=True)
    PTb = sb("PTb", [128, NT, 2, 128], BF16)

    oacc = xt[:, 0:520].rearrange("p (h e) -> p h e", h=H)

    def attn_finish(ti, Gt=None, gk="G1"):
        for g in range(2):
            A("act", lambda e, g=g: e.activation(out=xt[:, 260 * g:260 * g + 260], in_=ps[6 + g][:, 0:260], func=AF.Copy), [("ps", 6 + g)], ["xt"])
        A("dve", lambda e: e.reciprocal(out=rsum[:].unsqueeze(2), in_=oacc[:, :, 64:65]), ["xt"], ["rsum"])
        A("dve", lambda e: e.tensor_tensor(out=xsn[:, 0:512].rearrange("p (h e) -> p h e", h=H), in0=oacc[:, :, 0:64],
                                           in1=rsum[:].unsqueeze(2).broadcast_to([128, H, E]), op=ALU.mult), ["xt", "rsum"], ["xsn"])
        for c in range(4):
            A("pe", lambda e, c=c: e.transpose(out=ps[5][:, 128 * c:128 * c + 128], in_=xsn[:, 128 * c:128 * c + 128], identity=ident[:]),
              ["xsn", "ident"], [("ps", 5)])
        A("act", lambda e: e.activation(out=attT[:], in_=ps[5][:].rearrange("p (c t) -> p c t", c=4), func=AF.Copy), [("ps", 5)], ["attT"])
        for hf in range(2):
            for kc in range(8):
                lhs = attT[:, kc, :] if kc < 4 else rwT[:, ti, kc - 4, :]
                A("pe", lambda e, hf=hf, kc=kc, lhs=lhs: e.matmul(ps[2 + hf][:], lhsT=lhs, rhs=wout[:, kc, 512 * hf:512 * hf + 512],
                                                                  start=(kc == 0), stop=(kc == 7)),
                  ["attT", ("rwT", ti), ("wout", kc)], [("ps", 2 + hf)])
        dma("sp", xt[:], xsm if ti == NT else xp[ti * 128:(ti + 1) * 128, :], [], ["xt"])
        for hf in range(2):
            cs = slice(512 * hf, 512 * hf + 512)
            Gt_ = G1 if Gt is None else Gt
            A("dve", lambda e, hf=hf, cs=cs, Gt_=Gt_: e.tensor_tensor(out=xsn[:, cs], in0=ps[2 + hf][:], in1=Gt_[:, cs], op=ALU.mult),
              [("ps", 2 + hf), gk, "xsn"], ["xsn"])
            A("dve", lambda e, cs=cs: e.tensor_tensor(out=x1[:, ti, cs], in0=xsn[:, cs], in1=xt[:, cs], op=ALU.add), ["xsn", "xt"], [("x1", ti)])

    def attn_prompt(qt):
        qs_ = slice(qt * 128, qt * 128 + 128)
        for hg in range(4):
            for kt0 in range(0, qt + 1, 4):
                n = min(4, qt + 1 - kt0)
                for j in range(n):
                    kt = kt0 + j
                    for hh in range(2):
                        pr = slice(64 * hh, 64 * hh + 64)
                        A("pe", lambda e, j=j, kt=kt, pr=pr, hh=hh, hg=hg: e.matmul(
                            ps[hh][:, 128 * j:128 * j + 128], lhsT=KT[pr, hg, kt * 128:kt * 128 + 128], rhs=QT[pr, hg, qs_], start=True, stop=True),
                          [("KT", kt), ("QT", qt)], [("ps", hh)])
                for hh in range(2):
                    A("act", lambda e, kt0=kt0, n=n, hh=hh: e.activation(
                        out=PTb[:, kt0:kt0 + n, hh, :], in_=ps[hh][:, 0:128 * n].rearrange("p (k q) -> p k q", k=n), func=AF.Exp),
                      [("ps", hh)], [("PTb", kt_) for kt_ in range(kt0, kt0 + n)])
                for kt in range(kt0, kt0 + n):
                    d = qt - kt
                    eng = "dve" if d % 2 == 0 else "pool"
                    A(eng, lambda e, kt=kt, d=d: e.tensor_tensor(out=PTb[:, kt, :, :], in0=PTb[:, kt, :, :],
                                                                 in1=cntm[:, d, :].unsqueeze(1).broadcast_to([128, 2, 128]), op=ALU.mult),
                      [("PTb", kt), ("cntm", d // 8)], [("PTb", kt)])
            for hh in range(2):
                h = 2 * hg + hh
                ob = ps[6 + h // 4][:, 65 * (h % 4):65 * (h % 4) + 65]
                for kt in range(qt + 1):
                    A("pe", lambda e, kt=kt, hh=hh, h=h, ob=ob: e.matmul(ob, lhsT=PTb[:, kt, hh, :], rhs=Vaug[:, kt, h, :],
                                                                         start=(kt == 0), stop=(kt == qt)),
                      [("PTb", kt), ("Vaug", kt)], [("ps", 6 + h // 4)])

    QTL = [0, 1] if probe == "quick" else list(range(NT))
    attn_prompt(QTL[0])
    for n_, qt in enumerate(QTL):
        fin = capture(lambda: attn_finish(qt))
        nxt = capture(lambda: attn_prompt(QTL[n_ + 1])) if n_ + 1 < len(QTL) else []
        if nxt:
            interleave(nxt, fin)
        else:
            interleave(fin, [])


    if SAMPLE_ATTN_DONE:
        accv = xsn[:].rearrange("p (h t) -> p h t", h=H)
        dma("sp", xsn[0:65, :], scr_acc, ["scr_acc"], ["xsn"])
        for h in range(H):
            A("pe", lambda e, h=h: e.transpose(out=ps[6 + h // 4][:, 65 * (h % 4):65 * (h % 4) + 65], in_=accv[0:65, h, :], identity=ident[0:65, 0:65]),
              ["xsn", "ident"], [("ps", 6 + h // 4)])
        attn_finish(NT, G1s, "G1s")

    new_phase(KEEP_P3)
    ptr["R1"] = R1_0
    w1c = [sb("w1c%d" % i, [128, 8, D], BF16, "R1") for i in range(2)]
    w2c = [sb("w2c%d" % i, [128, 8, D], BF16, "R1") for i in range(2)]
    h2T = sb("h2T", [128, 8, (NT + 1) * 128], BF16)
    GT = 3
    hid = sb("hid", [128, 8, 128 * GT], BF16); rl = sb("rl", [128, 128 * GT])
    TILES = [0, 1, NT] if probe == "quick" else list(range(NT + 1))
    if not SAMPLE_ATTN_DONE:
        TILES = [t_ for t_ in TILES if t_ != NT]
    for ti in TILES:
        rms_from(x1[:, ti, :], ("x1", ti), "A2", "B2", ti == NT)
        A("act", lambda e, ti=ti: e.activation(out=h2T[:, :, ti * 128:(ti + 1) * 128], in_=hT[:], func=AF.Copy), ["hT"], [("h2T", ti)])
    groups = [TILES[i:i + GT] for i in range(0, len(TILES), GT)]
    for c in range(4):
        wb = c % 2
        for kc in range(8):
            A("pool", lambda e, kc=kc, c=c, wb=wb: e.dma_start(out=w1c[wb][:, kc, :], in_=w_ff1[kc * 128:(kc + 1) * 128, c * D:(c + 1) * D]),
              [], [("w1c%d" % wb, kc)], dma=True)
            A("pool", lambda e, kc=kc, c=c, wb=wb: e.dma_start(out=w2c[wb][:, kc, :], in_=w_ff2[c * D + kc * 128:c * D + (kc + 1) * 128, :]),
              [], [("w2c%d" % wb, kc)], dma=True)
        for gi, grp in enumerate(groups):
            contiguous = all(grp[i + 1] == grp[i] + 1 for i in range(len(grp) - 1))
            subgroups = [grp] if contiguous else [[t_] for t_ in grp]
            for sg in subgroups:
                ntok = 128 * len(sg)
                t0 = sg[0] * 128
                for fc in range(8):
                    hb = fc % 2
                    for kc in range(8):
                        A("pe", lambda e, fc=fc, kc=kc, hb=hb, t0=t0, ntok=ntok, wb=wb: e.matmul(
                            ps[hb][:, 0:ntok], lhsT=w1c[wb][:, kc, 128 * fc:128 * fc + 128], rhs=h2T[:, kc, t0:t0 + ntok],
                            start=(kc == 0), stop=(kc == 7)), [("w1c%d" % wb, kc)] + [("h2T", t_) for t_ in sg], [("ps", hb)])
                    A("act", lambda e, hb=hb, ntok=ntok: e.activation(out=rl[:, 0:ntok], in_=ps[hb][:, 0:ntok], func=AF.Relu), [("ps", hb)], ["rl"])
                    A("dve", lambda e, fc=fc, ntok=ntok: e.tensor_tensor(out=hid[:, fc, 0:ntok], in0=rl[:, 0:ntok], in1=rl[:, 0:ntok], op=ALU.mult),
                      ["rl"], [("hid", fc)])
                for tl, ti in enumerate(sg):
                    Gt = G2s if ti == NT else G2p
                    gk = "G2s" if ti == NT else "G2p"
                    for hf in range(2):
                        yb = 2 + 2 * tl + hf
                        cs = slice(512 * hf, 512 * hf + 512)
                        for fc in range(8):
                            A("pe", lambda e, fc=fc, tl=tl, yb=yb, cs=cs, wb=wb: e.matmul(ps[yb][:], lhsT=hid[:, fc, 128 * tl:128 * tl + 128], rhs=w2c[wb][:, fc, cs],
                                                                                   start=(fc == 0), stop=(fc == 7)),
                              [("hid", fc), ("w2c%d" % wb, fc)], [("ps", yb)])
                        A("dve", lambda e, yb=yb, cs=cs, Gt=Gt: e.tensor_tensor(out=xsn[:, cs], in0=ps[yb][:], in1=Gt[:, cs], op=ALU.mult),
                          [("ps", yb), gk, "xsn"], ["xsn"])
                        A("dve", lambda e, ti=ti, cs=cs: e.tensor_tensor(out=x1[:, ti, cs], in0=xsn[:, cs], in1=x1[:, ti, cs], op=ALU.add),
                          ["xsn", ("x1", ti)], [("x1", ti)])

    new_phase(X1_BYTES)
    gfb = sb("gfb", [128, D]); yo = [sb("yo%d" % i, [128, D]) for i in range(2)]
    fx = [xsn, sb("fxsn1", [128, D])]; fss = [ss, sb("fss1", [128, 1])]; frs = [rstd, sb("frstd1", [128, 1])]
    fk = [("xsn", "ss", "rstd"), ("fxsn1", "fss1", "frstd1")]
    dma("sp", gfb[:], gfin.partition_broadcast(128), [], ["gfb"])
    for n_, ti in enumerate(TILES):
        xa = x1[:, ti, :]
        y_ = yo[n_ % 2]; yk = "yo%d" % (n_ % 2)
        xs_, ss_, rs_ = fx[n_ % 2], fss[n_ % 2], frs[n_ % 2]
        xk, sk, rk_ = fk[n_ % 2]
        A("act", lambda e, xa=xa, xs_=xs_, ss_=ss_: e.activation(out=xs_[:], in_=xa, func=AF.Square, accum_out=ss_[:]), [("x1", ti)], [xk, sk])
        A("dve", lambda e, ss_=ss_, rs_=rs_: e.tensor_scalar(out=rs_[:], in0=ss_[:], scalar1=1.0 / D, scalar2=1e-6, op0=ALU.mult, op1=ALU.add), [sk], [rk_])
        A("act", lambda e, rs_=rs_: e.activation(out=rs_[:], in_=rs_[:], func=AF.Sqrt), [rk_], [rk_])
        A("dve", lambda e, rs_=rs_: e.reciprocal(out=rs_[:], in_=rs_[:]), [rk_], [rk_])
        A("act", lambda e, xa=xa, xs_=xs_, rs_=rs_: e.activation(out=xs_[:], in_=xa, func=AF.Copy, scale=rs_[:]), [("x1", ti), rk_], [xk])
        A("dve", lambda e, y_=y_, xs_=xs_: e.tensor_tensor(out=y_[:], in0=xs_[:], in1=gfb[:], op=ALU.mult), [xk, "gfb"], [yk])
        dma("sp", o_ys if ti == NT else o_yp[ti * 128:(ti + 1) * 128, :], y_[:], [yk], [("o_y", ti)], final=True)

    P.emit()
    es.close()
    return nc


def _rope_tables():
    half = 8
    inv = (500000.0 ** (-np.arange(half, dtype=np.float32) * np.float32(2.0 / 16))).astype(np.float32)
    tab = np.zeros((17, 128, 128), np.float32)
    for ti in range(17):
        if ti < NT:
            pos = (ti * 128 + np.arange(128)).astype(np.float32)
        else:
            pos = (PAST + (np.arange(128) % T)).astype(np.float32)
        ang = pos[:, None] * inv[None, :]
        tab[ti, :, 0:64] = np.tile(np.cos(ang), (1, H))
        tab[ti, :, 64:128] = np.tile(np.sin(ang), (1, H))
    return tab


def _cmask():
    m = np.zeros((128, 896), np.float32)
    m[0:64, 0:64] = 1.0; m[64:128, 64:128] = 1.0
    i = np.arange(128)
    strictT = (i[:, None] < i[None, :]).astype(np.float32)
    inclT = (i[:, None] <= i[None, :]).astype(np.float32)
    m[:, 128:256] = strictT; m[:, 256:384] = inclT; m[:, 384:512] = strictT; m[:, 512:640] = inclT
    m[:, 640:768] = (i[None, :] < i[:, None]).astype(np.float32)
    m[:, 768:896] = 1.0
    return m


def _cnt_table():
    k = np.arange(128)[:, None, None]; d = np.arange(16)[None, :, None]; q = np.arange(128)[None, None, :]
    dl = 128 * d + q - k
    c = ((dl >= 0) & (dl <= 128)).astype(np.float32) + ((dl >= 0) & (dl <= 512) & (dl % 4 == 0)) + ((dl >= 0) & (dl <= 2048) & (dl % 16 == 0))
    return np.ascontiguousarray(c.reshape(128, 2048).astype(np.float32))


def _cnts():
    c = np.zeros((128, 16, T), np.float32)
    for r in range(16):
        m0, n = (0, 128) if r < 8 else (96, 32)
        for p in range(n):
            R = 16 * (m0 + p) + r
            for i in range(T):
                v = 0
                if R >= 1920 + i: v += 1
                if R % 4 == i % 4 and R >= 1536 + i: v += 1
                if R % 16 == i % 16 and R >= i: v += 1
                c[p, r, i] = v
    return np.ascontiguousarray(c.reshape(128, 16 * T))


def _cntn():
    c = np.zeros((128, DB, T), np.float32)
    for b in range(DB):
        for s_ in range(T):
            for i in range(T):
                d = i - s_
                if d >= 0:
                    c[T * b + s_, b, i] = 1 + (d % 4 == 0) + (d % 16 == 0)
    return np.ascontiguousarray(c.reshape(128, DB * T))


def _esel():
    e = np.zeros((17, 256), np.float32)
    e[0, 0:128] = 1.0
    for b in range(DB):
        e[1 + b, 128 + T * b:128 + T * b + T] = 1.0
    return e


_NC_CACHE = {}


def kernel(**inp):
    f = lambda a: np.ascontiguousarray(np.asarray(a, dtype=np.float32))
    x_prompt = f(inp["x_prompt"]); x_sample = f(inp["x_sample"])
    c_prompt = f(inp["c_prompt"]); c_sample = f(inp["c_sample"])
    b_ada = f(inp["b_ada"])[0]
    vec_rows = [f(inp["mu"])[0].reshape(14, 128)]
    for n in ("w0", "a0", "k_k", "k_a"):
        vec_rows.append(f(inp[n])[0].reshape(4, 128))
    vec_rows.append(f(inp["r_k"])[0].reshape(4, 128))
    for n in ("lnx_g", "lnx_b"):
        vec_rows.append(f(inp[n])[0].reshape(4, 128))
    vec_rows.append(f(inp["norm1_g"])[0].reshape(8, 128))
    vec_rows.append(f(inp["norm2_g"])[0].reshape(8, 128))
    for blk in (0, 1, 3, 4):
        vec_rows.append(b_ada[blk * D:(blk + 1) * D].reshape(8, 128))
    vecs = np.ascontiguousarray(np.concatenate(vec_rows, axis=0))
    assert vecs.shape == (NVEC, 128)
    bgate = np.ascontiguousarray(np.stack([b_ada[2 * D:3 * D], b_ada[5 * D:6 * D]]))
    shared = {
        "vecs": vecs, "bgate": bgate, "w_ada": f(inp["w_ada"])[0], "w_in": f(inp["w_in"])[0],
        "ident": np.eye(128, dtype=np.float32), "rope": _rope_tables(), "cmask": _cmask(),
        "w2a2": np.ascontiguousarray(np.concatenate([f(inp["w2"])[0], f(inp["a2"])[0]], axis=0)),
        "w_out": f(inp["w_out"])[0], "w_ff1": f(inp["w_ff1"])[0], "w_ff2": f(inp["w_ff2"])[0], "gfin": f(inp["normf_g"]),
        "cnt": _cnt_table(), "esel": _esel(), "cnts": _cnts(), "cntn": _cntn(),
        "g2": f(inp["g2"])[0],
    }
    state_shift = f(inp["state_shift"])[0]
    state_wkv = f(inp["state_wkv"])[0]
    cache_k = np.asarray(inp["cache_k"], dtype=np.float32)[0].reshape(128, 2048, 512)
    cache_v = np.asarray(inp["cache_v"], dtype=np.float32)[0].reshape(128, 2048, 512)
    in_maps = []
    for i in range(NCORES):
        m = dict(shared)
        m["xp"] = x_prompt[i]
        m["xs"] = x_sample[DB * i:DB * (i + 1)].reshape(128, D)
        m["c17"] = np.ascontiguousarray(np.concatenate([c_prompt[i:i + 1], c_sample[DB * i:DB * (i + 1)]], axis=0))
        m["sshift"] = state_shift[DB * i:DB * (i + 1)]
        m["swkv"] = state_wkv[DB * i:DB * (i + 1)].reshape(128, E * E)
        m["ck"] = cache_k[DB * i:DB * (i + 1)]
        m["cv"] = cache_v[DB * i:DB * (i + 1)]
        in_maps.append(m)
    if "nc" not in _NC_CACHE:
        _NC_CACHE["nc"] = build_program()
    nc = _NC_CACHE["nc"]
    res = run_bass_kernel_spmd(nc, in_maps, core_ids=list(range(NCORES)))
    R = res.results
    g = lambda name: [np.asarray(R[i][name], dtype=np.float32) for i in range(NCORES)]
    y_prompt = np.stack(g("o_yp"))
    y_sample = np.concatenate(g("o_ys")).reshape(128, T, D) if SAMPLE_ATTN_DONE else np.zeros((128, T, D), np.float32)
    kwin = np.stack(g("o_kwin")).reshape(1, 8, S, H, E)
    vwin = np.stack(g("o_vwin")).reshape(1, 8, S, H, E)
    wkv_p = np.stack(g("o_wkvp")).reshape(1, 8, H, E, E)
    shp = np.stack(g("o_shp")).reshape(1, 8, RIN)
    knew = np.concatenate(g("o_knew")).reshape(1, 128, T, H, E)
    vnew = np.concatenate(g("o_vnew")).reshape(1, 128, T, H, E)
    wkv_s = np.concatenate(g("o_wkvs")).reshape(1, 128, H, E, E)
    shs = np.concatenate(g("o_shs")).reshape(1, 128, RIN)
    return (y_prompt, y_sample, kwin, vwin, wkv_p, shp, knew, vnew, wkv_s, shs)
```

```python
import contextlib
import numpy as np
import concourse.bass as bass
import concourse.mybir as mybir
from concourse.bass_utils import run_bass_kernel_spmd

F32 = mybir.dt.float32
BF16 = mybir.dt.bfloat16
AF = mybir.ActivationFunctionType
ALU = mybir.AluOpType
AX = mybir.AxisListType

NCORES = 8
D = 1024
S = 2048
NT = 16
DB = 16
T = 8
H = 8
E = 64
RIN = 1792
INW = 3328
DFF = 4096
PAST = 8192
ENGS = ("pe", "act", "dve", "pool", "sp")

V_MU, V_W0, V_A0, V_KK, V_KA, V_RK, V_LG, V_LB, V_G1, V_G2, V_BSH1, V_BSC1, V_BSH2, V_BSC2 = (
    0, 14, 18, 22, 26, 30, 34, 38, 42, 50, 58, 66, 74, 82)
NVEC = 90
SAMPLE_ATTN_DONE = True


class Op:
    __slots__ = ("eng", "fn", "deps", "is_dma", "signal", "sem", "target", "name")

    def __init__(s, eng, fn, is_dma, name):
        s.eng = eng; s.fn = fn; s.is_dma = is_dma; s.deps = []
        s.signal = False; s.sem = None; s.target = 0; s.name = name


class Prog:
    def __init__(s, nc, n_dma_sems=24):
        s.nc = nc
        s.ops = []
        s.st = {}
        s.group_of = {}
        s.n_dma_sems = n_dma_sems
        s.final_ops = []
        s.exclusive = {"ps"}

    def _conf(s, g, name, sub):
        ent, idx = g
        if name == "*":
            return list(ent.keys())
        if sub is None:
            keys = [(name, x) for x in idx.get(name, ())]
        else:
            keys = [k for k in ((name, sub), (name, None)) if k in ent]
        if ("*", None) in ent:
            keys.append(("*", None))
        return keys

    def _norm(s, k):
        if not isinstance(k, tuple):
            k = (k, None)
        name, sub = k
        if name.startswith("*@"):
            return name[2:], "*", None
        return s.group_of.get(name, name), name, sub

    def add(s, eng, fn, reads=(), writes=(), dma=False, name="", final=False):
        if getattr(s, "_cap", None) is not None:
            s._cap.append((eng, fn, reads, writes, dma, name, final))
            return None
        op = Op(eng, fn, dma, name)
        reads = [s._norm(k) for k in reads]; writes = [s._norm(k) for k in writes]
        deps = {}
        for gname, name_, sub in reads:
            g = s.st.setdefault(gname, ({}, {}))
            for k in s._conf(g, name_, sub):
                w = g[0][k][0]
                if w is not None:
                    deps[id(w)] = w
                if name_ in s.exclusive:
                    for r in g[0][k][1]:
                        if r.eng != eng:
                            deps[id(r)] = r
        for gname, name_, sub in writes:
            g = s.st.setdefault(gname, ({}, {}))
            for k in s._conf(g, name_, sub):
                w = g[0][k][0]
                if w is not None:
                    deps[id(w)] = w
                for r in g[0][k][1]:
                    deps[id(r)] = r
        for gname, name_, sub in reads:
            ent, idx = s.st[gname]
            rl_ = ent.setdefault((name_, sub), [None, []])[1]
            if not op.is_dma:
                rl_[:] = [r for r in rl_ if r.is_dma or r.eng != op.eng]
            rl_.append(op)
            idx.setdefault(name_, set()).add(sub)
        for gname, name_, sub in writes:
            ent, idx = s.st[gname]
            if name_ == "*":
                ent.clear(); idx.clear()
            elif sub is None:
                for x in idx.get(name_, ()):
                    ent.pop((name_, x), None)
                idx[name_] = set()
            ent[(name_, sub)] = [op, []]
            idx.setdefault(name_, set()).add(sub)
        for w in deps.values():
            if w is op:
                continue
            if (not w.is_dma) and (not op.is_dma) and w.eng == op.eng and op.eng == "pe":
                continue
            op.deps.append(w)
            w.signal = True
        s.ops.append(op)
        if final:
            op.signal = True
            s.final_ops.append(op)
        return op

    def emit(s):
        nc = s.nc
        with contextlib.ExitStack() as es:
            esems = {e: es.enter_context(nc.semaphore("s_" + e)) for e in ("pe", "act", "dve", "pool")}
            dpool = {q: [es.enter_context(nc.semaphore("d%s%d" % (q, i))) for i in range(s.n_dma_sems)]
                     for q in ("sp", "pool")}
            cnt = {e: 0 for e in esems}
            dcum = {q: [0] * s.n_dma_sems for q in dpool}
            dprev = {}
            kq = {q: 0 for q in dpool}
            for op in s.ops:
                if op.is_dma:
                    q = op.eng
                    i = kq[q] % s.n_dma_sems; kq[q] += 1
                    op.sem = dpool[q][i]; dprev[id(op)] = dcum[q][i]
                    dcum[q][i] += 16; op.target = dcum[q][i]
                elif op.signal:
                    cnt[op.eng] += 1
                    op.sem = esems[op.eng]; op.target = cnt[op.eng]
            by_eng = {e: [o for o in s.ops if o.eng == e] for e in ENGS}
            block = es.enter_context(nc.Block())

            def run(engname, eng):
                waited = {}

                def wait(sem, val):
                    if val <= 0:
                        return
                    key = id(sem)
                    if waited.get(key, 0) >= val:
                        return
                    eng.wait_ge(sem, val)
                    waited[key] = val

                for op in by_eng[engname]:
                    for w in op.deps:
                        wait(w.sem, w.target)
                    if op.is_dma:
                        wait(op.sem, dprev[id(op)])
                        op.fn(eng).then_inc(op.sem, 16)
                    else:
                        ins = op.fn(eng)
                        if op.signal:
                            ins.then_inc(op.sem, 1)
                for op in s.final_ops:
                    if op.eng == engname:
                        wait(op.sem, op.target)

            block.tensor(lambda e: run("pe", e))
            block.scalar(lambda e: run("act", e))
            block.vector(lambda e: run("dve", e))
            block.gpsimd(lambda e: run("pool", e))
            block.sync(lambda e: run("sp", e))


def build_program(probe=None):
    nc = bass.Bass("TRN2", target_bir_lowering=False)
    es = contextlib.ExitStack()
    P = Prog(nc)
    A = P.add

    def din(name, shape, dt=F32):
        return nc.dram_tensor(name, list(shape), dt, kind="ExternalInput").ap()

    def dout(name, shape, dt=F32):
        return nc.dram_tensor(name, list(shape), dt, kind="ExternalOutput").ap()

    START = 16512
    G0, R1_0, R2_0, END = START, START + 20480, START + 20480 + 71680, 229344
    ptr = {"G": G0, "R1": R1_0, "R2": R2_0}
    lim = {"G": R1_0, "R1": R2_0, "R2": END}
    cnt_names = [0]

    def sb(name, shape, dt=F32, reg="R2"):
        n = 1
        for d in shape[1:]:
            n *= d
        nbytes = n * (2 if dt == BF16 else 4)
        off = ptr[reg]
        ptr[reg] = off + (nbytes + 31) // 32 * 32
        assert ptr[reg] <= lim[reg], (name, reg, ptr[reg] - lim[reg])
        cnt_names[0] += 1
        if reg != "G":
            P.group_of[name] = "arena"
        return nc.alloc_sbuf_tensor_at("s%d_%s" % (cnt_names[0], name), list(shape), dt, offset=off)

    def new_phase(keep_r2=0):
        A("dve", lambda e: e.memset(bar_t[:], 0.0), [], ["*@arena", "bar_t"])
        ptr["R2"] = R2_0 + keep_r2

    def dma(q, out, in_, reads, writes, final=False):
        return A(q, lambda e: e.dma_start(out=out, in_=in_), reads, writes, dma=True, final=final)

    xp = din("xp", [S, D]); xsm = din("xs", [128, D])
    c17 = din("c17", [17, D]); vecs = din("vecs", [NVEC, 128]); bgate = din("bgate", [2, D])
    w_ada = din("w_ada", [D, 6 * D]); w_in = din("w_in", [D, INW])
    sshift = din("sshift", [DB, RIN])
    ident_d = din("ident", [128, 128]); rope_d = din("rope", [17, 128, 128])
    cmask_d = din("cmask", [128, 896])
    w2a2_d = din("w2a2", [128, 512]); g2_d = din("g2", [128, 512])
    o_kwin = dout("o_kwin", [S, 512]); o_vwin = dout("o_vwin", [S, 512])
    o_shp = dout("o_shp", [1, RIN]); o_knew = dout("o_knew", [128, 512]); o_vnew = dout("o_vnew", [128, 512])
    o_shs = dout("o_shs", [DB, RIN]); o_wkvp = dout("o_wkvp", [H, E, E])
    swkv = din("swkv", [128, E * E]); o_wkvs = dout("o_wkvs", [128, E * E])
    w_out = din("w_out", [D, D]); w_ff1 = din("w_ff1", [D, DFF]); w_ff2 = din("w_ff2", [DFF, D]); gfin = din("gfin", [D])
    cnt_d = din("cnt", [128, 2048]); esel_d = din("esel", [17, 256])
    ck = din("ck", [DB, 2048, 512]); cv = din("cv", [DB, 2048, 512])
    cnts_d = din("cnts", [128, 16 * T]); cntn_d = din("cntn", [128, DB * T])
    o_yp = dout("o_yp", [S, D]); o_ys = dout("o_ys", [128, D])
    scr_v = nc.dram_tensor("scr_v", [6, 128, 512], F32, kind="Internal").ap()
    scr_o = nc.dram_tensor("scr_o", [128, 512], F32, kind="Internal").ap()
    scr_acc = nc.dram_tensor("scr_acc", [65, 1024], F32, kind="Internal").ap()

    ps = [es.enter_context(nc.psum_tensor("ps%d" % i, [128, 512], F32)) for i in range(8)]

    bar_t = sb("bar_t", [128, 8], F32, "G")
    ident = sb("ident", [128, 128], F32, "G"); identb = sb("identb", [128, 128], BF16, "G")
    vecT = sb("vecT", [128, NVEC], F32, "G"); scT = sb("scT", [128, 8, 17], BF16, "G")
    modT = {n: sb("modT_" + n, [128, 8, 17], F32, "G") for n in ("A1", "B1", "A2", "B2")}
    cmask = sb("cmask", [128, 896], F32, "G"); ropet = sb("ropet", [128, 128], F32, "G")
    ss = sb("ss", [128, 1], F32, "G"); rstd = sb("rstd", [128, 1], F32, "G")
    xt = sb("xt", [128, D], F32, "G"); xsn = sb("xsn", [128, D], F32, "G"); hT = sb("hT", [128, 8, 128], BF16, "G")
    sshT = sb("sshT", [128, 14, DB], F32, "G")
    Esel = sb("Esel", [17, 256], F32, "G")
    blk = cmask[:, 0:128]; MT4 = cmask[:, 128:640]; MLs = cmask[:, 640:768]; ones_t = cmask[:, 768:896]
    QT = sb("QT", [128, 4, (NT + 1) * 128], BF16, "R1"); KT = sb("KT", [128, 4, (NT + 1) * 128], BF16, "R1")
    Vaug = sb("Vaug", [128, NT + 1, H, 65], BF16, "R1"); rwT = sb("rwT", [128, NT + 1, 4, 128], BF16, "R1")

    def vcol(c0, n=4, w=128):
        return vecT[:, c0:c0 + n].unsqueeze(2).broadcast_to([128, n, w])

    dma("sp", ident[:], ident_d, [], ["ident"])
    dma("sp", cmask[:], cmask_d, [], ["cmask"])
    dma("sp", Esel[:], esel_d, [], ["Esel"])
    A("dve", lambda e: e.tensor_copy(out=identb[:], in_=ident[:]), ["ident"], ["identb"])
    dma("sp", xt[0:NVEC, 0:128], vecs, [], ["xt"])
    A("pe", lambda e: e.transpose(out=ps[0][:, 0:NVEC], in_=xt[0:NVEC, 0:128], identity=ident[0:NVEC, 0:NVEC]),
      ["xt", "ident"], [("ps", 0)])
    A("dve", lambda e: e.tensor_copy(out=vecT[:], in_=ps[0][:, 0:NVEC]), [("ps", 0)], ["vecT"])
    c_sb = xsn[0:17, :]
    dma("sp", c_sb, c17, [], ["xsn"])
    A("act", lambda e: e.activation(out=c_sb, in_=c_sb, func=AF.Silu), ["xsn"], ["xsn"])
    for kc in range(8):
        A("pe", lambda e, kc=kc: e.transpose(out=ps[1][:, kc * 17:(kc + 1) * 17], in_=xsn[0:17, kc * 128:(kc + 1) * 128],
                                             identity=ident[0:17, 0:17]), ["xsn", "ident"], [("ps", 1)])
    A("dve", lambda e: e.tensor_copy(out=scT[:], in_=ps[1][:, 0:136].rearrange("p (k b) -> p k b", k=8)), [("ps", 1)], ["scT"])
    A("dve", lambda e: e.memset(Vaug[:, :, :, 64:65], 1.0), [], ["Vaug"])

    wada = [sb("wada%d" % i, [128, 8, D], BF16) for i in range(2)]
    ssh_sb = sb("ssh_sb", [DB, RIN])

    def ada_block(col0, buf):
        for half in range(2):
            A("pool", lambda e, half=half: e.dma_start(
                out=wada[buf][:, 4 * half:4 * half + 4, :],
                in_=w_ada[512 * half:512 * half + 512, col0:col0 + D].rearrange("(kc p) n -> p kc n", p=128)),
              [], ["wada%d" % buf], dma=True)

    def ada_fm(buf, bank, dst, bias_col, gain_col):
        wt = wada[buf]
        for c in range(8):
            for kc in range(8):
                A("pe", lambda e, c=c, kc=kc: e.matmul(ps[bank][:, c * 17:(c + 1) * 17], lhsT=wt[:, kc, c * 128:(c + 1) * 128],
                                                       rhs=scT[:, kc, :], start=(kc == 0), stop=(kc == 7)),
                  ["wada%d" % buf, "scT"], [("ps", bank)])
        pv = ps[bank][:, 0:136].rearrange("p (c b) -> p c b", c=8)
        dk = "modT_" + dst
        A("dve", lambda e: e.tensor_tensor(out=modT[dst][:], in0=pv, in1=vcol(bias_col, 8, 17), op=ALU.add), [("ps", bank), "vecT"], [dk])
        if gain_col is not None:
            A("dve", lambda e: e.scalar_tensor_tensor(out=modT[dst][:], in0=modT[dst][:], scalar=1.0, in1=vcol(gain_col, 8, 17),
                                                      op0=ALU.add, op1=ALU.mult), [dk, "vecT"], [dk])

    ada_block(1 * D, 0); ada_block(0 * D, 1)
    ada_fm(0, 2, "A1", V_BSC1, V_G1); ada_fm(1, 3, "B1", V_BSH1, None)
    ada_block(4 * D, 0); ada_block(3 * D, 1)
    ada_fm(0, 2, "A2", V_BSC2, V_G2); ada_fm(1, 3, "B2", V_BSH2, None)
    dma("sp", ssh_sb[:], sshift, [], ["ssh_sb"])
    for c in range(14):
        A("pe", lambda e, c=c: e.transpose(out=ps[4][:, c * 16:(c + 1) * 16], in_=ssh_sb[:, c * 128:(c + 1) * 128],
                                           identity=ident[0:16, 0:16]), ["ssh_sb", "ident"], [("ps", 4)])
    A("dve", lambda e: e.tensor_copy(out=sshT[:], in_=ps[4][:, 0:224].rearrange("p (c b) -> p c b", c=14)), [("ps", 4)], ["sshT"])

    BS0 = dict(xt=xt, xsn=xsn, hT=hT, ss=ss, rstd=rstd, sfx="")

    def rms_hT(ti, modA, modB, bs=None):
        bs = bs or BS0
        sample = (ti == NT)
        dma("sp", bs["xt"][:], xsm if sample else xp[ti * 128:(ti + 1) * 128, :], [], ["xt" + bs["sfx"]])
        rms_from(bs["xt"][:], "xt" + bs["sfx"], modA, modB, sample, bs)

    def rms_from(x_ap, xkey, modA, modB, sample, bs=None):
        bs = bs or BS0
        return _rms_from(x_ap, xkey, modA, modB, sample, bs["xsn"], bs["hT"], bs["ss"], bs["rstd"], bs["sfx"])

    def _rms_from(x_ap, xkey, modA, modB, sample, xsn, hT, ss, rstd, sfx):
        A("act", lambda e: e.activation(out=xsn[:], in_=x_ap, func=AF.Square, accum_out=ss[:]), [xkey], ["xsn" + sfx, "ss" + sfx])
        A("dve", lambda e: e.tensor_scalar(out=rstd[:], in0=ss[:], scalar1=1.0 / D, scalar2=1e-6, op0=ALU.mult, op1=ALU.add),
          ["ss" + sfx], ["rstd" + sfx])
        A("act", lambda e: e.activation(out=rstd[:], in_=rstd[:], func=AF.Sqrt), ["rstd" + sfx], ["rstd" + sfx])
        A("dve", lambda e: e.reciprocal(out=rstd[:], in_=rstd[:]), ["rstd" + sfx], ["rstd" + sfx])
        A("act", lambda e: e.activation(out=xsn[:], in_=x_ap, func=AF.Copy, scale=rstd[:]), [xkey, "rstd" + sfx], ["xsn" + sfx])
        for c in range(8):
            A("pe", lambda e, c=c: e.transpose(out=ps[c // 4][:, (c % 4) * 128:(c % 4 + 1) * 128], in_=xsn[:, c * 128:(c + 1) * 128],
                                               identity=ident[:]), ["xsn" + sfx, "ident"], [("ps", c // 4)])
        mA, mB = modT[modA], modT[modB]
        for g in range(2):
            if sample:
                pv = ps[g][:].rearrange("p (c b t) -> p c b t", c=4, b=DB)
                o = hT[:, 4 * g:4 * g + 4, :].rearrange("p c (b t) -> p c b t", b=DB)
                a_ = mA[:, 4 * g:4 * g + 4, 1:17].unsqueeze(3).broadcast_to([128, 4, DB, T])
                b_ = mB[:, 4 * g:4 * g + 4, 1:17].unsqueeze(3).broadcast_to([128, 4, DB, T])
                tmp = xsn[:, 512 * g:512 * g + 512].rearrange("p (c b t) -> p c b t", c=4, b=DB)
            else:
                pv = ps[g][:].rearrange("p (c t) -> p c t", c=4)
                o = hT[:, 4 * g:4 * g + 4, :]
                a_ = mA[:, 4 * g:4 * g + 4, 0:1].broadcast_to([128, 4, 128])
                b_ = mB[:, 4 * g:4 * g + 4, 0:1].broadcast_to([128, 4, 128])
                tmp = xsn[:, 512 * g:512 * g + 512].rearrange("p (c t) -> p c t", c=4)
            A("dve", lambda e, pv=pv, a_=a_, tmp=tmp: e.tensor_tensor(out=tmp, in0=pv, in1=a_, op=ALU.mult),
              [("ps", g), "modT_" + modA, "xsn" + sfx], ["xsn" + sfx])
            A("dve", lambda e, o=o, b_=b_, tmp=tmp: e.tensor_tensor(out=o, in0=tmp, in1=b_, op=ALU.add),
              ["xsn" + sfx, "modT_" + modB], [("hT" + sfx, g)])

    new_phase()
    winq = sb("winq", [128, 8, 1536], BF16)
    for kc in range(8):
        for c0 in (0, 768):
            A("pool", lambda e, kc=kc, c0=c0: e.dma_start(out=winq[:, kc, c0:c0 + 768], in_=w_in[kc * 128:(kc + 1) * 128, c0:c0 + 768]),
              [], [("winq", kc)], dma=True)
    xtB = sb("xt1", [128, D]); xsnB = sb("xsn1", [128, D]); hTB = sb("hT1", [128, 8, 128], BF16)
    ssB = sb("ss1", [128, 1]); rstdB = sb("rstd1", [128, 1])
    BS = [BS0, dict(xt=xtB, xsn=xsnB, hT=hTB, ss=ssB, rstd=rstdB, sfx="1")]
    QKV = [tuple(sb("%s%d" % (n, i), [128, 512]) for n in ("qs", "ks", "vs")) for i in range(2)]
    ropeB = [ropet, sb("ropet1", [128, 128])]
    rtmp = [sb("rtmp%d" % i, [128, H, 8]) for i in range(2)]

    def rope(bank, dst, dkey, rt, rtk):
        psb = ps[bank]
        A("act", lambda e: e.activation(out=dst[:], in_=psb[:], func=AF.Copy), [("ps", bank)], [dkey])
        pv = psb[:].rearrange("p (h e) -> p h e", h=H)
        dv = dst[:].rearrange("p (h e) -> p h e", h=H)
        cos = rt[:, 0:64].rearrange("p (h e) -> p h e", h=H)
        sin = rt[:, 64:128].rearrange("p (h e) -> p h e", h=H)
        t1 = pv[:, :, 0:8]; t2 = pv[:, :, 8:16]
        rk = [("ps", bank), rtk]
        A("dve", lambda e: e.tensor_tensor(out=rtmp[0][:], in0=t1, in1=cos, op=ALU.mult), rk, ["rtmp0"])
        A("dve", lambda e: e.tensor_tensor(out=rtmp[1][:], in0=t2, in1=sin, op=ALU.mult), rk, ["rtmp1"])
        A("dve", lambda e: e.tensor_tensor(out=dv[:, :, 0:8], in0=rtmp[0][:], in1=rtmp[1][:], op=ALU.subtract),
          ["rtmp0", "rtmp1", dkey], [dkey])
        A("dve", lambda e: e.tensor_tensor(out=rtmp[0][:], in0=t1, in1=sin, op=ALU.mult), rk, ["rtmp0"])
        A("dve", lambda e: e.tensor_tensor(out=rtmp[1][:], in0=t2, in1=cos, op=ALU.mult), rk, ["rtmp1"])
        A("dve", lambda e: e.tensor_tensor(out=dv[:, :, 8:16], in0=rtmp[0][:], in1=rtmp[1][:], op=ALU.add),
          ["rtmp0", "rtmp1", dkey], [dkey])

    def p1a_front(n_, ti):
        bs = BS[n_ % 2]
        dma("sp", ropeB[n_ % 2][:], rope_d[ti], [], ["ropet%d" % (n_ % 2)])
        rms_hT(ti, "A1", "B1", bs)

    def p1a_mm(n_, ti):
        bs = BS[n_ % 2]
        hT_ = bs["hT"]
        for g in range(3):
            for kc in range(8):
                A("pe", lambda e, g=g, kc=kc, hT_=hT_: e.matmul(ps[2 + g][:], lhsT=hT_[:, kc, :], rhs=winq[:, kc, 512 * g:512 * g + 512],
                                                                start=(kc == 0), stop=(kc == 7)), ["hT" + bs["sfx"], ("winq", kc)], [("ps", 2 + g)])

    def p1a_back(n_, ti):
        sample = (ti == NT)
        qs, ks, vs = QKV[n_ % 2]
        qk, kk_, vk = ("qs%d" % (n_ % 2), "ks%d" % (n_ % 2), "vs%d" % (n_ % 2))
        rope(2, qs, qk, ropeB[n_ % 2], "ropet%d" % (n_ % 2)); rope(3, ks, kk_, ropeB[n_ % 2], "ropet%d" % (n_ % 2))
        A("act", lambda e: e.activation(out=vs[:], in_=ps[4][:], func=AF.Copy), [("ps", 4)], [vk])
        if sample:
            dma("sp", o_knew, ks[:], [kk_], ["o_knew"], final=True)
            dma("sp", o_vnew, vs[:], [vk], ["o_vnew"], final=True)
        else:
            dma("sp", o_kwin[ti * 128:(ti + 1) * 128, :], ks[:], [kk_], [("o_kwin", ti)], final=True)
            dma("sp", o_vwin[ti * 128:(ti + 1) * 128, :], vs[:], [vk], [("o_vwin", ti)], final=True)
        for (src, skey, dst, dkey, bank, scl) in ((qs, qk, QT, "QT", 5, 0.125), (ks, kk_, KT, "KT", 6, 1.0)):
            for c in range(4):
                A("pe", lambda e, src=src, c=c, bank=bank: e.transpose(out=ps[bank][:, 128 * c:128 * c + 128], in_=src[:, 128 * c:128 * c + 128],
                                                                       identity=ident[:]), [skey, "ident"], [("ps", bank)])
            A("act", lambda e, dst=dst, bank=bank, scl=scl: e.activation(
                out=dst[:, :, ti * 128:(ti + 1) * 128], in_=ps[bank][:].rearrange("p (c t) -> p c t", c=4), func=AF.Copy, scale=scl),
              [("ps", bank)], [(dkey, ti)])
        A("act", lambda e: e.activation(out=Vaug[:, ti, :, 0:64], in_=vs[:].rearrange("p (h e) -> p h e", h=H), func=AF.Copy),
          [vk], [("Vaug", ti)])

    TL1A = [0, NT] if probe == "quick" else list(range(NT + 1))
    p1a_front(0, TL1A[0])
    for n_, ti in enumerate(TL1A):
        p1a_mm(n_, ti)
        if n_ + 1 < len(TL1A):
            p1a_front(n_ + 1, TL1A[n_ + 1])
        p1a_back(n_, ti)

    new_phase()
    Pm = sb("Pm", [128, 14, 128])

    def t4(name, dt=F32):
        return sb(name, [128, 4, 128], dt)

    gT = t4("gT"); bsum = t4("bsum"); onT = t4("onT")
    O32 = sb("O32", [128, H, 64]); Osq = sb("Osq", [128, H, 64]); gst = sb("gst", [128, 4, H])
    KEEP_1C = ptr["R2"] - R2_0
    winp = sb("winp", [128, 8, RIN], BF16)
    for kc in range(8):
        for c0 in (0, 896):
            A("pool", lambda e, kc=kc, c0=c0: e.dma_start(out=winp[:, kc, c0:c0 + 896], in_=w_in[kc * 128:(kc + 1) * 128, 1536 + c0:1536 + c0 + 896]),
              [], [("winp", kc)], dma=True)
    w2a2 = sb("w2a2", [128, 512], BF16); g2w = sb("g2w", [128, 512], BF16)
    A("pool", lambda e: e.dma_start(out=w2a2[:], in_=w2a2_d), [], ["w2a2"], dma=True)
    A("pool", lambda e: e.dma_start(out=g2w[:], in_=g2_d), [], ["g2w"], dma=True)
    PTx = sb("PTx", [128, 14, 129]); dtmp = sb("dtmp", [128, 14, 128])
    Ptok = dtmp[:].rearrange("p c t -> p (c t)")
    la = sb("la", [128, 128], BF16); sgl = sb("sgl", [128, 128], BF16)
    ld = t4("ld"); alpha = t4("alpha"); Lc = t4("Lc"); kxk = t4("kxk"); nrm = t4("nrm"); kk = t4("kk"); kmod = t4("kmod")
    bb = t4("bb"); tmp4 = t4("tmp4"); Wt = t4("Wt"); Winv = t4("Winv"); Wend = t4("Wend")
    Bh, Kh = kxk, nrm
    AR = sb("AR", [128, 4, 2, 128], BF16); Bt = t4("Bt", BF16); Kt = t4("Kt", BF16)
    Vtok = sb("Vtok", [128, 512], BF16); BhTok = sb("BhTok", [128, 512], BF16); KhTok = sb("KhTok", [128, 512], BF16)
    AX4 = sb("AX4", [128, 8, 4, 128], BF16)
    Lk = [sb("Lk%d" % i, [128, 4, 2, 128], BF16) for i in range(2)]
    MT32 = sb("MT32", [128, 4, 128]); MTb = sb("MTb", [128, 4, 128], BF16); NTb = sb("NTb", [128, 8, 128], BF16)
    St32 = sb("St32", [128, 4, 64]); Stb = sb("Stb", [128, 4, 64], BF16); WC = sb("WC", [128, 4])
    RHS32 = sb("RHS32", [128, 512]); RHSb = sb("RHSb", [128, 512], BF16); Ub = sb("Ub", [128, 512], BF16)
    A("dve", lambda e: e.memset(PTx[:, :, 0:1], 0.0), [], [("PTx", "carry")])
    A("dve", lambda e: e.memset(St32[:], 0.0), [], ["St32"])
    A("dve", lambda e: e.memset(Stb[:], 0.0), [], ["Stb"])

    def ptok_from_PTx():
        for c in range(14):
            A("pe", lambda e, c=c: e.transpose(out=ps[1][:, (c % 4) * 128:(c % 4 + 1) * 128], in_=PTx[:, c, 1:129], identity=ident[:]),
              [("PTx", "cur"), "ident"], [("ps", 1)])
            if c % 4 == 3 or c == 13:
                c0 = (c // 4) * 4
                n = c - c0 + 1
                A("act", lambda e, c0=c0, n=n: e.activation(out=Ptok[:, c0 * 128:(c0 + n) * 128], in_=ps[1][:, 0:128 * n], func=AF.Copy),
                  [("ps", 1)], ["dtmp"])

    def phase1b_proj(ti):
        sample = (ti == NT)
        rms_hT(ti, "A1", "B1")
        for g in range(4):
            bank = (2, 0, 1, 2)[g]
            n = 4 if g < 3 else 2
            for c in range(4 * g, 4 * g + n):
                for kc in range(8):
                    A("pe", lambda e, c=c, kc=kc, bank=bank: e.matmul(
                        ps[bank][:, (c % 4) * 128:(c % 4 + 1) * 128], lhsT=winp[:, kc, c * 128:(c + 1) * 128],
                        rhs=hT[:, kc, :], start=(kc == 0), stop=(kc == 7)), ["hT", ("winp", kc)], [("ps", bank)])
            A("act", lambda e, g=g, bank=bank, n=n: e.activation(
                out=PTx[:, 4 * g:4 * g + n, 1:129], in_=ps[bank][:, 0:128 * n].rearrange("p (c t) -> p c t", c=n), func=AF.Copy),
              [("ps", bank)], [("PTx", "cur")])
        if sample or ti == NT - 1:
            ptok_from_PTx()
            if sample:
                dma("sp", o_shs, Ptok[T - 1:128:T, :], ["dtmp"], ["o_shs"], final=True)
            else:
                dma("sp", o_shp, Ptok[127:128, :], ["dtmp"], ["o_shp"], final=True)

    def rwkv_prep(ti):
        sample = (ti == NT)
        Pcur = PTx[:, :, 1:129]
        if sample:
            d4 = dtmp[:].rearrange("p c (b t) -> p c b t", b=DB); c4 = Pcur.rearrange("p c (b t) -> p c b t", b=DB)
            A("dve", lambda e: e.tensor_tensor(out=d4[:, :, :, 1:T], in0=c4[:, :, :, 0:T - 1], in1=c4[:, :, :, 1:T], op=ALU.subtract),
              [("PTx", "cur")], ["dtmp"])
            A("dve", lambda e: e.tensor_tensor(out=d4[:, :, :, 0:1], in0=sshT[:].unsqueeze(3), in1=c4[:, :, :, 0:1], op=ALU.subtract),
              [("PTx", "cur"), "sshT", "dtmp"], ["dtmp"])
        else:
            A("dve", lambda e: e.tensor_tensor(out=dtmp[:], in0=PTx[:, :, 0:128], in1=Pcur, op=ALU.subtract),
              [("PTx", "cur"), ("PTx", "carry")], ["dtmp"])
        A("dve", lambda e: e.tensor_tensor(out=dtmp[:], in0=dtmp[:], in1=vcol(V_MU, 14), op=ALU.mult), ["dtmp", "vecT"], ["dtmp"])
        A("dve", lambda e: e.tensor_tensor(out=Pm[:], in0=dtmp[:], in1=Pcur, op=ALU.add), ["dtmp", ("PTx", "cur")], ["Pm"])
        if ti < NT - 1:
            A("dve", lambda e: e.tensor_copy(out=PTx[:, :, 0:1], in_=PTx[:, :, 128:129]), [("PTx", "cur")], [("PTx", "carry")])
        kc_ = Pm[:, 4:8, :]
        A("dve", lambda e: e.tensor_tensor(out=kxk[:], in0=kc_, in1=vcol(V_KK), op=ALU.mult), ["Pm", "vecT"], ["kxk"])
        A("act", lambda e: e.activation(out=nrm[:], in_=kxk[:], func=AF.Square), ["kxk"], ["nrm"])
        A("pe", lambda e: e.matmul(ps[3][:], lhsT=blk, rhs=nrm[:].rearrange("p c t -> p (c t)"), start=True, stop=True),
          ["cmask", "nrm"], [("ps", 3)])
        A("act", lambda e: e.activation(out=nrm[:], in_=ps[3][:].rearrange("p (c t) -> p c t", c=4), func=AF.Sqrt), [("ps", 3)], ["nrm"])
        A("dve", lambda e: e.tensor_scalar(out=nrm[:], in0=nrm[:], scalar1=1e-12, scalar2=None, op0=ALU.max), ["nrm"], ["nrm"])
        A("dve", lambda e: e.reciprocal(out=nrm[:], in_=nrm[:]), ["nrm"], ["nrm"])
        A("dve", lambda e: e.tensor_tensor(out=kk[:], in0=kxk[:], in1=nrm[:], op=ALU.mult), ["kxk", "nrm"], ["kk"])
        A("act", lambda e: e.activation(out=la[0:64, :], in_=Pm[0:64, 12, :], func=AF.Tanh), ["Pm"], [("la", 0)])
        A("act", lambda e: e.activation(out=la[64:128, :], in_=Pm[64:128, 12, :], func=AF.Copy), ["Pm"], [("la", 1)])
        A("act", lambda e: e.activation(out=sgl[:], in_=Pm[:, 13, :], func=AF.Sigmoid), ["Pm"], ["sgl"])
        for c in range(4):
            cs = slice(128 * c, 128 * c + 128)
            A("pe", lambda e, cs=cs: e.matmul(ps[0][:, cs], lhsT=w2a2[0:64, cs], rhs=la[0:64, :], start=True, stop=True),
              ["w2a2", ("la", 0)], [("ps", 0)])
            A("pe", lambda e, cs=cs: e.matmul(ps[1][:, cs], lhsT=w2a2[64:128, cs], rhs=la[64:128, :], start=True, stop=True),
              ["w2a2", ("la", 1)], [("ps", 1)])
            A("pe", lambda e, cs=cs: e.matmul(ps[2][:, cs], lhsT=g2w[:, cs], rhs=sgl[:], start=True, stop=True), ["g2w", "sgl"], [("ps", 2)])
        for c in range(4):
            cs = slice(128 * c, 128 * c + 128)
            A("act", lambda e, c=c, cs=cs: e.activation(out=ld[:, c, :], in_=ps[0][:, cs], func=AF.Sigmoid, bias=vecT[:, V_W0 + c:V_W0 + c + 1]),
              [("ps", 0), "vecT"], [("ld", c)])
            A("act", lambda e, c=c, cs=cs: e.activation(out=alpha[:, c, :], in_=ps[1][:, cs], func=AF.Sigmoid, bias=vecT[:, V_A0 + c:V_A0 + c + 1]),
              [("ps", 1), "vecT"], [("alpha", c)])
        A("act", lambda e: e.activation(out=gT[:], in_=ps[2][:].rearrange("p (c t) -> p c t", c=4), func=AF.Copy), [("ps", 2)], ["gT"])
        A("dve", lambda e: e.tensor_scalar(out=ld[:], in0=ld[:], scalar1=-0.6065306597126334, scalar2=None, op0=ALU.mult), ["ld"], ["ld"])
        A("dve", lambda e: e.scalar_tensor_tensor(out=tmp4[:], in0=alpha[:], scalar=-1.0, in1=vcol(V_KA), op0=ALU.add, op1=ALU.mult),
          ["alpha", "vecT"], ["tmp4"])
        A("dve", lambda e: e.scalar_tensor_tensor(out=kmod[:], in0=tmp4[:], scalar=1.0, in1=kc_, op0=ALU.add, op1=ALU.mult),
          ["tmp4", "Pm"], ["kmod"])
        A("dve", lambda e: e.tensor_tensor(out=bb[:], in0=kk[:], in1=alpha[:], op=ALU.mult), ["kk", "alpha"], ["bb"])
        A("dve", lambda e: e.tensor_tensor(out=tmp4[:], in0=Pm[:, 0:4, :], in1=kmod[:], op=ALU.mult), ["Pm", "kmod"], ["tmp4"])
        A("dve", lambda e: e.tensor_tensor(out=tmp4[:], in0=tmp4[:], in1=vcol(V_RK), op=ALU.mult), ["tmp4", "vecT"], ["tmp4"])
        A("pe", lambda e: e.matmul(ps[3][:], lhsT=blk, rhs=tmp4[:].rearrange("p c t -> p (c t)"), start=True, stop=True),
          ["cmask", "tmp4"], [("ps", 3)])
        A("act", lambda e: e.activation(out=bsum[:], in_=ps[3][:].rearrange("p (c t) -> p c t", c=4), func=AF.Copy), [("ps", 3)], ["bsum"])

    def rwkv_chunk(ti):
        for c in range(4):
            A("dve", lambda e, c=c: e.tensor_tensor_scan(out=Lc[:, c, :], data0=ones_t, data1=ld[:, c, :], initial=0.0,
                                                         op0=ALU.mult, op1=ALU.add), ["ld", "cmask"], [("Lc", c)])
        A("act", lambda e: e.activation(out=Wt[:], in_=Lc[:], func=AF.Exp), ["Lc"], ["Wt"])
        A("act", lambda e: e.activation(out=Winv[:], in_=Lc[:], func=AF.Exp, scale=-1.0), ["Lc"], ["Winv"])
        A("dve", lambda e: e.tensor_tensor(out=tmp4[:], in0=Lc[:], in1=ld[:], op=ALU.subtract), ["Lc", "ld"], ["tmp4"])
        A("act", lambda e: e.activation(out=tmp4[:], in_=tmp4[:], func=AF.Exp), ["tmp4"], ["tmp4"])
        for c in range(4):
            A("act", lambda e, c=c: e.activation(out=Wend[:, c, :], in_=Lc[:, c, :], func=AF.Exp, scale=-1.0, bias=Lc[:, c, 127:128]),
              ["Lc"], [("Wend", c)])
        A("act", lambda e: e.activation(out=WC[:].unsqueeze(2), in_=Lc[:, :, 127:128], func=AF.Exp), ["Lc"], ["WC"])
        A("dve", lambda e: e.tensor_tensor(out=St32[:], in0=St32[:], in1=WC[:].unsqueeze(2).broadcast_to([128, 4, 64]), op=ALU.mult),
          ["St32", "WC"], ["St32"])
        A("dve", lambda e: e.scalar_tensor_tensor(out=AR[:, :, 0, :], in0=kk[:], scalar=-1.0, in1=tmp4[:], op0=ALU.mult, op1=ALU.mult),
          ["kk", "tmp4"], [("AR", 0)])
        A("dve", lambda e: e.tensor_tensor(out=AR[:, :, 1, :], in0=Pm[:, 0:4, :], in1=Wt[:], op=ALU.mult), ["Pm", "Wt"], [("AR", 1)])
        A("dve", lambda e: e.tensor_tensor(out=Bt[:], in0=bb[:], in1=Winv[:], op=ALU.mult), ["bb", "Winv"], ["Bt"])
        A("dve", lambda e: e.tensor_tensor(out=Kt[:], in0=kmod[:], in1=Winv[:], op=ALU.mult), ["kmod", "Winv"], ["Kt"])
        A("dve", lambda e: e.tensor_tensor(out=Bh[:], in0=bb[:], in1=Wend[:], op=ALU.mult), ["bb", "Wend"], ["kxk"])
        A("dve", lambda e: e.tensor_tensor(out=Kh[:], in0=kmod[:], in1=Wend[:], op=ALU.mult), ["kmod", "Wend"], ["nrm"])
        for (src, skey, dst, dkey) in ((Pm[:, 8:12, :], "Pm", Vtok, "Vtok"), (Bh[:], "kxk", BhTok, "BhTok"), (Kh[:], "nrm", KhTok, "KhTok")):
            for c in range(4):
                A("pe", lambda e, src=src, c=c: e.transpose(out=ps[3][:, 128 * c:128 * c + 128], in_=src[:, c, :], identity=ident[:]),
                  [skey, "ident"], [("ps", 3)])
            A("act", lambda e, dst=dst: e.activation(out=dst[:], in_=ps[3][:], func=AF.Copy), [("ps", 3)], [dkey])
        identb4 = ident[:].unsqueeze(1).broadcast_to([128, 4, 128])
        for g in range(2):
            for hl in range(4):
                h = 4 * g + hl
                hp, hh = h // 2, h % 2
                pr = slice(64 * hh, 64 * hh + 64)
                arv = AR[pr, hp, :, :].rearrange("p a t -> p (a t)")
                ab = 4 if hh == 0 else 6
                lb = 5 if hh == 0 else 7
                lc = slice(128 * (hl // 2), 128 * (hl // 2) + 128)
                A("pe", lambda e, pr=pr, hp=hp, arv=arv, ab=ab: e.matmul(ps[ab][:, 0:256], lhsT=Bt[pr, hp, :], rhs=arv, start=True, stop=True),
                  ["Bt", "AR"], [("ps", ab)])
                A("pe", lambda e, pr=pr, hp=hp, arv=arv, ab=ab: e.matmul(ps[ab][:, 256:512], lhsT=Kt[pr, hp, :], rhs=arv, start=True, stop=True),
                  ["Kt", "AR"], [("ps", ab)])
                A("pe", lambda e, pr=pr, hp=hp, lb=lb, lc=lc: e.matmul(ps[lb][:, lc], lhsT=AR[pr, hp, 0, :], rhs=Bt[pr, hp, :], start=True, stop=True),
                  ["Bt", "AR"], [("ps", lb)])
                A("dve", lambda e, h=h, ab=ab: e.tensor_tensor(out=AX4[:, h, :, :].rearrange("p a t -> p (a t)"), in0=ps[ab][:], in1=MT4, op=ALU.mult),
                  [("ps", ab), "cmask"], [("AX4", h)])
            for hh in range(2):
                lb = 5 if hh == 0 else 7
                A("dve", lambda e, hh=hh, lb=lb: e.tensor_tensor(out=Lk[0][:, hh::2, 0, :], in0=ps[lb][:, 0:256].rearrange("p (a t) -> p a t", a=2),
                                                                 in1=MLs.unsqueeze(1).broadcast_to([128, 2, 128]), op=ALU.mult),
                  [("ps", lb), "cmask"], [("Lk0", ("L", hh))])
            axg = AX4[:, 4 * g:4 * g + 4, 0, :]
            axk = [("AX4", 4 * g + i) for i in range(4)]
            A("act", lambda e, axg=axg: e.activation(out=Lk[0][:, :, 1, :], in_=axg, func=AF.Copy), axk, [("Lk0", "T")])
            A("dve", lambda e, axg=axg: e.tensor_tensor(out=MTb[:], in0=axg, in1=identb4, op=ALU.add), axk + ["ident"], ["MTb"])
            for lv in range(6):
                a_, b_ = Lk[lv % 2], Lk[(lv + 1) % 2]
                ak, bk = "Lk%d" % (lv % 2), "Lk%d" % ((lv + 1) % 2)
                for hl in range(4):
                    bank = 4 if hl < 2 else 6
                    c0 = 256 * (hl % 2)
                    A("pe", lambda e, a_=a_, hl=hl, bank=bank, c0=c0: e.matmul(ps[bank][:, c0:c0 + 128], lhsT=a_[:, hl, 1, :], rhs=a_[:, hl, 0, :],
                                                                               start=True, stop=True), [ak], [("ps", bank)])
                    A("pe", lambda e, a_=a_, hl=hl, bank=bank, c0=c0: e.matmul(ps[bank][:, c0 + 128:c0 + 256], lhsT=a_[:, hl, 0, :], rhs=a_[:, hl, 1, :],
                                                                               start=True, stop=True), [ak], [("ps", bank)])
                A("act", lambda e, b_=b_: e.activation(out=b_[:, 0:2, :, :].rearrange("p h a t -> p (h a t)"), in_=ps[4][:], func=AF.Copy),
                  [("ps", 4)], [(bk, 0)])
                A("dve", lambda e, b_=b_: e.tensor_copy(out=b_[:, 2:4, :, :].rearrange("p h a t -> p (h a t)"), in_=ps[6][:]),
                  [("ps", 6)], [(bk, 1)])
                for hl in range(4):
                    A("pe", lambda e, b_=b_, hl=hl: e.matmul(ps[5][:, 128 * hl:128 * hl + 128], lhsT=b_[:, hl, 0, :], rhs=MTb[:, hl, :],
                                                             start=True, stop=True), [(bk, hl // 2), "MTb"], [("ps", 5)])
                A("dve", lambda e: e.tensor_tensor(out=MTb[:], in0=ps[5][:].rearrange("p (h t) -> p h t", h=4), in1=MTb[:], op=ALU.add),
                  [("ps", 5), "MTb"], ["MTb"])
            A("dve", lambda e, g=g: e.tensor_tensor(out=NTb[:, 4 * g:4 * g + 4, :], in0=MTb[:], in1=identb4, op=ALU.subtract),
              ["MTb", "ident"], [("NTb", g)])
        for h in range(H):
            hp, hh = h // 2, h % 2
            pr = slice(64 * hh, 64 * hh + 64); hs = slice(64 * h, 64 * h + 64)
            A("pe", lambda e, pr=pr, hp=hp, hs=hs: e.matmul(ps[7][:, hs], lhsT=AR[pr, hp, 0, :], rhs=Stb[pr, hp, :], start=True, stop=False),
              ["AR", "Stb"], [("ps", 7)])
            A("pe", lambda e, h=h, hs=hs: e.matmul(ps[7][:, hs], lhsT=AX4[:, h, 2, :], rhs=Vtok[:, hs], start=False, stop=True),
              [("AX4", h), "Vtok"], [("ps", 7)])
        A("act", lambda e: e.activation(out=RHSb[:], in_=ps[7][:], func=AF.Copy), [("ps", 7)], ["RHSb"])
        A("dve", lambda e: e.tensor_copy(out=RHS32[:], in_=ps[7][:]), [("ps", 7)], ["RHS32"])
        for h in range(H):
            hs = slice(64 * h, 64 * h + 64)
            A("pe", lambda e, h=h, hs=hs: e.matmul(ps[6][:, hs], lhsT=NTb[:, h, :], rhs=RHSb[:, hs], start=True, stop=True),
              [("NTb", h // 4), "RHSb"], [("ps", 6)])
        A("dve", lambda e: e.tensor_tensor(out=Ub[:], in0=ps[6][:], in1=RHS32[:], op=ALU.add), [("ps", 6), "RHS32"], ["Ub"])
        for h in range(H):
            hp, hh = h // 2, h % 2
            pr = slice(64 * hh, 64 * hh + 64); hs = slice(64 * h, 64 * h + 64)
            A("pe", lambda e, pr=pr, hp=hp, hs=hs: e.matmul(ps[7][:, hs], lhsT=AR[pr, hp, 1, :], rhs=Stb[pr, hp, :], start=True, stop=False),
              ["AR", "Stb"], [("ps", 7)])
            A("pe", lambda e, h=h, hs=hs: e.matmul(ps[7][:, hs], lhsT=AX4[:, h, 1, :], rhs=Ub[:, hs], start=False, stop=False),
              [("AX4", h), "Ub"], [("ps", 7)])
            A("pe", lambda e, h=h, hs=hs: e.matmul(ps[7][:, hs], lhsT=AX4[:, h, 3, :], rhs=Vtok[:, hs], start=False, stop=True),
              [("AX4", h), "Vtok"], [("ps", 7)])
        A("act", lambda e: e.activation(out=O32[:].rearrange("p h i -> p (h i)"), in_=ps[7][:], func=AF.Copy), [("ps", 7)], ["O32"])
        for h in range(H):
            hp, hh = h // 2, h % 2
            pr = slice(64 * hh, 64 * hh + 64); hs = slice(64 * h, 64 * h + 64)
            A("pe", lambda e, pr=pr, hp=hp, hs=hs: e.matmul(ps[5][pr, 256 + 64 * hp:256 + 64 * hp + 64], lhsT=BhTok[:, hs], rhs=Ub[:, hs],
                                                            start=True, stop=False), ["BhTok", "Ub"], [("ps", 5)])
            A("pe", lambda e, pr=pr, hp=hp, hs=hs: e.matmul(ps[5][pr, 256 + 64 * hp:256 + 64 * hp + 64], lhsT=KhTok[:, hs], rhs=Vtok[:, hs],
                                                            start=False, stop=True), ["KhTok", "Vtok"], [("ps", 5)])
        A("dve", lambda e: e.tensor_tensor(out=St32[:], in0=St32[:], in1=ps[5][:, 256:512].rearrange("p (c i) -> p c i", c=4), op=ALU.add),
          ["St32", ("ps", 5)], ["St32"])
        A("act", lambda e: e.activation(out=Stb[:], in_=St32[:], func=AF.Copy), ["St32"], ["Stb"])

    def wkv_prompt_out():
        stv = St32[:].rearrange("p c i -> p (c i)")
        for q in range(2):
            A("pe", lambda e, q=q: e.transpose(out=ps[3][:, 128 * q:128 * q + 128], in_=stv[:, 128 * q:128 * q + 128], identity=ident[:]),
              ["St32", "ident"], [("ps", 3)])
        A("act", lambda e: e.activation(out=RHS32[:, 0:256], in_=ps[3][:, 0:256], func=AF.Copy), [("ps", 3)], ["RHS32"])
        for q in range(2):
            for hpp in range(2):
                hp = 2 * q + hpp
                dst = o_wkvp[2 * hp:2 * hp + 2, :, :].rearrange("hh i j -> i hh j")
                src = RHS32[64 * hpp:64 * hpp + 64, 128 * q:128 * q + 128].rearrange("p (hh j) -> p hh j", hh=2)
                dma("sp", dst, src, ["RHS32"], [("o_wkvp", hp)], final=True)


    def rwkv_out(ti):
        A("dve", lambda e: e.tensor_reduce(out=gst[:, 0, :], in_=O32[:], axis=AX.X, op=ALU.add), ["O32"], [("gst", 0)])
        A("act", lambda e: e.activation(out=Osq[:], in_=O32[:], func=AF.Square), ["O32"], ["Osq"])
        A("dve", lambda e: e.tensor_reduce(out=gst[:, 1, :], in_=Osq[:], axis=AX.X, op=ALU.add), ["Osq"], [("gst", 1)])
        A("dve", lambda e: e.tensor_scalar(out=gst[:, 0, :], in0=gst[:, 0, :], scalar1=1.0 / E, scalar2=None, op0=ALU.mult), [("gst", 0)], [("gst", 0)])
        A("dve", lambda e: e.tensor_tensor(out=gst[:, 2, :], in0=gst[:, 0, :], in1=gst[:, 0, :], op=ALU.mult), [("gst", 0)], [("gst", 2)])
        A("dve", lambda e: e.scalar_tensor_tensor(out=gst[:, 3, :], in0=gst[:, 1, :], scalar=1.0 / E, in1=gst[:, 2, :], op0=ALU.mult, op1=ALU.subtract),
          [("gst", 1), ("gst", 2)], [("gst", 3)])
        A("dve", lambda e: e.tensor_scalar(out=gst[:, 3, :], in0=gst[:, 3, :], scalar1=64e-5, scalar2=None, op0=ALU.add), [("gst", 3)], [("gst", 3)])
        A("act", lambda e: e.activation(out=gst[:, 3, :], in_=gst[:, 3, :], func=AF.Sqrt), [("gst", 3)], [("gst", 3)])
        A("dve", lambda e: e.reciprocal(out=gst[:, 3, :], in_=gst[:, 3, :]), [("gst", 3)], [("gst", 3)])
        mb = gst[:, 0, :].unsqueeze(2).broadcast_to([128, H, E]); rb = gst[:, 3, :].unsqueeze(2).broadcast_to([128, H, E])
        A("dve", lambda e: e.tensor_tensor(out=Osq[:], in0=O32[:], in1=mb, op=ALU.subtract), ["O32", ("gst", 0)], ["Osq"])
        A("dve", lambda e: e.tensor_tensor(out=Osq[:], in0=Osq[:], in1=rb, op=ALU.mult), ["Osq", ("gst", 3)], ["Osq"])
        ov = Osq[:].rearrange("p h i -> p (h i)")
        for c in range(4):
            A("pe", lambda e, c=c: e.transpose(out=ps[3][:, 128 * c:128 * c + 128], in_=ov[:, 128 * c:128 * c + 128], identity=ident[:]),
              ["Osq", "ident"], [("ps", 3)])
        A("dve", lambda e: e.tensor_tensor(out=onT[:], in0=ps[3][:].rearrange("p (c t) -> p c t", c=4), in1=vcol(V_LG), op=ALU.mult),
          [("ps", 3), "vecT"], ["onT"])
        A("dve", lambda e: e.tensor_tensor(out=onT[:], in0=onT[:], in1=vcol(V_LB), op=ALU.add), ["onT", "vecT"], ["onT"])
        A("dve", lambda e: e.tensor_tensor(out=bsum[:], in0=bsum[:], in1=Pm[:, 8:12, :], op=ALU.mult), ["bsum", "Pm"], ["bsum"])
        A("dve", lambda e: e.tensor_tensor(out=onT[:], in0=onT[:], in1=bsum[:], op=ALU.add), ["onT", "bsum"], ["onT"])
        A("dve", lambda e: e.tensor_tensor(out=rwT[:, ti, :, :], in0=onT[:], in1=gT[:], op=ALU.mult), ["onT", "gT"], [("rwT", ti)])

    def sample_vectors_out():
        A("act", lambda e: e.activation(out=Wt[:], in_=ld[:], func=AF.Exp), ["ld"], ["Wt"])
        A("dve", lambda e: e.tensor_scalar(out=Winv[:], in0=kk[:], scalar1=-1.0, scalar2=None, op0=ALU.mult), ["kk"], ["Winv"])
        srcs = ((Pm[:, 0:4, :], "Pm"), (Wt[:], "Wt"), (kmod[:], "kmod"), (Pm[:, 8:12, :], "Pm"), (Winv[:], "Winv"), (bb[:], "bb"))
        for v, (src, skey) in enumerate(srcs):
            for c in range(4):
                A("pe", lambda e, src=src, c=c: e.transpose(out=ps[3][:, 128 * c:128 * c + 128], in_=src[:, c, :], identity=ident[:]),
                  [skey, "ident"], [("ps", 3)])
            A("act", lambda e: e.activation(out=RHS32[:], in_=ps[3][:], func=AF.Copy), [("ps", 3)], ["RHS32"])
            dma("sp", scr_v[v], RHS32[:], ["RHS32"], [("scr_v", v)])

    def capture(f_):
        P._cap = []
        f_()
        lst, P._cap = P._cap, None
        return lst

    def interleave(main, side):
        j = 0
        for k_, a_ in enumerate(main):
            P.add(*a_)
            want = (len(side) * (k_ + 1)) // max(len(main), 1)
            while j < want:
                P.add(*side[j]); j += 1
        while j < len(side):
            P.add(*side[j]); j += 1

    TL1B = [0, NT - 1, NT] if probe == "quick" else list(range(NT + 1))
    phase1b_proj(TL1B[0])
    for n_, ti in enumerate(TL1B):
        rwkv_prep(ti)
        nxt = capture(lambda: phase1b_proj(TL1B[n_ + 1])) if n_ + 1 < len(TL1B) else []
        if ti < NT:
            def stage(ti=ti):
                rwkv_chunk(ti)
                rwkv_out(ti)
                if ti == NT - 1:
                    wkv_prompt_out()
            interleave(capture(stage), nxt)
        else:
            interleave([], nxt)
            sample_vectors_out()

    new_phase(KEEP_1C)
    Ss = sb("Ss", [128, E, E]); tmpS = sb("tmpS", [128, E, E]); vkS = sb("vkS", [128, E, E])
    vec6 = sb("vec6", [128, 6, T, E]); sa = sb("sa", [128, E]); outs = sb("outs", [128, T, E])
    dma("sp", Ss[:].rearrange("p i j -> p (i j)"), swkv, [], ["Ss"])
    for b in range(DB):
        for v in range(6):
            dma("sp", vec6[8 * b:8 * b + 8, v, :, :], scr_v[v, 8 * b:8 * b + 8, :].rearrange("t (h j) -> h t j", h=H),
                [("scr_v", v)], [("vec6", (b, v))])
    bi = lambda ap: ap.unsqueeze(1).broadcast_to([128, E, E])
    bj = lambda ap: ap.unsqueeze(2).broadcast_to([128, E, E])
    rec = []
    NQ = 4; QI = E // NQ
    bq = lambda ap: ap.unsqueeze(1).broadcast_to([128, QI, E])
    for t in range(T):
        r_, w_, k_, v_, nk_, ka_ = (vec6[:, v, t, :] for v in range(6))
        rec.append(lambda v_=v_, k_=k_: A("pool", lambda e: e.tensor_tensor(out=vkS[:], in0=bj(v_), in1=bi(k_), op=ALU.mult), ["vec6"], ["vkS"]))
        def q_ops(kind, t=t, r_=r_, w_=w_, nk_=nk_, ka_=ka_):
            for q in range(NQ):
                isl = slice(QI * q, QI * q + QI)
                S_, T_ = Ss[:, isl, :], tmpS[:, isl, :]
                sk, tk, ak = ("Ss", q), ("tmpS", q), ("sa", q)
                if kind == 0:
                    f = lambda S_=S_, T_=T_, sk=sk, tk=tk: A("dve", lambda e: e.tensor_tensor(out=T_, in0=S_, in1=bq(nk_), op=ALU.mult), [sk, "vec6"], [tk])
                elif kind == 1:
                    f = lambda T_=T_, isl=isl, tk=tk, ak=ak: A("dve", lambda e: e.tensor_reduce(out=sa[:, isl], in_=T_, axis=AX.X, op=ALU.add), [tk], [ak])
                elif kind == 2:
                    f = lambda S_=S_, sk=sk: A("dve", lambda e: e.tensor_tensor(out=S_, in0=S_, in1=bq(w_), op=ALU.mult), [sk, "vec6"], [sk])
                elif kind == 3:
                    f = lambda T_=T_, isl=isl, tk=tk, ak=ak: A("dve", lambda e: e.tensor_tensor(
                        out=T_, in0=sa[:, isl].unsqueeze(2).broadcast_to([128, QI, E]), in1=bq(ka_), op=ALU.mult), [ak, "vec6"], [tk])
                elif kind == 4:
                    f = lambda S_=S_, T_=T_, sk=sk, tk=tk: A("dve", lambda e: e.tensor_tensor(out=S_, in0=S_, in1=T_, op=ALU.add), [sk, tk], [sk])
                elif kind == 5:
                    f = lambda S_=S_, isl=isl, sk=sk: A("dve", lambda e: e.tensor_tensor(out=S_, in0=S_, in1=vkS[:, isl, :], op=ALU.add), [sk, "vkS"], [sk])
                elif kind == 6:
                    f = lambda S_=S_, T_=T_, sk=sk, tk=tk: A("dve", lambda e: e.tensor_tensor(out=T_, in0=S_, in1=bq(r_), op=ALU.mult), [sk, "vec6"], [tk])
                else:
                    f = lambda T_=T_, isl=isl, tk=tk, q=q: A("dve", lambda e: e.tensor_reduce(out=outs[:, t, isl], in_=T_, axis=AX.X, op=ALU.add),
                                                             [tk], [("outs", (t, q))])
                rec.append(f)
        for kind in range(8):
            q_ops(kind)

    units = []
    if SAMPLE_ATTN_DONE:
        NLB = 4
        kcf = [sb("kc_f%d" % i, [128, 512]) for i in range(NLB)]; vcf = [sb("vc_f%d" % i, [128, 512]) for i in range(NLB)]
        NU = 4
        KTcs = [sb("KTc%d" % i, [128, 4, 128], BF16) for i in range(NU)]
        Vcs = [sb("Vc%d" % i, [128, H, 65], BF16) for i in range(NU)]
        PTss = [sb("PTs%d" % i, [128, H, T], BF16) for i in range(NU)]
        QTbd = sb("QTbd", [128, 4, DB, 2 * T], BF16)
        A("dve", lambda e: e.memset(QTbd[:], 0.0), [], ["QTbd"])
        qsv = lambda pr: QT[pr, :, NT * 128:NT * 128 + 128].rearrange("p c (b q) -> p c b q", b=DB)
        A("act", lambda e: e.activation(out=QTbd[0:64, :, :, 0:T], in_=qsv(slice(0, 64)), func=AF.Copy), [("QT", NT), "QTbd"], ["QTbd"])
        A("act", lambda e: e.activation(out=QTbd[64:128, :, :, T:2 * T], in_=qsv(slice(64, 128)), func=AF.Copy), [("QT", NT), "QTbd"], ["QTbd"])
        acc = sb("acc", [128, H, 128]); cnts = sb("cnts", [128, 16, T]); cntn = sb("cntn", [128, DB, T])
        dma("sp", cnts[:].rearrange("p a q -> p (a q)"), cnts_d, [], ["cnts"])
        dma("sp", cntn[:].rearrange("p a q -> p (a q)"), cntn_d, [], ["cntn"])
        for i in range(NU):
            A("dve", lambda e, i=i: e.memset(Vcs[i][:, :, 64:65], 1.0), [], ["Vc%d" % i])
        A("dve", lambda e: e.memset(acc[:], 0.0), [], ["acc"])
        QS0 = NT * 128
        TB = (0, 2, 4, 6); SVB = (1, 3, 5, 7)

        def sattn_unit(u, b, tl):
            p_ = u % NU
            KTc, Vc, PTs = KTcs[p_], Vcs[p_], PTss[p_]
            ktk, vkk, ptk = "KTc%d" % p_, "Vc%d" % p_, "PTs%d" % p_
            tb, sv = TB[p_], SVB[p_]
            if tl == 16:
                npart = 128
                kT = lambda hp: KT[:, hp, QS0:QS0 + 128]
                vT = lambda h: Vaug[:, NT, h, :]
                kkeys = [("KT", NT)]; vkeys = [("Vaug", NT)]
                msk = cntn[:, b, :]; mkey = "cntn"
            else:
                r = tl
                m0, npart = (0, 128) if r < 8 else (96, 32)
                l_ = sattn_unit.nload % NLB; sattn_unit.nload += 1
                kc_f, vc_f = kcf[l_], vcf[l_]
                kck, vck = "kc_f%d" % l_, "vc_f%d" % l_
                dma("sp", kc_f[0:npart, :], ck[b, 16 * m0 + r:2048:16, :], [], [kck])
                dma("sp", vc_f[0:npart, :], cv[b, 16 * m0 + r:2048:16, :], [], [vck])
                for c in range(4):
                    A("pe", lambda e, c=c: e.transpose(out=ps[tb][:, 128 * c:128 * c + npart], in_=kc_f[0:npart, 128 * c:128 * c + 128],
                                                       identity=ident[0:npart, 0:npart]), [kck, "ident"], [("ps", tb)])
                A("act", lambda e: e.activation(out=KTc[:, :, 0:npart], in_=ps[tb][:].rearrange("p (c k) -> p c k", c=4)[:, :, 0:npart], func=AF.Copy),
                  [("ps", tb)], [ktk])
                A("act", lambda e: e.activation(out=Vc[0:npart, :, 0:64], in_=vc_f[0:npart, :].rearrange("p (h e) -> p h e", h=H), func=AF.Copy),
                  [vck], [vkk])
                kT = lambda hp: KTc[:, hp, 0:npart]
                vT = lambda h: Vc[0:npart, h, :]
                kkeys = [ktk]; vkeys = [vkk]
                msk = cnts[0:npart, tl, :]; mkey = "cnts"
            for hp in range(4):
                A("pe", lambda e, hp=hp: e.matmul(ps[sv][0:npart, 2 * T * hp:2 * T * hp + 2 * T], lhsT=kT(hp), rhs=QTbd[:, hp, b, :], start=True, stop=True),
                  kkeys + ["QTbd"], [("ps", sv)])
            A("act", lambda e: e.activation(out=PTs[0:npart, :, :].rearrange("p h q -> p (h q)"), in_=ps[sv][0:npart, 0:H * T], func=AF.Exp), [("ps", sv)], [ptk])
            A("dve", lambda e: e.tensor_tensor(out=PTs[0:npart, :, :], in0=PTs[0:npart, :, :], in1=msk.unsqueeze(1).broadcast_to([npart, H, T]), op=ALU.mult),
              [ptk, mkey], [ptk])
            for h in range(H):
                A("pe", lambda e, h=h: e.matmul(ps[sv][0:65, 64 + T * h:64 + T * h + T], lhsT=vT(h), rhs=PTs[0:npart, h, :], start=True, stop=True),
                  vkeys + [ptk], [("ps", sv)])
            A("dve", lambda e: e.tensor_tensor(out=acc[0:65, :, T * b:T * b + T], in0=acc[0:65, :, T * b:T * b + T],
                                               in1=ps[sv][0:65, 64:64 + H * T].rearrange("p (h q) -> p h q", h=H), op=ALU.add),
              [("ps", sv), "acc"], ["acc"])

        sattn_unit.nload = 0
        u = 0
        for b in (range(2) if probe == "quick" else range(DB)):
            for tl in range(17):
                units.append(lambda u=u, b=b, tl=tl: sattn_unit(u, b, tl))
                u += 1
    done = 0
    for k_, th in enumerate(rec):
        th()
        want = (len(units) * (k_ + 1)) // len(rec)
        while done < want:
            units[done](); done += 1
    while done < len(units):
        units[done](); done += 1
    if SAMPLE_ATTN_DONE:
        dma("sp", scr_acc, acc[0:65, :, :].rearrange("p h t -> p (h t)"), ["acc"], ["scr_acc"])
    dma("sp", o_wkvs, Ss[:].rearrange("p i j -> p (i j)"), ["Ss"], ["o_wkvs"], final=True)
    for b in range(DB):
        dma("sp", scr_o[8 * b:8 * b + 8, :].rearrange("t (h i) -> h t i", h=H), outs[8 * b:8 * b + 8, :, :], ["outs"], [("scr_o", b)])
    dma("sp", O32[:].rearrange("p h i -> p (h i)"), scr_o, ["scr_o"], ["O32"])
    rwkv_out(NT)

    new_phase()
    X1_BYTES = (NT + 1) * D * 4
    x1 = sb("x1", [128, NT + 1, D])
    G2p = sb("G2p", [128, D]); G2s = sb("G2s", [128, D])
    KEEP_P3 = ptr["R2"] - R2_0
    G1 = sb("G1", [128, D]); G1s = sb("G1s", [128, D])
    KEEP_P2 = ptr["R2"] - R2_0
    wadah = sb("wadah", [128, 8, 512], BF16); m17 = sb("m17", [17, D]); bgb = sb("bgb", [17, 512])

    def gate_m17(gidx):
        col0 = (2 if gidx == 0 else 5) * D
        for hf in range(2):
            for q in range(2):
                A("pool", lambda e, hf=hf, q=q: e.dma_start(
                    out=wadah[:, 4 * q:4 * q + 4, :],
                    in_=w_ada[512 * q:512 * q + 512, col0 + 512 * hf:col0 + 512 * hf + 512].rearrange("(kc p) n -> p kc n", p=128)),
                  [], ["wadah"], dma=True)
            dma("sp", bgb[:], bgate[gidx, 512 * hf:512 * hf + 512].partition_broadcast(17), [], ["bgb"])
            for kc in range(8):
                A("pe", lambda e, kc=kc: e.matmul(ps[0][0:17, :], lhsT=scT[:, kc, :], rhs=wadah[:, kc, :], start=(kc == 0), stop=(kc == 7)),
                  ["scT", "wadah"], [("ps", 0)])
            A("dve", lambda e, hf=hf: e.tensor_tensor(out=m17[:, 512 * hf:512 * hf + 512], in0=ps[0][0:17, :], in1=bgb[:], op=ALU.add),
              [("ps", 0), "bgb"], [("m17", hf)])

    def gate_bcast(sel, dst, dkey):
        for hf in range(2):
            A("pe", lambda e, hf=hf: e.matmul(ps[1][:], lhsT=Esel[:, 128 * sel:128 * sel + 128], rhs=m17[:, 512 * hf:512 * hf + 512],
                                              start=True, stop=True), ["Esel", ("m17", hf)], [("ps", 1)])
            A("act", lambda e, hf=hf: e.activation(out=dst[:, 512 * hf:512 * hf + 512], in_=ps[1][:], func=AF.Copy), [("ps", 1)], [dkey])

    gate_m17(1); gate_bcast(0, G2p, "G2p"); gate_bcast(1, G2s, "G2s")
    gate_m17(0); gate_bcast(0, G1, "G1"); gate_bcast(1, G1s, "G1s")

    new_phase(KEEP_P2)
    wout = sb("wout", [128, 8, D], BF16)
    for kc in range(8):
        A("pool", lambda e, kc=kc: e.dma_start(out=wout[:, kc, :], in_=w_out[kc * 128:(kc + 1) * 128, :]), [], [("wout", kc)], dma=True)
    attT = sb("attT", [128, 4, 128], BF16); rsum = sb("rsum", [128, H])
    KEEP_2B = ptr["R2"] - R2_0
    cntm = sb("cntm", [128, 16, 128], BF16)
    for hf in range(2):
        A("pool", lambda e, hf=hf: e.dma_start(out=cntm[:, 8 * hf:8 * hf + 8, :].rearrange("p d q -> p (d q)"), in_=cnt_d[:, 1024 * hf:1024 * hf + 1024]),
          [], [("cntm", hf)], dma=True)
    PTb = sb("PTb", [128, NT, 2, 128], BF16)

    oacc = xt[:, 0:520].rearrange("p (h e) -> p h e", h=H)

    def attn_finish(ti, Gt=None, gk="G1"):
        for g in range(2):
            A("act", lambda e, g=g: e.activation(out=xt[:, 260 * g:260 * g + 260], in_=ps[6 + g][:, 0:260], func=AF.Copy), [("ps", 6 + g)], ["xt"])
        A("dve", lambda e: e.reciprocal(out=rsum[:].unsqueeze(2), in_=oacc[:, :, 64:65]), ["xt"], ["rsum"])
        A("dve", lambda e: e.tensor_tensor(out=xsn[:, 0:512].rearrange("p (h e) -> p h e", h=H), in0=oacc[:, :, 0:64],
                                           in1=rsum[:].unsqueeze(2).broadcast_to([128, H, E]), op=ALU.mult), ["xt", "rsum"], ["xsn"])
        for c in range(4):
            A("pe", lambda e, c=c: e.transpose(out=ps[5][:, 128 * c:128 * c + 128], in_=xsn[:, 128 * c:128 * c + 128], identity=ident[:]),
              ["xsn", "ident"], [("ps", 5)])
        A("act", lambda e: e.activation(out=attT[:], in_=ps[5][:].rearrange("p (c t) -> p c t", c=4), func=AF.Copy), [("ps", 5)], ["attT"])
        for hf in range(2):
            for kc in range(8):
                lhs = attT[:, kc, :] if kc < 4 else rwT[:, ti, kc - 4, :]
                A("pe", lambda e, hf=hf, kc=kc, lhs=lhs: e.matmul(ps[2 + hf][:], lhsT=lhs, rhs=wout[:, kc, 512 * hf:512 * hf + 512],
                                                                  start=(kc == 0), stop=(kc == 7)),
                  ["attT", ("rwT", ti), ("wout", kc)], [("ps", 2 + hf)])
        dma("sp", xt[:], xsm if ti == NT else xp[ti * 128:(ti + 1) * 128, :], [], ["xt"])
        for hf in range(2):
            cs = slice(512 * hf, 512 * hf + 512)
            Gt_ = G1 if Gt is None else Gt
            A("dve", lambda e, hf=hf, cs=cs, Gt_=Gt_: e.tensor_tensor(out=xsn[:, cs], in0=ps[2 + hf][:], in1=Gt_[:, cs], op=ALU.mult),
              [("ps", 2 + hf), gk, "xsn"], ["xsn"])
            A("dve", lambda e, cs=cs: e.tensor_tensor(out=x1[:, ti, cs], in0=xsn[:, cs], in1=xt[:, cs], op=ALU.add), ["xsn", "xt"], [("x1", ti)])

    def attn_prompt(qt):
        qs_ = slice(qt * 128, qt * 128 + 128)
        for hg in range(4):
            for kt0 in range(0, qt + 1, 4):
                n = min(4, qt + 1 - kt0)
                for j in range(n):
                    kt = kt0 + j
                    for hh in range(2):
                        pr = slice(64 * hh, 64 * hh + 64)
                        A("pe", lambda e, j=j, kt=kt, pr=pr, hh=hh, hg=hg: e.matmul(
                            ps[hh][:, 128 * j:128 * j + 128], lhsT=KT[pr, hg, kt * 128:kt * 128 + 128], rhs=QT[pr, hg, qs_], start=True, stop=True),
                          [("KT", kt), ("QT", qt)], [("ps", hh)])
                for hh in range(2):
                    A("act", lambda e, kt0=kt0, n=n, hh=hh: e.activation(
                        out=PTb[:, kt0:kt0 + n, hh, :], in_=ps[hh][:, 0:128 * n].rearrange("p (k q) -> p k q", k=n), func=AF.Exp),
                      [("ps", hh)], [("PTb", kt_) for kt_ in range(kt0, kt0 + n)])
                for kt in range(kt0, kt0 + n):
                    d = qt - kt
                    eng = "dve" if d % 2 == 0 else "pool"
                    A(eng, lambda e, kt=kt, d=d: e.tensor_tensor(out=PTb[:, kt, :, :], in0=PTb[:, kt, :, :],
                                                                 in1=cntm[:, d, :].unsqueeze(1).broadcast_to([128, 2, 128]), op=ALU.mult),
                      [("PTb", kt), ("cntm", d // 8)], [("PTb", kt)])
            for hh in range(2):
                h = 2 * hg + hh
                ob = ps[6 + h // 4][:, 65 * (h % 4):65 * (h % 4) + 65]
                for kt in range(qt + 1):
                    A("pe", lambda e, kt=kt, hh=hh, h=h, ob=ob: e.matmul(ob, lhsT=PTb[:, kt, hh, :], rhs=Vaug[:, kt, h, :],
                                                                         start=(kt == 0), stop=(kt == qt)),
                      [("PTb", kt), ("Vaug", kt)], [("ps", 6 + h // 4)])

    QTL = [0, 1] if probe == "quick" else list(range(NT))
    attn_prompt(QTL[0])
    for n_, qt in enumerate(QTL):
        fin = capture(lambda: attn_finish(qt))
        nxt = capture(lambda: attn_prompt(QTL[n_ + 1])) if n_ + 1 < len(QTL) else []
        if nxt:
            interleave(nxt, fin)
        else:
            interleave(fin, [])


    if SAMPLE_ATTN_DONE:
        accv = xsn[:].rearrange("p (h t) -> p h t", h=H)
        dma("sp", xsn[0:65, :], scr_acc, ["scr_acc"], ["xsn"])
        for h in range(H):
            A("pe", lambda e, h=h: e.transpose(out=ps[6 + h // 4][:, 65 * (h % 4):65 * (h % 4) + 65], in_=accv[0:65, h, :], identity=ident[0:65, 0:65]),
              ["xsn", "ident"], [("ps", 6 + h // 4)])
        attn_finish(NT, G1s, "G1s")

    new_phase(KEEP_P3)
    ptr["R1"] = R1_0
    w1c = [sb("w1c%d" % i, [128, 8, D], BF16, "R1") for i in range(2)]
    w2c = [sb("w2c%d" % i, [128, 8, D], BF16, "R1") for i in range(2)]
    h2T = sb("h2T", [128, 8, (NT + 1) * 128], BF16)
    GT = 3
    hid = sb("hid", [128, 8, 128 * GT], BF16); rl = sb("rl", [128, 128 * GT])
    TILES = [0, 1, NT] if probe == "quick" else list(range(NT + 1))
    if not SAMPLE_ATTN_DONE:
        TILES = [t_ for t_ in TILES if t_ != NT]
    for ti in TILES:
        rms_from(x1[:, ti, :], ("x1", ti), "A2", "B2", ti == NT)
        A("act", lambda e, ti=ti: e.activation(out=h2T[:, :, ti * 128:(ti + 1) * 128], in_=hT[:], func=AF.Copy), ["hT"], [("h2T", ti)])
    groups = [TILES[i:i + GT] for i in range(0, len(TILES), GT)]
    for c in range(4):
        wb = c % 2
        for kc in range(8):
            A("pool", lambda e, kc=kc, c=c, wb=wb: e.dma_start(out=w1c[wb][:, kc, :], in_=w_ff1[kc * 128:(kc + 1) * 128, c * D:(c + 1) * D]),
              [], [("w1c%d" % wb, kc)], dma=True)
            A("pool", lambda e, kc=kc, c=c, wb=wb: e.dma_start(out=w2c[wb][:, kc, :], in_=w_ff2[c * D + kc * 128:c * D + (kc + 1) * 128, :]),
              [], [("w2c%d" % wb, kc)], dma=True)
        for gi, grp in enumerate(groups):
            contiguous = all(grp[i + 1] == grp[i] + 1 for i in range(len(grp) - 1))
            subgroups = [grp] if contiguous else [[t_] for t_ in grp]
            for sg in subgroups:
                ntok = 128 * len(sg)
                t0 = sg[0] * 128
                for fc in range(8):
                    hb = fc % 2
                    for kc in range(8):
                        A("pe", lambda e, fc=fc, kc=kc, hb=hb, t0=t0, ntok=ntok, wb=wb: e.matmul(
                            ps[hb][:, 0:ntok], lhsT=w1c[wb][:, kc, 128 * fc:128 * fc + 128], rhs=h2T[:, kc, t0:t0 + ntok],
                            start=(kc == 0), stop=(kc == 7)), [("w1c%d" % wb, kc)] + [("h2T", t_) for t_ in sg], [("ps", hb)])
                    A("act", lambda e, hb=hb, ntok=ntok: e.activation(out=rl[:, 0:ntok], in_=ps[hb][:, 0:ntok], func=AF.Relu), [("ps", hb)], ["rl"])
                    A("dve", lambda e, fc=fc, ntok=ntok: e.tensor_tensor(out=hid[:, fc, 0:ntok], in0=rl[:, 0:ntok], in1=rl[:, 0:ntok], op=ALU.mult),
                      ["rl"], [("hid", fc)])
                for tl, ti in enumerate(sg):
                    Gt = G2s if ti == NT else G2p
                    gk = "G2s" if ti == NT else "G2p"
                    for hf in range(2):
                        yb = 2 + 2 * tl + hf
                        cs = slice(512 * hf, 512 * hf + 512)
                        for fc in range(8):
                            A("pe", lambda e, fc=fc, tl=tl, yb=yb, cs=cs, wb=wb: e.matmul(ps[yb][:], lhsT=hid[:, fc, 128 * tl:128 * tl + 128], rhs=w2c[wb][:, fc, cs],
                                                                                   start=(fc == 0), stop=(fc == 7)),
                              [("hid", fc), ("w2c%d" % wb, fc)], [("ps", yb)])
                        A("dve", lambda e, yb=yb, cs=cs, Gt=Gt: e.tensor_tensor(out=xsn[:, cs], in0=ps[yb][:], in1=Gt[:, cs], op=ALU.mult),
                          [("ps", yb), gk, "xsn"], ["xsn"])
                        A("dve", lambda e, ti=ti, cs=cs: e.tensor_tensor(out=x1[:, ti, cs], in0=xsn[:, cs], in1=x1[:, ti, cs], op=ALU.add),
                          ["xsn", ("x1", ti)], [("x1", ti)])

    new_phase(X1_BYTES)
    gfb = sb("gfb", [128, D]); yo = [sb("yo%d" % i, [128, D]) for i in range(2)]
    fx = [xsn, sb("fxsn1", [128, D])]; fss = [ss, sb("fss1", [128, 1])]; frs = [rstd, sb("frstd1", [128, 1])]
    fk = [("xsn", "ss", "rstd"), ("fxsn1", "fss1", "frstd1")]
    dma("sp", gfb[:], gfin.partition_broadcast(128), [], ["gfb"])
    for n_, ti in enumerate(TILES):
        xa = x1[:, ti, :]
        y_ = yo[n_ % 2]; yk = "yo%d" % (n_ % 2)
        xs_, ss_, rs_ = fx[n_ % 2], fss[n_ % 2], frs[n_ % 2]
        xk, sk, rk_ = fk[n_ % 2]
        A("act", lambda e, xa=xa, xs_=xs_, ss_=ss_: e.activation(out=xs_[:], in_=xa, func=AF.Square, accum_out=ss_[:]), [("x1", ti)], [xk, sk])
        A("dve", lambda e, ss_=ss_, rs_=rs_: e.tensor_scalar(out=rs_[:], in0=ss_[:], scalar1=1.0 / D, scalar2=1e-6, op0=ALU.mult, op1=ALU.add), [sk], [rk_])
        A("act", lambda e, rs_=rs_: e.activation(out=rs_[:], in_=rs_[:], func=AF.Sqrt), [rk_], [rk_])
        A("dve", lambda e, rs_=rs_: e.reciprocal(out=rs_[:], in_=rs_[:]), [rk_], [rk_])
        A("act", lambda e, xa=xa, xs_=xs_, rs_=rs_: e.activation(out=xs_[:], in_=xa, func=AF.Copy, scale=rs_[:]), [("x1", ti), rk_], [xk])
        A("dve", lambda e, y_=y_, xs_=xs_: e.tensor_tensor(out=y_[:], in0=xs_[:], in1=gfb[:], op=ALU.mult), [xk, "gfb"], [yk])
        dma("sp", o_ys if ti == NT else o_yp[ti * 128:(ti + 1) * 128, :], y_[:], [yk], [("o_y", ti)], final=True)

    P.emit()
    es.close()
    return nc


def _rope_tables():
    half = 8
    inv = (500000.0 ** (-np.arange(half, dtype=np.float32) * np.float32(2.0 / 16))).astype(np.float32)
    tab = np.zeros((17, 128, 128), np.float32)
    for ti in range(17):
        if ti < NT:
            pos = (ti * 128 + np.arange(128)).astype(np.float32)
        else:
            pos = (PAST + (np.arange(128) % T)).astype(np.float32)
        ang = pos[:, None] * inv[None, :]
        tab[ti, :, 0:64] = np.tile(np.cos(ang), (1, H))
        tab[ti, :, 64:128] = np.tile(np.sin(ang), (1, H))
    return tab


def _cmask():
    m = np.zeros((128, 896), np.float32)
    m[0:64, 0:64] = 1.0; m[64:128, 64:128] = 1.0
    i = np.arange(128)
    strictT = (i[:, None] < i[None, :]).astype(np.float32)
    inclT = (i[:, None] <= i[None, :]).astype(np.float32)
    m[:, 128:256] = strictT; m[:, 256:384] = inclT; m[:, 384:512] = strictT; m[:, 512:640] = inclT
    m[:, 640:768] = (i[None, :] < i[:, None]).astype(np.float32)
    m[:, 768:896] = 1.0
    return m


def _cnt_table():
    k = np.arange(128)[:, None, None]; d = np.arange(16)[None, :, None]; q = np.arange(128)[None, None, :]
    dl = 128 * d + q - k
    c = ((dl >= 0) & (dl <= 128)).astype(np.float32) + ((dl >= 0) & (dl <= 512) & (dl % 4 == 0)) + ((dl >= 0) & (dl <= 2048) & (dl % 16 == 0))
    return np.ascontiguousarray(c.reshape(128, 2048).astype(np.float32))


def _cnts():
    c = np.zeros((128, 16, T), np.float32)
    for r in range(16):
        m0, n = (0, 128) if r < 8 else (96, 32)
        for p in range(n):
            R = 16 * (m0 + p) + r
            for i in range(T):
                v = 0
                if R >= 1920 + i: v += 1
                if R % 4 == i % 4 and R >= 1536 + i: v += 1
                if R % 16 == i % 16 and R >= i: v += 1
                c[p, r, i] = v
    return np.ascontiguousarray(c.reshape(128, 16 * T))


def _cntn():
    c = np.zeros((128, DB, T), np.float32)
    for b in range(DB):
        for s_ in range(T):
            for i in range(T):
                d = i - s_
                if d >= 0:
                    c[T * b + s_, b, i] = 1 + (d % 4 == 0) + (d % 16 == 0)
    return np.ascontiguousarray(c.reshape(128, DB * T))


def _esel():
    e = np.zeros((17, 256), np.float32)
    e[0, 0:128] = 1.0
    for b in range(DB):
        e[1 + b, 128 + T * b:128 + T * b + T] = 1.0
    return e


_NC_CACHE = {}


def kernel(**inp):
    f = lambda a: np.ascontiguousarray(np.asarray(a, dtype=np.float32))
    x_prompt = f(inp["x_prompt"]); x_sample = f(inp["x_sample"])
    c_prompt = f(inp["c_prompt"]); c_sample = f(inp["c_sample"])
    b_ada = f(inp["b_ada"])[0]
    vec_rows = [f(inp["mu"])[0].reshape(14, 128)]
    for n in ("w0", "a0", "k_k", "k_a"):
        vec_rows.append(f(inp[n])[0].reshape(4, 128))
    vec_rows.append(f(inp["r_k"])[0].reshape(4, 128))
    for n in ("lnx_g", "lnx_b"):
        vec_rows.append(f(inp[n])[0].reshape(4, 128))
    vec_rows.append(f(inp["norm1_g"])[0].reshape(8, 128))
    vec_rows.append(f(inp["norm2_g"])[0].reshape(8, 128))
    for blk in (0, 1, 3, 4):
        vec_rows.append(b_ada[blk * D:(blk + 1) * D].reshape(8, 128))
    vecs = np.ascontiguousarray(np.concatenate(vec_rows, axis=0))
    assert vecs.shape == (NVEC, 128)
    bgate = np.ascontiguousarray(np.stack([b_ada[2 * D:3 * D], b_ada[5 * D:6 * D]]))
    shared = {
        "vecs": vecs, "bgate": bgate, "w_ada": f(inp["w_ada"])[0], "w_in": f(inp["w_in"])[0],
        "ident": np.eye(128, dtype=np.float32), "rope": _rope_tables(), "cmask": _cmask(),
        "w2a2": np.ascontiguousarray(np.concatenate([f(inp["w2"])[0], f(inp["a2"])[0]], axis=0)),
        "w_out": f(inp["w_out"])[0], "w_ff1": f(inp["w_ff1"])[0], "w_ff2": f(inp["w_ff2"])[0], "gfin": f(inp["normf_g"]),
        "cnt": _cnt_table(), "esel": _esel(), "cnts": _cnts(), "cntn": _cntn(),
        "g2": f(inp["g2"])[0],
    }
    state_shift = f(inp["state_shift"])[0]
    state_wkv = f(inp["state_wkv"])[0]
    cache_k = np.asarray(inp["cache_k"], dtype=np.float32)[0].reshape(128, 2048, 512)
    cache_v = np.asarray(inp["cache_v"], dtype=np.float32)[0].reshape(128, 2048, 512)
    in_maps = []
    for i in range(NCORES):
        m = dict(shared)
        m["xp"] = x_prompt[i]
        m["xs"] = x_sample[DB * i:DB * (i + 1)].reshape(128, D)
        m["c17"] = np.ascontiguousarray(np.concatenate([c_prompt[i:i + 1], c_sample[DB * i:DB * (i + 1)]], axis=0))
        m["sshift"] = state_shift[DB * i:DB * (i + 1)]
        m["swkv"] = state_wkv[DB * i:DB * (i + 1)].reshape(128, E * E)
        m["ck"] = cache_k[DB * i:DB * (i + 1)]
        m["cv"] = cache_v[DB * i:DB * (i + 1)]
        in_maps.append(m)
    if "nc" not in _NC_CACHE:
        _NC_CACHE["nc"] = build_program()
    nc = _NC_CACHE["nc"]
    res = run_bass_kernel_spmd(nc, in_maps, core_ids=list(range(NCORES)))
    R = res.results
    g = lambda name: [np.asarray(R[i][name], dtype=np.float32) for i in range(NCORES)]
    y_prompt = np.stack(g("o_yp"))
    y_sample = np.concatenate(g("o_ys")).reshape(128, T, D) if SAMPLE_ATTN_DONE else np.zeros((128, T, D), np.float32)
    kwin = np.stack(g("o_kwin")).reshape(1, 8, S, H, E)
    vwin = np.stack(g("o_vwin")).reshape(1, 8, S, H, E)
    wkv_p = np.stack(g("o_wkvp")).reshape(1, 8, H, E, E)
    shp = np.stack(g("o_shp")).reshape(1, 8, RIN)
    knew = np.concatenate(g("o_knew")).reshape(1, 128, T, H, E)
    vnew = np.concatenate(g("o_vnew")).reshape(1, 128, T, H, E)
    wkv_s = np.concatenate(g("o_wkvs")).reshape(1, 128, H, E, E)
    shs = np.concatenate(g("o_shs")).reshape(1, 128, RIN)
    return (y_prompt, y_sample, kwin, vwin, wkv_p, shp, knew, vnew, wkv_s, shs)
```
